# Optimizing a Trainium2 kernel written in Bass

```python
import math
import jax, jax.numpy as jnp
from jax import lax
import numpy as np

D_MODEL = 2048
BATCH = 4
SEQ = 2048
DEPTH = 2
DEC_BATCH = 128
DEC_SEQ = 8
PAST_LEN = 8192
PAGE_SIZE = 128

N_META = 16
N_MIXERS = 2
N_HGRN_LAYERS = (DEPTH + 1) // 2
N_SWA_LAYERS = DEPTH // 2
HGRN_EXPAND = 128
HGRN_WIDTH = D_MODEL
HGRN_HEADS = HGRN_WIDTH // HGRN_EXPAND
HGRN_DK = HGRN_EXPAND
HGRN_DV = HGRN_WIDTH // HGRN_HEADS
HGRN_CHUNK = 64
SWA_HEAD_DIM = 64
SWA_Q_HEADS = D_MODEL // SWA_HEAD_DIM
SWA_KV_HEADS = SWA_Q_HEADS // 8
SWA_GROUP = SWA_Q_HEADS // SWA_KV_HEADS
SWA_WIDTH = SWA_Q_HEADS * SWA_HEAD_DIM
SWA_KV_WIDTH = SWA_KV_HEADS * SWA_HEAD_DIM
SWA_SCALE = SWA_HEAD_DIM ** -0.5
WINDOW = 128
SWA_BLOCK = 128
DEEPNORM_ALPHA = (2.0 * DEPTH) ** 0.25
DEEPNORM_BETA = (8.0 * DEPTH) ** -0.25
LN_EPS = 1e-5
RMS_EPS = 1e-6

kernel_name = 'hybrid_hgrn2_swa_sink_deepnorm_step'


def layer_norm(x, g, b):
    mu = jnp.mean(x, axis=-1, keepdims=True)
    var = jnp.mean(jnp.square(x - mu), axis=-1, keepdims=True)
    return (x - mu) * lax.rsqrt(var + LN_EPS) * g.astype(jnp.float32) + b.astype(jnp.float32)


def hgrn_chunk(q, logf, k, v, s0):
    c = q.shape[1]
    b = jnp.cumsum(logf, axis=1)
    causal = jnp.tril(jnp.ones((c, c), dtype=bool))[None, :, :, None, None]
    diff = jnp.where(causal, b[:, :, None] - b[:, None, :], -jnp.inf)
    a = jnp.sum(q[:, :, None] * jnp.exp(diff) * k[:, None, :], axis=-1)
    o = (jnp.einsum('bthk,bhkv->bthv', q * jnp.exp(b), s0)
         + jnp.einsum('btsh,bshv->bthv', a, v))
    b_last = b[:, -1]
    s_new = (jnp.exp(b_last)[..., None] * s0
             + jnp.einsum('bshk,bshv->bhkv', k * jnp.exp(b_last[:, None] - b), v))
    return o, s_new


def hgrn_branch(xin, w_in, lb, norm_w, w_out, s0, n_lead, chunk):
    bsz, t, _ = xin.shape
    h = jnp.einsum('btd,de->bte', xin, w_in.astype(jnp.float32))
    q, fx, i_in, g = jnp.split(h, 4, axis=-1)
    heads = lambda a: a.reshape(bsz, t, HGRN_HEADS, -1)
    f = lb + (1.0 - lb) * jax.nn.sigmoid(heads(fx))
    q, logf, k, v = heads(q), jnp.log(f), 1.0 - f, heads(i_in)
    outs = []
    s = s0
    if n_lead > 0:
        o_lead, s = hgrn_chunk(q[:, :n_lead], logf[:, :n_lead], k[:, :n_lead], v[:, :n_lead], s)
        outs.append(o_lead)
    rest = t - n_lead
    nc = rest // chunk
    blocks = tuple(a[:, n_lead:].reshape(bsz, nc, chunk, HGRN_HEADS, -1).swapaxes(0, 1)
                   for a in (q, logf, k, v))

    def step(state, blk):
        o_blk, state = hgrn_chunk(*blk, state)
        return state, o_blk

    s, o_blocks = lax.scan(step, s, blocks)
    outs.append(o_blocks.swapaxes(0, 1).reshape(bsz, rest, HGRN_HEADS, HGRN_DV))
    o = jnp.concatenate(outs, axis=1)
    o = o * lax.rsqrt(jnp.mean(jnp.square(o), axis=-1, keepdims=True) + RMS_EPS) * norm_w.astype(jnp.float32)
    o = o.reshape(bsz, t, HGRN_WIDTH) * jax.nn.silu(g)
    y = jnp.einsum('bte,ed->btd', o, w_out.astype(jnp.float32))
    return y, s


def swa_project(xin, w_in):
    bsz, t, _ = xin.shape
    h = jnp.einsum('btd,de->bte', xin, w_in.astype(jnp.float32))
    q, k, v, g = jnp.split(h, [SWA_WIDTH, SWA_WIDTH + SWA_KV_WIDTH, SWA_WIDTH + 2 * SWA_KV_WIDTH], axis=-1)
    q = q.reshape(bsz, t, SWA_KV_HEADS, SWA_GROUP, SWA_HEAD_DIM)
    k = k.reshape(bsz, t, SWA_KV_HEADS, SWA_HEAD_DIM)
    v = v.reshape(bsz, t, SWA_KV_HEADS, SWA_HEAD_DIM)
    return q, k, v, g


def sink_attention(q, k, v, mask, sinks):
    s = jnp.einsum('bnqhgd,bnshd->bnhgqs', q, k) * SWA_SCALE
    s = jnp.where(mask[None, :, None, None], s, -jnp.inf)
    sink = sinks.astype(jnp.float32)[None, None, :, :, None, None]
    m = jnp.maximum(jnp.max(s, axis=-1, keepdims=True), sink)
    p = jnp.exp(s - m)
    p = p / (jnp.sum(p, axis=-1, keepdims=True) + jnp.exp(sink - m))
    return jnp.einsum('bnhgqs,bnshd->bnqhgd', p, v)


def swa_output(o, g, w_out):
    return jnp.einsum('bte,ed->btd', o * jax.nn.silu(g), w_out.astype(jnp.float32))


def swa_prompt(xin, w_in, sinks, w_out):
    bsz, t, _ = xin.shape
    q, k, v, g = swa_project(xin, w_in)
    nb = -(-t // SWA_BLOCK)
    pad = nb * SWA_BLOCK - t

    def blocks(a):
        a = jnp.pad(a, ((0, 0), (0, pad)) + ((0, 0),) * (a.ndim - 2))
        return a.reshape((bsz, nb, SWA_BLOCK) + a.shape[2:])

    def band(a):
        ab = blocks(a)
        prev = jnp.concatenate([jnp.zeros_like(ab[:, :1]), ab[:, :-1]], axis=1)
        return jnp.concatenate([prev, ab], axis=2)

    qb = blocks(q)
    kb, vb = band(k), band(v)
    qi = jnp.arange(SWA_BLOCK)[:, None]
    kj = jnp.arange(2 * SWA_BLOCK)[None, :]
    dist = SWA_BLOCK + qi - kj
    in_window = (dist >= 0) & (dist < WINDOW)
    kpos = (jnp.arange(nb)[:, None] - 1) * SWA_BLOCK + kj
    mask = in_window[None] & (kpos >= 0)[:, None, :]
    o = sink_attention(qb, kb, vb, mask, sinks.reshape(SWA_KV_HEADS, SWA_GROUP))
    o = o.reshape(bsz, nb * SWA_BLOCK, SWA_WIDTH)[:, :t]
    y = swa_output(o, g, w_out)
    return y, k[:, t - WINDOW:], v[:, t - WINDOW:]


def swa_sample(xin, cache_k, cache_v, w_in, sinks, w_out):
    bsz, t, _ = xin.shape
    q, k, v, g = swa_project(xin, w_in)
    keys = jnp.concatenate([cache_k.astype(jnp.float32), k], axis=1)
    vals = jnp.concatenate([cache_v.astype(jnp.float32), v], axis=1)
    qi = jnp.arange(t)[:, None]
    kj = jnp.arange(WINDOW + t)[None, :]
    dist = WINDOW + qi - kj
    mask = ((dist >= 0) & (dist < WINDOW))[None]
    o = sink_attention(q[:, None], keys[:, None], vals[:, None], mask,
                       sinks.reshape(SWA_KV_HEADS, SWA_GROUP))
    o = o.reshape(bsz, t, SWA_WIDTH)
    y = swa_output(o, g, w_out)
    return y, keys[:, -WINDOW:], vals[:, -WINDOW:]


def setup_inputs(seed: int = 0) -> dict:
    key = jax.random.key(seed)
    ks = jax.random.split(key, 16)
    nrm = jax.random.normal
    f32 = jnp.float32
    swa_in_width = 2 * SWA_WIDTH + 2 * SWA_KV_WIDTH
    return {
        'x_prompt': nrm(ks[0], (BATCH, SEQ, D_MODEL), f32),
        'x_sample': nrm(ks[1], (DEC_BATCH, DEC_SEQ, D_MODEL), f32),
        'state_hgrn': 0.5 * nrm(ks[2], (N_HGRN_LAYERS, DEC_BATCH, HGRN_HEADS, HGRN_DK, HGRN_DV), f32),
        'cache_swa_k': nrm(ks[3], (N_SWA_LAYERS, DEC_BATCH, WINDOW, SWA_KV_HEADS, SWA_HEAD_DIM), f32),
        'cache_swa_v': nrm(ks[4], (N_SWA_LAYERS, DEC_BATCH, WINDOW, SWA_KV_HEADS, SWA_HEAD_DIM), f32),
        'meta_tokens': nrm(ks[5], (N_META, D_MODEL), f32),
        'hgrn_w_in': nrm(ks[6], (N_HGRN_LAYERS, D_MODEL, 4 * HGRN_WIDTH), f32) * D_MODEL ** -0.5,
        'hgrn_lb_logits': 0.1 * nrm(ks[7], (DEPTH + 1, HGRN_WIDTH), f32),
        'hgrn_norm_w': 1.0 + 0.02 * nrm(ks[8], (N_HGRN_LAYERS, HGRN_HEADS, HGRN_DV), f32),
        'hgrn_w_out': nrm(ks[9], (N_HGRN_LAYERS, HGRN_WIDTH, D_MODEL), f32) * (HGRN_WIDTH ** -0.5 * DEEPNORM_BETA),
        'swa_w_in': nrm(ks[10], (N_SWA_LAYERS, D_MODEL, swa_in_width), f32) * D_MODEL ** -0.5,
        'swa_sinks': 0.5 * nrm(ks[11], (N_SWA_LAYERS, SWA_Q_HEADS), f32),
        'swa_w_out': nrm(ks[12], (N_SWA_LAYERS, SWA_WIDTH, D_MODEL), f32) * (SWA_WIDTH ** -0.5 * DEEPNORM_BETA),
        'ln_g': 1.0 + 0.02 * nrm(ks[13], (DEPTH, D_MODEL), f32),
        'ln_b': 0.02 * nrm(ks[14], (DEPTH, D_MODEL), f32),
    }


def reference(x_prompt, x_sample, state_hgrn, cache_swa_k, cache_swa_v, meta_tokens,
              hgrn_w_in, hgrn_lb_logits, hgrn_norm_w, hgrn_w_out,
              swa_w_in, swa_sinks, swa_w_out, ln_g, ln_b):
    out_dtype = x_prompt.dtype
    f32 = jnp.float32
    bsz = x_prompt.shape[0]
    meta = jnp.broadcast_to(meta_tokens.astype(f32)[None], (bsz, N_META, D_MODEL))
    xp = jnp.concatenate([meta, x_prompt.astype(f32)], axis=1)
    xs = x_sample.astype(f32)
    lb_all = jnp.cumsum(jax.nn.softmax(hgrn_lb_logits.astype(f32), axis=0), axis=0)
    st_p, st_s, kp, vp, ksm, vsm = [], [], [], [], [], []
    for i in range(DEPTH):
        j = i // N_MIXERS
        if i % N_MIXERS == 0:
            lb = lb_all[i].reshape(HGRN_HEADS, HGRN_DK)
            s0p = jnp.zeros((bsz, HGRN_HEADS, HGRN_DK, HGRN_DV), f32)
            yp, sp = hgrn_branch(xp, hgrn_w_in[j], lb, hgrn_norm_w[j], hgrn_w_out[j], s0p, N_META, HGRN_CHUNK)
            ys, ss = hgrn_branch(xs, hgrn_w_in[j], lb, hgrn_norm_w[j], hgrn_w_out[j],
                                 state_hgrn[j].astype(f32), 0, xs.shape[1])
            st_p.append(sp)
            st_s.append(ss)
        else:
            yp, kpr, vpr = swa_prompt(xp, swa_w_in[j], swa_sinks[j], swa_w_out[j])
            ys, ksr, vsr = swa_sample(xs, cache_swa_k[j], cache_swa_v[j], swa_w_in[j], swa_sinks[j], swa_w_out[j])
            kp.append(kpr)
            vp.append(vpr)
            ksm.append(ksr)
            vsm.append(vsr)
        xp = layer_norm(DEEPNORM_ALPHA * xp + yp, ln_g[i], ln_b[i])
        xs = layer_norm(DEEPNORM_ALPHA * xs + ys, ln_g[i], ln_b[i])
    y_prompt = xp[:, N_META:].astype(out_dtype)
    y_sample = xs.astype(out_dtype)
    new_state_hgrn_prompt = jnp.stack(st_p).astype(out_dtype)
    new_state_hgrn_sample = jnp.stack(st_s).astype(out_dtype)
    new_cache_swa_k_prompt = jnp.stack(kp).astype(out_dtype)
    new_cache_swa_v_prompt = jnp.stack(vp).astype(out_dtype)
    new_cache_swa_k_sample = jnp.stack(ksm).astype(out_dtype)
    new_cache_swa_v_sample = jnp.stack(vsm).astype(out_dtype)
    return (y_prompt, y_sample, new_state_hgrn_prompt, new_state_hgrn_sample,
            new_cache_swa_k_prompt, new_cache_swa_v_prompt, new_cache_swa_k_sample, new_cache_swa_v_sample)
```

```python
import contextlib
import numpy as np
import concourse.bass as bass
import concourse.mybir as mybir
from concourse.bass_utils import run_bass_kernel_spmd

F32 = mybir.dt.float32
BF16 = mybir.dt.bfloat16
AF = mybir.ActivationFunctionType
ALU = mybir.AluOpType

ENGS = ("pe", "act", "dve", "pool", "sp")

D = 2048
NCH = 16
NPRE = 1024
NMP = 1152
NM = 1280
NX = NPRE + NMP
ALPHA = (2.0 * 2) ** 0.25
LN_EPS = 1e-5
RMS_EPS = 1e-6
SHIFT = 30.0
NSLAB = 136
DEBUG = False

C_ID, C_MC, C_MP, C_MP1, C_MS, C_RP, C_RM2, C_SEL, C_MCA, C_ONE = 0, 128, 256, 384, 512, 640, 1152, 1408, 1424, 1432
NCONST = 1560
P_LBL, P_NW, P_LNG, P_LNB, P_SINK = 0, 48, 64, 96, 128
NPAR = 144


class Res:
    __slots__ = ("name", "writer", "readers")

    def __init__(self, name):
        self.name = name
        self.writer = None
        self.readers = {}


class Prog:
    def __init__(self, nc, n_dma_sems=(16, 8, 12)):
        self.nc = nc
        self.ops = {e: [] for e in ENGS}
        self.count = {e: 0 for e in ENGS}
        self.waited = {e: {} for e in ENGS}
        self.dma_ring = {"sp": [("dsp", i) for i in range(n_dma_sems[0])],
                         "act": [("dact", i) for i in range(n_dma_sems[1])],
                         "pool": [("dpool", i) for i in range(n_dma_sems[2])]}
        self.dma_pos = {"sp": 0, "act": 0, "pool": 0}
        self.dma_total = {}
        for q in self.dma_ring:
            for k in self.dma_ring[q]:
                self.dma_total[k] = 0

    def _collect(self, eng, reads, writes):
        need = {}

        def add(tok):
            if tok is None:
                return
            k, v = tok
            if need.get(k, 0) < v:
                need[k] = v
        for r in reads:
            add(r.writer)
        for w in writes:
            add(w.writer)
            for k, v in w.readers.items():
                add((k, v))
        out = []
        wd = self.waited[eng]
        for k, v in need.items():
            if eng == "pe" and k == "pe":
                continue
            if wd.get(k, 0) >= v:
                continue
            wd[k] = v
            out.append((k, v))
        return out

    def _mark(self, tok, reads, writes):
        k, v = tok
        for r in reads:
            if r.readers.get(k, 0) < v:
                r.readers[k] = v
        for w in writes:
            w.writer = tok
            w.readers = {}

    def op(self, eng, fn, reads=(), writes=()):
        bk = [r for r in reads if r.name.startswith("bank")]
        if bk:
            reads = [r for r in reads if not r.name.startswith("bank")]
            writes = list(writes) + [b for b in bk if b not in writes]
        waits = self._collect(eng, reads, writes)
        self.count[eng] += 1
        tok = (eng, self.count[eng])
        self._mark(tok, reads, writes)
        self.ops[eng].append((waits, fn, (eng, 1)))
        return tok

    def dma(self, q, fn, reads=(), writes=()):
        ring = self.dma_ring[q]
        key = ring[self.dma_pos[q] % len(ring)]
        self.dma_pos[q] += 1
        waits = self._collect(q, reads, writes)
        prev = self.dma_total[key]
        if prev > 0 and self.waited[q].get(key, 0) < prev:
            self.waited[q][key] = prev
            waits.append((key, prev))
        self.dma_total[key] = prev + 16
        tok = (key, prev + 16)
        self._mark(tok, reads, writes)
        self.ops[q].append((waits, fn, (key, 16)))
        return tok

    def barrier(self):
        toks = [(e, self.count[e]) for e in ("pe", "act", "dve", "pool") if self.count[e] > 0]
        toks += [(k, v) for k, v in self.dma_total.items() if v > 0]
        for e in ENGS:
            waits = []
            for k, v in toks:
                if self.waited[e].get(k, 0) >= v:
                    continue
                self.waited[e][k] = v
                waits.append((k, v))
            if waits:
                self.ops[e].append((waits, None, None))

    def emit(self):
        nc = self.nc
        keys = ["pe", "act", "dve", "pool"] + list(self.dma_total)
        with contextlib.ExitStack() as st:
            sems = {}
            for k in keys:
                nm = k if isinstance(k, str) else f"{k[0]}{k[1]}"
                sems[k] = st.enter_context(nc.semaphore("s_" + nm))
            block = st.enter_context(nc.Block())

            def run(engname):
                def body(e):
                    for waits, fn, inc in self.ops[engname]:
                        for k, v in waits:
                            e.wait_ge(sems[k], v)
                        if fn is None:
                            continue
                        ins = fn(e)
                        ins.then_inc(sems[inc[0]], inc[1])
                return body
            block.sync(run("sp"))
            block.tensor(run("pe"))
            block.scalar(run("act"))
            block.vector(run("dve"))
            block.gpsimd(run("pool"))


class Builder:
    def __init__(self, nc, stop_after=None):
        self.nc = nc
        self.P = Prog(nc)
        self.res = {}
        self.stop_after = stop_after

    def R(self, name):
        r = self.res.get(name)
        if r is None:
            r = self.res[name] = Res(name)
        return r

    def act(self, out, in_, func, reads, writes, bias=None, scale=None):
        kw = {}
        if bias is not None:
            kw["bias"] = bias
        if scale is not None:
            kw["scale"] = scale
        self.P.op("act", lambda e: e.activation(out=out, in_=in_, func=func, **kw), reads, writes)

    def tt(self, eng, out, in0, in1, op, reads, writes):
        self.P.op(eng, lambda e: e.tensor_tensor(out=out, in0=in0, in1=in1, op=op), reads, writes)

    def ts(self, eng, out, in0, s1, op0, reads, writes, s2=None, op1=None):
        if op1 is None:
            self.P.op(eng, lambda e: e.tensor_scalar(out=out, in0=in0, scalar1=s1, scalar2=None, op0=op0), reads, writes)
        else:
            self.P.op(eng, lambda e: e.tensor_scalar(out=out, in0=in0, scalar1=s1, scalar2=s2, op0=op0, op1=op1), reads, writes)

    def stt(self, out, in0, scalar, in1, op0, op1, reads, writes):
        self.P.op("dve", lambda e: e.scalar_tensor_tensor(out=out, in0=in0, scalar=scalar, in1=in1, op0=op0, op1=op1), reads, writes)

    def copy(self, eng, out, in_, reads, writes):
        if eng == "act":
            self.P.op("act", lambda e: e.activation(out=out, in_=in_, func=AF.Copy), reads, writes)
        else:
            self.P.op(eng, lambda e: e.tensor_copy(out=out, in_=in_), reads, writes)

    def memset(self, eng, ap, val, writes):
        self.P.op(eng, lambda e: e.memset(ap, val), (), writes)

    def mm(self, mms, reads, writes):
        def fn(e):
            ins = None
            for (o, l, r, s0, s1) in mms:
                ins = e.matmul(o, lhsT=l, rhs=r, start=s0, stop=s1)
            return ins
        self.P.op("pe", fn, reads, writes)

    def tr(self, trs, reads, writes):
        def fn(e):
            ins = None
            for (o, i, ident) in trs:
                ins = e.transpose(out=o, in_=i, identity=ident)
            return ins
        self.P.op("pe", fn, reads, writes)

    def dma(self, q, out, in_, reads, writes):
        self.P.dma(q, lambda e: e.dma_start(out=out, in_=in_), reads, writes)

    def slab_setup(self, schedule):
        self.sched = schedule
        self.sl_issued = 0
        self.sl_used = 0

    def slab_issue_upto(self, n):
        n = min(n, len(self.sched))
        while self.sl_issued < n:
            i = self.sl_issued
            slot = i % 8
            sid = self.sched[i]
            dst = self.RING[:, slot * 2048:(slot + 1) * 2048]
            self.dma("pool", dst, self.wall[sid], [], [self.R(f"slot{slot}")])
            self.sl_issued += 1

    def next_slab(self, sid):
        i = self.sl_used
        assert self.sched[i] == sid, (i, self.sched[i], sid)
        if getattr(self, "auto_prefetch", True):
            self.slab_issue_upto(i + 5)
        self.sl_used += 1
        slot = i % 8
        ap = self.RING[:, slot * 2048:(slot + 1) * 2048].rearrange("p (c e) -> p c e", c=NCH)
        return ap, self.R(f"slot{slot}")

    def next_bank(self):
        b = self.bank_pos % self.n_ring
        self.bank_pos += 1
        return b

    def proj(self, slab, slabres, X, xres, c0, n):
        b = self.next_bank()
        out = self.ps32[:, b * 512:b * 512 + n]
        mms = [(out, slab[:, c, :], X[:, c, c0:c0 + n], c == 0, c == NCH - 1) for c in range(NCH)]
        self.mm(mms, [slabres] + list(xres), [self.R(f"bank{b}")])
        return out, self.R(f"bank{b}")

    def build(self):
        nc = self.nc
        dt_in = lambda name, shape: nc.dram_tensor(name, shape, F32, kind="ExternalInput").ap()
        dt_out = lambda name, shape: nc.dram_tensor(name, shape, F32, kind="ExternalOutput").ap()
        self.xT = dt_in("xT", [128, NCH, NX])
        self.xsT = dt_in("xsT", [128, NCH, 128])
        self.s0 = dt_in("s0", [16, 16, 128, 128])
        self.ckT = dt_in("ckT", [16, 4, 64, 128])
        self.cv = dt_in("cv", [16, 128, 256])
        self.ck = dt_in("ck", [16, 128, 256])
        self.wall = dt_in("wall", [NSLAB, 128, 2048])
        self.consts = dt_in("consts", [128, NCONST])
        self.pars = dt_in("pars", [128, NPAR])
        self.yT = dt_out("yT", [128, NCH, NMP])
        self.sp_out = dt_out("sp_out", [16, 128, 128])
        self.ss_out = dt_out("ss_out", [16, 16, 128, 128])
        self.kp = dt_out("kp", [128, 256])
        self.vp = dt_out("vp", [128, 256])
        self.ks = dt_out("ks", [16, 128, 256])
        self.vs = dt_out("vs", [16, 128, 256])
        self.x1T = nc.dram_tensor("x1T", [128, NCH, NM], F32, kind="Internal").ap()
        if DEBUG:
            self.dbg_x1 = dt_out("dbg_x1", [128, NCH, NM])
            self.dbg_ot = dt_out("dbg_ot", [128, NCH, NM])

        with contextlib.ExitStack() as st:
            E = st.enter_context
            self.XTm_t = E(nc.sbuf_tensor("XTm", [128, NCH * NM], BF16))
            self.OTa_t = E(nc.sbuf_tensor("OTa", [128, NCH * NM], BF16))
            self.RING = E(nc.sbuf_tensor("RING", [128, 8 * 2048], BF16))
            self.C32 = E(nc.sbuf_tensor("C32", [128, NCONST], F32))
            self.C16 = E(nc.sbuf_tensor("C16", [128, NCONST], BF16))
            self.PAR = E(nc.sbuf_tensor("PAR", [128, 256], F32))
            self.SCR = E(nc.sbuf_tensor("SCR", [128, 21800], F32))
            self.ps32 = E(nc.psum_tensor("ps32", [128, 7 * 512], F32))
            self.ps16 = E(nc.psum_tensor("ps16", [128, 1024], BF16))
            self.XTm = self.XTm_t[:].rearrange("p (c n) -> p c n", c=NCH)
            self.OTa = self.OTa_t[:].rearrange("p (c n) -> p c n", c=NCH)
            self.program()
            self.P.barrier()
            self.P.emit()

    def carve_reset(self):
        self.cpos = 0

    def c32(self, n):
        a = self.SCR[:, self.cpos:self.cpos + n]
        self.cpos += n
        assert self.cpos <= 21800, self.cpos
        return a

    def c16(self, n):
        n32 = (n + 1) // 2
        a = self.SCR[:, self.cpos:self.cpos + n32].bitcast(BF16)
        self.cpos += n32
        assert self.cpos <= 21800, self.cpos
        return a

    def program(self):
        R = self.R
        sched = []
        for h in range(16):
            sched += [h, 16 + h, 32 + h, 48 + h]
        sched += [64 + j for j in range(16)] * 2
        sched += [116, 117, 118, 119, 112, 113, 114, 115]
        for kvh in range(4):
            sched += [80 + 4 * kvh + i for i in range(4)] + [96 + 4 * kvh + i for i in range(4)]
        sched += [120 + j for j in range(16)] * 2
        self.slab_setup(sched)

        self.dma("sp", self.C32[:], self.consts, [], [R("C32")])
        self.dma("sp", self.PAR[:, 0:NPAR], self.pars, [], [R("PAR")])
        self.copy("act", self.C16[:], self.C32[:], [R("C32")], [R("C16")])
        self.setup_params()
        self.hgrn()
        if self.stop_after == "hgrn":
            return
        self.P.barrier()
        self.wout_ln(0)
        if self.stop_after == "ln0":
            return
        self.P.barrier()
        self.swa()
        if self.stop_after in ("swa0", "swa1a", "swa1b", "swa"):
            return
        self.P.barrier()
        self.wout_ln(1)

    def setup_params(self):
        R = self.R
        PAR = self.PAR
        l0, l1, l2 = PAR[:, 0:16], PAR[:, 16:32], PAR[:, 32:48]
        mx = PAR[:, 144:160]
        ex = PAR[:, 160:208]
        sm = PAR[:, 208:224]
        self.LB = PAR[:, 224:240]
        self.C1 = PAR[:, 240:256]
        rP = [R("PAR")]
        self.tt("dve", mx, l0, l1, ALU.max, rP, [R("pmx")])
        self.tt("dve", mx, mx, l2, ALU.max, rP + [R("pmx")], [R("pmx")])
        for i in range(3):
            self.tt("dve", ex[:, 16 * i:16 * i + 16], PAR[:, 16 * i:16 * i + 16], mx, ALU.subtract, rP + [R("pmx")], [R(f"pex{i}")])
        self.act(ex, ex, AF.Exp, [R("pex0"), R("pex1"), R("pex2")], [R("pex")])
        self.tt("dve", sm, ex[:, 0:16], ex[:, 16:32], ALU.add, [R("pex")], [R("psm")])
        self.tt("dve", sm, sm, ex[:, 32:48], ALU.add, [R("pex"), R("psm")], [R("psm")])
        self.P.op("dve", lambda e: e.reciprocal(out=sm, in_=sm), [R("psm")], [R("psm")])
        self.tt("dve", self.LB, ex[:, 0:16], sm, ALU.mult, [R("pex"), R("psm")], [R("LB")])
        self.act(self.C1, self.LB, AF.Ln, [R("LB")], [R("C1")], bias=1.0, scale=-1.0)
        self.ESINK = PAR[:, P_SINK:P_SINK + 16]
        self.act(self.ESINK, self.ESINK, AF.Exp, rP, [R("ESINK")], bias=-SHIFT)

    def hgrn(self):
        R = self.R
        self.carve_reset()
        XTp_t = self.c16(NCH * NPRE)
        XTp = XTp_t.rearrange("p (c n) -> p c n", c=NCH)
        QT = [self.c16(512) for _ in range(2)]
        KT = [self.c16(512) for _ in range(2)]
        KH = [self.c16(512) for _ in range(2)]
        VT = [self.c16(512) for _ in range(2)]
        GATE = [self.c16(512) for _ in range(2)]
        VK = [self.c16(256) for _ in range(3)]
        ATm = [self.c16(128) for _ in range(3)]
        SQ = [self.c16(128) for _ in range(3)]
        VbAll = self.c16(2048)
        Vb = [VbAll[:, 512 * i:512 * (i + 1)] for i in range(4)]
        self.S0BF = VbAll
        self.VKS = self.c16(256)
        U, LA, Bb, L1, SG = [self.c32(512) for _ in range(5)]
        EQ = U
        DK = LA
        Sbf = [self.c16(128) for _ in range(2)]
        RS = [self.c32(128) for _ in range(3)]
        T1 = [self.c32(128) for _ in range(3)]
        S = [self.c32(128) for _ in range(2)]
        S0b = [self.c32(2048) for _ in range(2)]
        CB = [self.c32(8) for _ in range(2)]
        EBL = [self.c32(8) for _ in range(2)]
        CBs = [self.c32(16) for _ in range(2)]
        EBLs = [self.c32(16) for _ in range(2)]
        self.n_ring = 4
        self.bank_pos = 0
        self.auto_prefetch = False
        C32, C16 = self.C32, self.C16
        ident16 = C16[:, C_ID:C_ID + 128]
        ones16 = C16[:, C_ONE:C_ONE + 128]

        xgroups = [("pre", 0, 512), ("pre", 512, 512), ("main", 0, 512), ("main", 512, 512), ("main", 1024, 256)]
        self.dma("pool", XTp[:, :, 0:512], self.xT[:, :, 0:512], [], [R("XTp_0")])
        self.slab_issue_upto(4)
        self.dma("pool", XTp[:, :, 512:1024], self.xT[:, :, 512:1024], [], [R("XTp_1")])
        self.dma("pool", self.XTm[:, :, 0:512], self.xT[:, :, NPRE:NPRE + 512], [], [R("XTm_0")])
        self.dma("pool", self.XTm[:, :, 512:1024], self.xT[:, :, NPRE + 512:NPRE + 1024], [], [R("XTm_1")])
        self.dma("pool", self.XTm[:, :, 1024:1152], self.xT[:, :, NPRE + 1024:NPRE + 1152], [], [R("XTm_2")])
        self.dma("pool", self.XTm[:, :, 1152:1280], self.xsT, [], [R("XTm_2")])

        def load_s0(h):
            dst = S0b[h % 2].rearrange("p (s v) -> p s v", s=16)
            src = self.s0[:, h].rearrange("s k v -> k s v")
            self.dma("sp", dst, src, [], [R(f"S0b{h % 2}")])

        class Item:
            pass
        items = []
        for h in range(16):
            for gidx, (kind, c0, n) in enumerate(xgroups):
                it = Item()
                it.h, it.kind, it.c0, it.n, it.gidx = h, kind, c0, n, gidx
                it.main = kind == "main"
                it.m2 = it.main and c0 == 1024
                it.nt = n // 128
                it.gb = len(items) % 2
                items.append(it)
        heads = {}
        tstate = {"ti": 0}

        def setup_head(h):
            hd = Item()
            hd.sq, hd.rq = self.next_slab(h)
            hd.sf, hd.rf = self.next_slab(16 + h)
            hd.si, hd.ri = self.next_slab(32 + h)
            hd.sg, hd.rg = self.next_slab(48 + h)
            self.slab_issue_upto(4 * (h + 2))
            hd.Sh = S[h % 2]
            hd.rS = R(f"S{h % 2}")
            self.memset("dve", hd.Sh, 0.0, [hd.rS])
            hd.Sbf = Sbf[h % 2]
            hd.rSbf = R(f"Sbf{h % 2}")
            self.memset("dve", hd.Sbf, 0.0, [hd.rSbf])
            hd.lbh = self.LB[:, h:h + 1]
            hd.c1h = self.C1[:, h:h + 1]
            hd.nwh = self.PAR[:, P_NW + h:P_NW + h + 1]
            heads[h] = hd

        rpar = [R("LB"), R("C1"), R("PAR")]

        def proj_chunks(it):
            hd = heads[it.h]
            if it.main:
                X, xres = self.XTm, [R(f"XTm_{it.c0 // 512}")]
            else:
                X, xres = XTp, [R(f"XTp_{it.c0 // 512}")]

            def mk(attr, rattr, slab, rsl):
                def f():
                    o, r = self.proj(slab, rsl, X, xres, it.c0, it.n)
                    setattr(it, attr, o)
                    setattr(it, rattr, r)
                return f
            ch = [mk("pf", "rpf", hd.sf, hd.rf), mk("pi", "rpi", hd.si, hd.ri)]
            if it.main:
                ch += [mk("pq", "rpq", hd.sq, hd.rq), mk("pg", "rpg", hd.sg, hd.rg)]
            return ch

        def proj_item(it):
            for f in proj_chunks(it):
                f()

        def gatingA(it):
            hd = heads[it.h]
            n, gb, m2, nt = it.n, it.gb, it.m2, it.nt
            u, la, bb, l1, dk = U[:, :n], LA[:, :n], Bb[:, :n], L1[:, :n], DK[:, :n]
            ops = []
            ops.append(lambda: self.act(u, it.pf, AF.Exp, [it.rpf], [R("U")]))
            ops.append(lambda: self.act(la, u, AF.Ln, [R("U")] + rpar, [R("LA")], bias=hd.lbh))
            ops.append(lambda: self.act(l1, u, AF.Ln, [R("U")], [R("L1")], bias=1.0))
            ops.append(lambda: self.tt("dve", la, la, l1, ALU.subtract, [R("LA"), R("L1")], [R("LA")]))
            rst = C32[:, C_RM2:C_RM2 + 256] if m2 else C32[:, C_RP:C_RP + n]
            ops.append(lambda: self.P.op("dve", lambda e: e.tensor_tensor_scan(out=bb, data0=rst, data1=la, initial=0.0, op0=ALU.mult, op1=ALU.add),
                                         [R("LA"), R("C32")], [R("B")]))
            it.nA1a = len(ops)
            ops.append(lambda: self.copy("act", VT[gb][:, :n], it.pi, [it.rpi], [R(f"VT{gb}")]))
            ntp = 1 if m2 else nt
            blast = Bb[:, 127:128 * ntp:128]
            it.nA1 = len(ops)
            ops.append(lambda: self.act(EBL[gb][:, 0:ntp], blast, AF.Exp, [R("B")], [R(f"EBL{gb}")]))
            ops.append(lambda: self.ts("dve", CB[gb][:, 0:ntp], blast, hd.c1h, ALU.add, [R("B")] + rpar, [R(f"CB{gb}")]))
            if m2:
                bl_s = Bb[:, 128 + 7:256:8]
                ops.append(lambda: self.act(EBLs[gb][:, :], bl_s, AF.Exp, [R("B")], [R(f"EBLs{gb}")]))
                ops.append(lambda: self.ts("dve", CBs[gb][:, :], bl_s, hd.c1h, ALU.add, [R("B")] + rpar, [R(f"CBs{gb}")]))
            ops.append(lambda: self.tt("dve", l1, bb, l1, ALU.add, [R("B"), R("L1")], [R("L1")]))
            dkv = DK[:, 0:128 * ntp].rearrange("p (t n) -> p t n", n=128)
            zv = L1[:, 0:128 * ntp].rearrange("p (t n) -> p t n", n=128)
            cbv = CB[gb][:, 0:ntp].unsqueeze(2).broadcast_to([128, ntp, 128])
            ops.append(lambda: self.tt("dve", dkv, cbv, zv, ALU.subtract, [R(f"CB{gb}"), R("L1")], [R("LA")]))
            if m2:
                dkv2 = DK[:, 128:256].rearrange("p (s n) -> p s n", n=8)
                zv2 = L1[:, 128:256].rearrange("p (s n) -> p s n", n=8)
                cbv2 = CBs[gb][:, :].unsqueeze(2).broadcast_to([128, 16, 8])
                ops.append(lambda: self.tt("dve", dkv2, cbv2, zv2, ALU.subtract, [R(f"CBs{gb}"), R("L1")], [R("LA")]))
            ops.append(lambda: self.act(KH[gb][:, :n], dk, AF.Exp, [R("LA")], [R(f"KH{gb}")]))
            return ops

        def gatingB(it):
            if not it.main:
                return []
            hd = heads[it.h]
            n, gb = it.n, it.gb
            bb, l1, eq, sgt = Bb[:, :n], L1[:, :n], EQ[:, :n], SG[:, :n]
            ops = []
            ops.append(lambda: self.act(eq, bb, AF.Exp, [R("B")], [R("U")]))
            ops.append(lambda: self.act(KT[gb][:, :n], l1, AF.Exp, [R("L1")] + rpar, [R(f"KT{gb}")], bias=hd.c1h, scale=-1.0))
            ops.append(lambda: self.tt("dve", QT[gb][:, :n], it.pq, eq, ALU.mult, [it.rpq, R("U")], [R(f"QT{gb}")]))
            ops.append(lambda: self.act(sgt, it.pg, AF.Exp, [it.rpg], [R("SG")], scale=-1.0))
            ops.append(lambda: self.act(sgt, sgt, AF.Ln, [R("SG")], [R("SG")], bias=1.0))
            ops.append(lambda: self.act(sgt, sgt, AF.Exp, [R("SG")], [R("SG")], scale=-1.0))
            ops.append(lambda: self.stt(GATE[gb][:, :n], it.pg, hd.nwh, sgt, ALU.mult, ALU.mult, [it.rpg, R("SG"), R("PAR")], [R(f"GATE{gb}")]))
            return ops

        def mk_tile(it, t):
            td = Item()
            td.it, td.t = it, t
            td.tb = tstate["ti"] % 3
            tstate["ti"] += 1
            td.stage = 0
            return td

        def tile_views(td):
            tb = td.tb
            sm0 = (4 + tb) * 512
            v = Item()
            v.AT = self.ps32[:, sm0:sm0 + 128]
            v.Op = self.ps32[:, sm0 + 128:sm0 + 256]
            v.SSp = self.ps32[:, sm0 + 256:sm0 + 384]
            v.SPp = self.ps32[:, sm0 + 384:sm0 + 512]
            v.psb = self.ps16[:, tb * 256:(tb + 1) * 256]
            v.bk = R(f"bank{4 + tb}")
            v.cs = slice(td.t * 128, (td.t + 1) * 128)
            v.sample = td.it.m2 and td.t == 1
            return v

        def stageA(td):
            it, tb = td.it, td.tb
            gb = it.gb
            v = tile_views(td)
            self.tr([(v.psb[:, 0:128], VT[gb][:, v.cs], ident16), (v.psb[:, 128:256], KH[gb][:, v.cs], ident16)],
                    [R(f"VT{gb}"), R(f"KH{gb}"), R("C16")], [R("bank7")])
            self.copy("act", VK[tb][:, :], v.psb, [R("bank7")], [R(f"VK{tb}")])
            if it.main:
                self.mm([(v.AT, KT[gb][:, v.cs], QT[gb][:, v.cs], True, True)], [R(f"KT{gb}"), R(f"QT{gb}")], [v.bk])
                mask = C32[:, C_MS:C_MS + 128] if v.sample else C32[:, C_MC:C_MC + 128]
                self.tt("dve", ATm[tb][:, :], v.AT, mask, ALU.mult, [v.bk, R("C32")], [R(f"ATm{tb}")])

        def stageB(td):
            it, tb = td.it, td.tb
            hd = heads[it.h]
            h, gb = it.h, it.gb
            Sh, rS = hd.Sh, hd.rS
            v = tile_views(td)
            if it.main:
                if not v.sample:
                    self.mm([(v.Op, VK[tb][:, 0:128], ATm[tb][:, :], True, False),
                             (v.Op, hd.Sbf, QT[gb][:, v.cs], False, True)],
                            [R(f"VK{tb}"), R(f"ATm{tb}"), hd.rSbf, R(f"QT{gb}")], [v.bk])
                else:
                    s0v = self.S0BF.rearrange("p (s v) -> p s v", s=16)
                    mms = [(v.Op, VK[tb][:, 0:128], ATm[tb][:, :], True, False)]
                    for sq_ in range(16):
                        mms.append((v.Op[:, sq_ * 8:(sq_ + 1) * 8], s0v[:, sq_, :], QT[gb][:, 128 + sq_ * 8:128 + (sq_ + 1) * 8], False, sq_ == 15))
                    self.mm(mms, [R(f"VK{tb}"), R(f"ATm{tb}"), R("Vb0"), R("Vb1"), R("Vb2"), R("Vb3"), R(f"QT{gb}")], [v.bk])
            if not v.sample:
                self.mm([(v.SPp, VK[tb][:, 128:256], VK[tb][:, 0:128], True, True)], [R(f"VK{tb}")], [v.bk])
                self.stt(Sh, Sh, EBL[gb][:, td.t:td.t + 1], v.SPp, ALU.mult, ALU.add, [rS, R(f"EBL{gb}"), v.bk], [rS])
                if it.main or it.gidx == 1:
                    self.copy("act", hd.Sbf, Sh, [rS], [hd.rSbf])
                if it.m2:
                    self.dma("sp", self.sp_out[h], Sh, [rS], [])
            else:
                self.copy("dve", self.VKS[:, :], VK[tb][:, :], [R(f"VK{tb}")], [R("VKS")])
            if it.main:
                self.act(SQ[tb][:, :], v.Op, AF.Square, [v.bk], [R(f"SQ{tb}")])

        def stageC(td):
            it, tb = td.it, td.tb
            if not it.main:
                return
            gb = it.gb
            v = tile_views(td)
            mcol = it.c0 + td.t * 128
            self.mm([(v.SSp, ones16, SQ[tb][:, :], True, True)], [R(f"SQ{tb}"), R("C16")], [v.bk])
            self.act(RS[tb][:, :], v.SSp, AF.Ln, [v.bk], [R(f"RS{tb}")], bias=RMS_EPS, scale=1.0 / 128.0)
            self.act(RS[tb][:, :], RS[tb][:, :], AF.Exp, [R(f"RS{tb}")], [R(f"RS{tb}")], scale=-0.5)
            self.tt("dve", T1[tb][:, :], v.Op, RS[tb][:, :], ALU.mult, [v.bk, R(f"RS{tb}")], [R(f"T1{tb}")])
            self.tt("pool", self.OTa[:, it.h, mcol:mcol + 128], T1[tb][:, :], GATE[gb][:, v.cs], ALU.mult,
                    [R(f"T1{tb}"), R(f"GATE{gb}")], [R(f"OTa_{mcol // 128}")])

        tqueue = []

        def step(td_new):
            if td_new is not None:
                stageA(td_new)
            for td in reversed(tqueue):
                if td.stage == 1:
                    stageB(td)
                    td.stage = 2
                elif td.stage == 2:
                    stageC(td)
                    td.stage = 3
            tqueue[:] = [td for td in tqueue if td.stage < 3]
            if td_new is not None:
                td_new.stage = 1
                tqueue.append(td_new)

        def sample_state(it):
            h, gb = it.h, it.gb
            s0v = S0b[h % 2].rearrange("p (s v) -> p s v", s=16)
            VKs = self.VKS
            holder = {}

            def vb_op(q4):
                def f():
                    vbv = Vb[q4].rearrange("p (s v) -> p s v", s=4)
                    vin = VKs[:, 0:128].unsqueeze(1).broadcast_to([128, 4, 128])
                    sel = C16[:, C_SEL + 4 * q4:C_SEL + 4 * q4 + 4].unsqueeze(2).broadcast_to([128, 4, 128])
                    self.tt("dve", vbv, vin, sel, ALU.mult, [R("VKS"), R("C16")], [R(f"Vb{q4}")])
                return f

            def mm_op(q4):
                def f():
                    b = self.next_bank()
                    holder[q4] = b
                    sps = self.ps32[:, b * 512:(b + 1) * 512]
                    self.mm([(sps, VKs[:, 128:256], Vb[q4], True, True)], [R("VKS"), R(f"Vb{q4}")], [R(f"bank{b}")])
                return f

            def wb_op(q4):
                def f():
                    b = holder[q4]
                    sps = self.ps32[:, b * 512:(b + 1) * 512]
                    for s_ in range(4):
                        sq_ = q4 * 4 + s_
                        self.stt(s0v[:, sq_, :], s0v[:, sq_, :], EBLs[gb][:, sq_:sq_ + 1], sps[:, s_ * 128:(s_ + 1) * 128],
                                 ALU.mult, ALU.add, [R(f"S0b{h % 2}"), R(f"EBLs{gb}"), R(f"bank{b}")], [R(f"S0b{h % 2}")])
                return f
            dst = self.ss_out[:, h].rearrange("s k v -> k s v")
            out_op = lambda: self.dma("sp", dst, s0v, [R(f"S0b{h % 2}")], [])
            chunks = [[vb_op(0), vb_op(1), vb_op(2), vb_op(3)],
                      [mm_op(0), mm_op(1)],
                      [wb_op(0), wb_op(1), mm_op(2), mm_op(3)],
                      [wb_op(2), wb_op(3), out_op]]
            return chunks

        def run(ops):
            for o in ops:
                o()

        load_s0(0)
        setup_head(0)
        proj_item(items[0])
        run(gatingA(items[0]))
        run(gatingB(items[0]))
        deferred = []
        for i, it in enumerate(items):
            nxt = items[i + 1] if i + 1 < len(items) else None
            P, A1, A2, B = [], [], [], []
            if nxt is not None:
                if nxt.gidx == 0:
                    setup_head(nxt.h)
                P = proj_chunks(nxt)
            avail = []
            if it.m2:
                self.copy("dve", self.S0BF[:, :], S0b[it.h % 2][:, :], [R(f"S0b{it.h % 2}")], [R("Vb0"), R("Vb1"), R("Vb2"), R("Vb3")])
            nsteps = it.nt
            T = max(nsteps, len(P))
            for t in range(T):
                if t < len(P):
                    P[t]()
                if t == 0 and nxt is not None:
                    ga = gatingA(nxt)
                    run(ga[:nxt.nA1a])
                if t == 1 and nxt is not None:
                    run(ga[nxt.nA1a:nxt.nA1])
                    A2 = ga[nxt.nA1:]
                if t == 3 and nxt is not None and nxt.main:
                    B = gatingB(nxt)
                if t < nsteps:
                    step(mk_tile(it, t))
                if t == 0:
                    avail = [(lambda ch: (lambda: run(ch)))(ch) for ch in deferred] + avail
                    deferred = []
                if t == 1:
                    run(A2)
                if t == 3:
                    avail = avail + B
                remaining = max(T - t - 1, 0)
                k = -(-len(avail) // (remaining + 1)) if avail else 0
                run(avail[:k])
                avail = avail[k:]
            run(avail)
            if it.gidx == 0 and it.h + 1 < 16:
                load_s0(it.h + 1)
            if it.m2:
                deferred = sample_state(it)
        step(None)
        step(None)
        for ch in deferred:
            run(ch)
        self.slab_issue_upto(self.sl_used + 8)
        self.auto_prefetch = True
        if DEBUG:
            self.dbg_dump_bf16(self.dbg_ot, self.OTa, [R(f"OTa_{i}") for i in range(10)])

    def dbg_dump_bf16(self, dst, src3, reads):
        self.P.barrier()
        tmp = self.SCR[:, 21800 - 1280:21800]
        for c in range(NCH):
            self.copy("dve", tmp, src3[:, c, :], reads, [self.R("dbgtmp")])
            self.dma("sp", dst[:, c, :], tmp, [self.R("dbgtmp")], [self.R("dbgout")])
        self.P.barrier()

    def wout_ln(self, layer):
        R = self.R
        self.carve_reset()
        ZW = 640
        Z_t = self.c32(NCH * ZW)
        Z = Z_t.rearrange("p (c n) -> p c n", c=NCH)
        ACC = [self.c32(ZW), self.c32(ZW)]
        ACC2 = [self.c32(ZW), self.c32(ZW)]
        MEANb = self.c32(ZW)
        MEAN = [MEANb, MEANb]
        RSTD = [self.c32(ZW), self.c32(ZW)]
        NMR = [self.c32(ZW), self.c32(ZW)]
        M2 = self.c32(512)
        SQT = [self.c32(512) for _ in range(2)]
        XR = [self.c32(512) for _ in range(4)]
        XO = [self.c32(512) for _ in range(4)]
        self.n_ring = 5
        ones32 = self.C32[:, C_ONE:C_ONE + 128]
        if layer == 0:
            passes = [[(0, 512), (512, 128)], [(640, 512), (1152, 128)]]
            sbase = 64
        else:
            passes = [[(128, 512), (640, 128)], [(768, 384), (1152, 128)]]
            sbase = 120
        cnt = {"k": 0, "ko": 0}
        if layer == 0:
            self.dma("sp", self.ks[:, 0:120, :], self.ck[:, 8:128, :], [], [R("ks_copy")])
            self.dma("sp", self.vs[:, 0:120, :], self.cv[:, 8:128, :], [], [R("vs_copy")])

        steps = [(pi, j, c0, n) for pi in range(2) for j in range(NCH) for (c0, n) in passes[pi]]
        xrs = {"issued": 0}

        def issue_xr(upto):
            upto = min(upto, len(steps))
            while xrs["issued"] < upto:
                i = xrs["issued"]
                pi_, j_, c0, n = steps[i]
                xb = i % 4
                xr = XR[xb][:, :n]
                if layer == 0:
                    src = self.xsT[:, j_, :] if c0 >= NMP else self.xT[:, j_, NPRE + c0:NPRE + c0 + n]
                else:
                    src = self.x1T[:, j_, c0:c0 + n]
                if layer == 1:
                    l0_pieces = [(0, 512), (512, 128), (640, 512), (1152, 128)]
                    xdeps = [R(f"x1T_{j_}_{p0_}") for (p0_, pn_) in l0_pieces if p0_ < c0 + n and c0 < p0_ + pn_]
                else:
                    xdeps = []
                self.dma("sp", xr, src, xdeps, [R(f"XR{xb}")])
                xrs["issued"] += 1

        def accumulate(pi, j):
            cgs = passes[pi]
            p0 = cgs[0][0]
            slab, rsl = self.next_slab(sbase + j)
            for (c0, n) in cgs:
                kb = cnt["k"] % 2
                xb = cnt["k"] % 4
                assert steps[cnt["k"]] == (pi, j, c0, n)
                issue_xr(cnt["k"] + 3)
                cnt["k"] += 1
                xr = XR[xb][:, :n]
                y, ry = self.proj(slab, rsl, self.OTa, [R(f"OTa_{i}") for i in range(c0 // 128, (c0 + n) // 128)], c0, n)
                sl = slice(c0 - p0, c0 - p0 + n)
                zs = Z[:, j, sl]
                self.stt(zs, xr, ALPHA, y, ALU.mult, ALU.add, [R(f"XR{xb}"), ry], [R(f"Z{j}_{sl.start}")])
                sqt = SQT[kb][:, :n]
                self.act(sqt, zs, AF.Square, [R(f"Z{j}_{sl.start}")], [R(f"SQT{kb}")])
                a1 = ACC[pi][:, sl]
                a2 = ACC2[pi][:, sl]
                if j == 0:
                    self.copy("dve", a1, zs, [R(f"Z{j}_{sl.start}")], [R(f"ACC{pi}")])
                    self.copy("dve", a2, sqt, [R(f"SQT{kb}")], [R(f"ACC2{pi}")])
                else:
                    self.tt("dve", a1, a1, zs, ALU.add, [R(f"Z{j}_{sl.start}"), R(f"ACC{pi}")], [R(f"ACC{pi}")])
                    self.tt("dve", a2, a2, sqt, ALU.add, [R(f"SQT{kb}"), R(f"ACC2{pi}")], [R(f"ACC2{pi}")])

        def stats(pi):
            cgs = passes[pi]
            p0 = cgs[0][0]
            for (c0, n) in cgs:
                sl = slice(c0 - p0, c0 - p0 + n)
                tot = self.ps32[:, 5 * 512:5 * 512 + n]
                tot2 = self.ps32[:, 6 * 512:6 * 512 + n]
                self.mm([(tot, ones32, ACC[pi][:, sl], True, True)], [R(f"ACC{pi}"), R("C32")], [R("bank5")])
                self.mm([(tot2, ones32, ACC2[pi][:, sl], True, True)], [R(f"ACC2{pi}"), R("C32")], [R("bank6")])
                self.act(MEAN[pi][:, sl], tot, AF.Copy, [R("bank5")], [R("MEAN")], scale=1.0 / D)
                self.tt("dve", M2[:, :n], MEAN[pi][:, sl], MEAN[pi][:, sl], ALU.mult, [R("MEAN")], [R("M2")])
                self.stt(RSTD[pi][:, sl], tot2, 1.0 / D, M2[:, :n], ALU.mult, ALU.subtract, [R("bank6"), R("M2")], [R(f"RSTD{pi}")])
                self.act(RSTD[pi][:, sl], RSTD[pi][:, sl], AF.Ln, [R(f"RSTD{pi}")], [R(f"RSTD{pi}")], bias=LN_EPS)
                self.act(RSTD[pi][:, sl], RSTD[pi][:, sl], AF.Exp, [R(f"RSTD{pi}")], [R(f"RSTD{pi}")], scale=-0.5)
                self.stt(NMR[pi][:, sl], MEAN[pi][:, sl], -1.0, RSTD[pi][:, sl], ALU.mult, ALU.mult, [R("MEAN"), R(f"RSTD{pi}")], [R(f"NMR{pi}")])

        def normalize(pi, j):
            cgs = passes[pi]
            p0 = cgs[0][0]
            gj = self.PAR[:, P_LNG + layer * 16 + j:P_LNG + layer * 16 + j + 1]
            bj = self.PAR[:, P_LNB + layer * 16 + j:P_LNB + layer * 16 + j + 1]
            for (c0, n) in cgs:
                sl = slice(c0 - p0, c0 - p0 + n)
                kb = cnt["ko"] % 4
                cnt["ko"] += 1
                zs = Z[:, j, sl]
                self.tt("dve", zs, zs, RSTD[pi][:, sl], ALU.mult, [R(f"Z{j}_{sl.start}"), R(f"RSTD{pi}")], [R(f"Z{j}_{sl.start}")])
                self.tt("dve", zs, zs, NMR[pi][:, sl], ALU.add, [R(f"Z{j}_{sl.start}"), R(f"NMR{pi}")], [R(f"Z{j}_{sl.start}")])
                xo = XO[kb][:, :n]
                self.act(xo, zs, AF.Identity, [R(f"Z{j}_{sl.start}"), R("PAR")], [R(f"XO{kb}")], bias=bj, scale=gj)
                if layer == 0:
                    xw = sorted({min(c0 // 512, 2), min((c0 + n - 1) // 512, 2)})
                    self.act(self.XTm[:, j, c0:c0 + n], zs, AF.Identity, [R(f"Z{j}_{sl.start}"), R("PAR")], [R(f"XTm_{i}") for i in xw], bias=bj, scale=gj)
                    self.dma("act", self.x1T[:, j, c0:c0 + n], xo, [R(f"XO{kb}")], [R(f"x1T_{j}_{c0}")])
                    if DEBUG:
                        self.dma("act", self.dbg_x1[:, j, c0:c0 + n], xo, [R(f"XO{kb}")], [])
                else:
                    self.dma("act", self.yT[:, j, c0 - 128:c0 - 128 + n], xo, [R(f"XO{kb}")], [])

        for j in range(NCH):
            accumulate(0, j)
        stats(0)
        for j in range(NCH):
            normalize(0, j)
            accumulate(1, j)
        stats(1)
        self.slab_issue_upto(self.sl_used + 8)
        for j in range(NCH):
            normalize(1, j)

    def swa(self):
        R = self.R
        self.carve_reset()
        KcT = self.c16(16 * 4 * 128)
        Vc = self.c16(16 * 256)
        Vtok = self.c16(10 * 256)
        KTd = self.c16(4 * NM)
        QT = self.c16(4 * NMP)
        GT = self.c16(4 * NMP)
        PT = [self.c16(512) for _ in range(8)]
        att = {"st_i": 0, "on_i": 0}
        ON = [self.c16(512) for _ in range(2)]
        D2 = [self.c32(512) for _ in range(2)]
        EG = [self.c32(512) for _ in range(2)]
        STG = [self.c32(128) for _ in range(2)]
        KcTv = KcT.rearrange("p (q k s) -> p q k s", q=16, k=4)
        Vcv = Vc.rearrange("p (q d) -> p q d", q=16)
        Vtv = Vtok.rearrange("p (t d) -> p t d", t=10)
        KTdv = KTd.rearrange("p (k n) -> p k n", k=4)
        QTv = QT.rearrange("p (c n) -> p c n", c=4)
        GTv = GT.rearrange("p (c n) -> p c n", c=4)
        C16 = self.C16
        ones16 = C16[:, C_ONE:C_ONE + 128]
        self.n_ring = 3
        ST_BANKS = [(self.ps32[:, 3 * 512:4 * 512], R("bank3")), (self.ps32[:, 4 * 512:5 * 512], R("bank4")), (self.ps16[:, :].bitcast(F32), R("bank7"))]
        self.bank_pos = 0
        XT = self.XTm
        xall = [R("XTm_0"), R("XTm_1"), R("XTm_2")]

        import os
        skip = os.environ.get("KSKIP", "")
        for kk in range(4 if "cache" not in skip else 0):
            src = self.ckT[:, kk].rearrange("q d s -> d q s")
            self.dma("pool", KcTv[0:64, :, kk, :], src, [], [R("KcT")])
            self.dma("pool", KcTv[64:128, :, kk, :], src, [], [R("KcT")])
        if "cache" not in skip:
            self.dma("pool", Vcv, self.cv.rearrange("q s d -> s q d"), [], [R("Vc")])

        if self.stop_after == "swa0":
            return
        stg_i = 0
        for si in range(4):
            slab, rsl = self.next_slab(116 + si)
            isv = si >= 2
            col = (si % 2) * 128
            tiles = range(10) if isv else (8, 9)
            for t in tiles:
                b = self.next_bank()
                out = self.ps32[:, b * 512:b * 512 + 128]
                mms = [(out, XT[:, c, t * 128:(t + 1) * 128], slab[:, c, :], c == 0, c == NCH - 1) for c in range(NCH)]
                self.mm(mms, [rsl] + xall, [R(f"bank{b}")])
                if isv:
                    self.copy("act", Vtv[:, t, col:col + 128], out, [R(f"bank{b}")], [R("Vtok")])
                if t >= 8:
                    sb = stg_i % 2
                    stg_i += 1
                    self.copy("dve", STG[sb][:, :], out, [R(f"bank{b}")], [R(f"STG{sb}")])
                    if t == 8:
                        dst = (self.vp if isv else self.kp)[:, col:col + 128]
                        self.dma("sp", dst, STG[sb][:, :], [R(f"STG{sb}")], [])
                    else:
                        dd = self.vs if isv else self.ks
                        for q in range(16 if "small" not in skip else 0):
                            self.dma("sp", dd[q, 120:128, col:col + 128], STG[sb][q * 8:(q + 1) * 8, :], [R(f"STG{sb}"), R("ks_copy"), R("vs_copy")], [])
        if self.stop_after == "swa1a":
            return
        for kvh in range(4):
            slab, rsl = self.next_slab(112 + kvh)
            for (c0, n) in [(0, 512), (512, 512), (1024, 256)]:
                o, ro = self.proj(slab, rsl, XT, xall, c0, n)
                self.copy("act", KTdv[:, kvh, c0:c0 + n], o, [ro], [R("KTd")])
        if self.stop_after == "swa1b":
            return
        st_i = 0
        on_i = 0
        for kvh in range(4):
            for cc in range(4):
                slab, rsl = self.next_slab(80 + 4 * kvh + cc)
                for (c0, n) in [(128, 512), (640, 512), (1152, 128)]:
                    o, ro = self.proj(slab, rsl, XT, xall, c0, n)
                    self.copy("act", QTv[:, cc, c0 - 128:c0 - 128 + n], o, [ro], [R("QTs")])
            for cc in range(4):
                slab, rsl = self.next_slab(96 + 4 * kvh + cc)
                for gi_, (c0, n) in enumerate([(128, 512), (640, 512), (1152, 128)]):
                    o, ro = self.proj(slab, rsl, XT, xall, c0, n)
                    eg = EG[gi_ % 2][:, :n]
                    rE = R(f"EG{gi_ % 2}")
                    self.act(eg, o, AF.Exp, [ro], [rE], scale=-1.0)
                    self.act(eg, eg, AF.Ln, [rE], [rE], bias=1.0)
                    self.act(eg, eg, AF.Exp, [rE], [rE], scale=-1.0)
                    self.tt("dve", GTv[:, cc, c0 - 128:c0 - 128 + n], o, eg, ALU.mult, [ro, rE], [R("GTs")])
            Ob = self.ps32[:, 5 * 512:6 * 512]
            Db = self.ps32[:, 6 * 512:7 * 512]
            Obv = Ob.rearrange("p (c n) -> p c n", c=4)
            Dbv = Db.rearrange("p (c n) -> p c n", c=4)
            units = [(jt, par) for jt in list(range(1, 9)) + [9] for par in range(2)]

            def S1(jt):
                sample = jt == 9
                qc = (jt - 1) * 128
                st = Item2()
                st.pts = {0: [], 1: []}
                st.ptc = {}
                kts = [jt] if sample else [jt - 1, jt]
                for kt in kts:
                    slots = []
                    mms_all = []
                    for par in range(2):
                        hp = slice(par * 64, par * 64 + 64)
                        i_ = att["st_i"]
                        att["st_i"] += 1
                        pt = PT[i_ % 8]
                        rpt = R(f"PT{i_ % 8}")
                        ST, rST = ST_BANKS[i_ % 3]
                        if sample:
                            rhs = QTv[hp, :, qc:qc + 128].rearrange("p c (q t) -> p q c t", t=8)
                        else:
                            rhs = QTv[hp, :, qc:qc + 128]
                        self.mm([(ST, KTdv[hp, kvh, kt * 128:(kt + 1) * 128], rhs, True, True)],
                                [R("KTd"), R("QTs")], [rST])
                        slots.append((pt, rpt, ST, rST, par))
                    for (pt, rpt, ST, rST, par) in slots:
                        self.act(pt[:, :], ST, AF.Exp, [rST], [rpt], bias=-SHIFT, scale=0.125)
                        if sample:
                            mk = C16[:, C_MS:C_MS + 128].rearrange("p (q t) -> p q t", t=8).unsqueeze(2).broadcast_to([128, 16, 4, 8])
                            ptv = pt.rearrange("p (q c t) -> p q c t", q=16, c=4)
                        else:
                            if kt == jt:
                                mcol = C_MC
                            else:
                                mcol = C_MP1 if jt == 1 else C_MP
                            mk = C16[:, mcol:mcol + 128].unsqueeze(1).broadcast_to([128, 4, 128])
                            ptv = pt.rearrange("p (c n) -> p c n", c=4)
                        self.tt("dve", ptv, ptv, mk, ALU.mult, [rpt, R("C16")], [rpt])
                        st.pts[par].append((pt, rpt, kt))
                if sample:
                    for par in range(2):
                        hp = slice(par * 64, par * 64 + 64)
                        i_ = att["st_i"]
                        att["st_i"] += 1
                        ptc = PT[i_ % 8]
                        rptc = R(f"PT{i_ % 8}")
                        STcb, rSTc = ST_BANKS[i_ % 3]
                        mms = [(STcb[:, q * 32:(q + 1) * 32], KcTv[hp, q, kvh, :], QTv[hp, :, qc + q * 8:qc + (q + 1) * 8], True, True) for q in range(16)]
                        self.mm(mms, [R("KcT"), R("QTs")], [rSTc])
                        self.act(ptc[:, :], STcb, AF.Exp, [rSTc], [rptc], bias=-SHIFT, scale=0.125)
                        ptcv4 = ptc.rearrange("p (q c t) -> p q c t", q=16, c=4)
                        mk = C16[:, C_MCA:C_MCA + 8].unsqueeze(1).unsqueeze(1).broadcast_to([128, 16, 4, 8])
                        self.tt("dve", ptcv4, ptcv4, mk, ALU.mult, [rptc, R("C16")], [rptc])
                        st.ptc[par] = (ptc, rptc)
                return st

            def S2(jt, st):
                sample = jt == 9
                mo = []
                md = []
                rr = [R("Vtok"), R("C16")]
                nk = len(st.pts[0])
                for i in range(nk):
                    for par in range(2):
                        hp = slice(par * 64, par * 64 + 64)
                        pt, rpt, kt = st.pts[par][i]
                        last = (i == nk - 1) and not sample
                        mo.append((Ob[hp, :], Vtv[:, kt, kvh * 64:(kvh + 1) * 64], pt[:, :], i == 0, last))
                        md.append((Db[hp, :], ones16[:, 0:64], pt[:, :], i == 0, last))
                        rr.append(rpt)
                if sample:
                    for q in range(16):
                        for par in range(2):
                            hp = slice(par * 64, par * 64 + 64)
                            ptc, rptc = st.ptc[par]
                            mo.append((Ob[hp, q * 32:(q + 1) * 32], Vcv[:, q, kvh * 64:(kvh + 1) * 64], ptc[:, q * 32:(q + 1) * 32], False, q == 15))
                            md.append((Db[hp, q * 32:(q + 1) * 32], ones16[:, 0:64], ptc[:, q * 32:(q + 1) * 32], False, q == 15))
                    rr += [st.ptc[0][1], st.ptc[1][1], R("Vc")]
                self.mm(mo, rr, [R("bank5")])
                self.mm(md, rr, [R("bank6")])

            def S3(jt):
                sample = jt == 9
                qc = (jt - 1) * 128
                ob_ = att["on_i"] % 2
                att["on_i"] += 1
                d2 = D2[ob_][:, :]
                es = self.ESINK[:, kvh * 4:(kvh + 1) * 4]
                if sample:
                    esb = es.unsqueeze(1).unsqueeze(3).broadcast_to([128, 16, 4, 8])
                    d2v = d2.rearrange("p (q c t) -> p q c t", q=16, c=4)
                    dbv = Db.rearrange("p (q c t) -> p q c t", q=16, c=4)
                else:
                    esb = es.unsqueeze(2).broadcast_to([128, 4, 128])
                    d2v = d2.rearrange("p (c n) -> p c n", c=4)
                    dbv = Dbv
                self.tt("dve", d2v, dbv, esb, ALU.add, [R("bank6"), R("ESINK")], [R(f"D2{ob_}")])
                self.act(d2, d2, AF.Ln, [R(f"D2{ob_}")], [R(f"D2{ob_}")])
                self.act(d2, d2, AF.Exp, [R(f"D2{ob_}")], [R(f"D2{ob_}")], scale=-1.0)
                self.tt("dve", ON[ob_][:, :], Ob, d2, ALU.mult, [R("bank5"), R(f"D2{ob_}")], [R(f"ON{ob_}")])
                mcol0 = 1152 if sample else jt * 128
                if sample:
                    o_out = self.OTa[:, kvh * 4:(kvh + 1) * 4, mcol0:mcol0 + 128].rearrange("p c (q t) -> p q c t", t=8)
                    o_in0 = ON[ob_].rearrange("p (q c t) -> p q c t", q=16, c=4)
                    o_in1 = GTv[:, :, qc:qc + 128].rearrange("p c (q t) -> p q c t", t=8)
                else:
                    o_out = self.OTa[:, kvh * 4:(kvh + 1) * 4, mcol0:mcol0 + 128]
                    o_in0 = ON[ob_].rearrange("p (c n) -> p c n", c=4)
                    o_in1 = GTv[:, :, qc:qc + 128]
                self.tt("pool", o_out, o_in0, o_in1, ALU.mult, [R(f"ON{ob_}"), R("GTs")], [R(f"OTa_{mcol0 // 128}")])

            tiles_ = list(range(1, 9)) + [9]
            cur = S1(tiles_[0])
            for ti_, jt in enumerate(tiles_):
                nxt_st = S1(tiles_[ti_ + 1]) if ti_ + 1 < len(tiles_) else None
                S2(jt, cur)
                S3(jt)
                cur = nxt_st


class Item2:
    pass


def build_nc(stop_after=None):
    nc = bass.Bass("TRN2", target_bir_lowering=False)
    b = Builder(nc, stop_after=stop_after)
    b.build()
    return nc


def _fm(a):
    cols = a.shape[0]
    return np.ascontiguousarray(a.reshape(cols, NCH, 128).transpose(2, 1, 0))


def _slab(w):
    return np.ascontiguousarray(w.reshape(NCH, 128, 128).transpose(1, 0, 2).reshape(128, 2048))


def _consts(half):
    c = np.zeros((128, NCONST), np.float32)
    s = np.arange(128)[:, None]
    t = np.arange(128)[None, :]
    c[:, C_ID:C_ID + 128] = (s == t)
    c[:, C_MC:C_MC + 128] = (s <= t)
    c[:, C_MP:C_MP + 128] = (s > t)
    mp1 = (s > t)
    if half == 0:
        mp1 = mp1 & (s >= 112)
    c[:, C_MP1:C_MP1 + 128] = mp1
    c[:, C_MS:C_MS + 128] = (s // 8 == t // 8) & (s % 8 <= t % 8)
    rp = np.ones(512, np.float32)
    rp[::128] = 0
    c[:, C_RP:C_RP + 512] = rp[None, :]
    rm2 = np.ones(256, np.float32)
    rm2[0] = 0
    rm2[128::8] = 0
    c[:, C_RM2:C_RM2 + 256] = rm2[None, :]
    c[:, C_SEL:C_SEL + 16] = (np.arange(128)[:, None] // 8 == np.arange(16)[None, :])
    c[:, C_MCA:C_MCA + 8] = (np.arange(128)[:, None] >= np.arange(8)[None, :] + 1)
    c[:, C_ONE:C_ONE + 128] = 1.0
    return c


def prepare_inputs(x_prompt, x_sample, state_hgrn, cache_swa_k, cache_swa_v, meta_tokens,
                   hgrn_w_in, hgrn_lb_logits, hgrn_norm_w, hgrn_w_out,
                   swa_w_in, swa_sinks, swa_w_out, ln_g, ln_b):
    f32 = np.float32
    x_prompt = np.asarray(x_prompt, f32)
    x_sample = np.asarray(x_sample, f32)
    wall = np.empty((NSLAB, 128, 2048), f32)
    hw = np.asarray(hgrn_w_in, f32)[0]
    for j in range(64):
        wall[j] = _slab(hw[:, j * 128:(j + 1) * 128])
    ho = np.asarray(hgrn_w_out, f32)[0]
    for j in range(16):
        wall[64 + j] = _slab(ho[:, j * 128:(j + 1) * 128])
    sw = np.asarray(swa_w_in, f32)[0]
    for j in range(16):
        wall[80 + j] = _slab(sw[:, j * 128:(j + 1) * 128])
        wall[96 + j] = _slab(sw[:, 2560 + j * 128:2560 + (j + 1) * 128])
    for kvh in range(4):
        wk = sw[:, 2048 + kvh * 64:2048 + (kvh + 1) * 64]
        wall[112 + kvh] = _slab(np.concatenate([wk, wk], axis=1))
    for i in range(4):
        wall[116 + i] = _slab(sw[:, 2048 + i * 128:2048 + (i + 1) * 128])
    so = np.asarray(swa_w_out, f32)[0]
    for j in range(16):
        wall[120 + j] = _slab(so[:, j * 128:(j + 1) * 128])
    pars = np.zeros((128, NPAR), f32)
    lbl = np.asarray(hgrn_lb_logits, f32)
    for i in range(3):
        pars[:, P_LBL + 16 * i:P_LBL + 16 * i + 16] = lbl[i].reshape(16, 128).T
    pars[:, P_NW:P_NW + 16] = np.asarray(hgrn_norm_w, f32)[0].T
    for l in range(2):
        pars[:, P_LNG + 16 * l:P_LNG + 16 * l + 16] = np.asarray(ln_g, f32)[l].reshape(16, 128).T
        pars[:, P_LNB + 16 * l:P_LNB + 16 * l + 16] = np.asarray(ln_b, f32)[l].reshape(16, 128).T
    sk = np.asarray(swa_sinks, f32)[0]
    pars[0:64, P_SINK:P_SINK + 16] = sk[0::2][None, :]
    pars[64:128, P_SINK:P_SINK + 16] = sk[1::2][None, :]
    meta = np.asarray(meta_tokens, f32)
    st = np.asarray(state_hgrn, f32)[0]
    ck = np.asarray(cache_swa_k, f32)[0].reshape(128, 128, 256)
    cv = np.asarray(cache_swa_v, f32)[0].reshape(128, 128, 256)
    consts = [_consts(0), _consts(1)]
    in_maps = []
    for core in range(8):
        seq, half = core // 2, core % 2
        cols = np.zeros((NX, D), f32)
        if half == 0:
            cols[NPRE + 112:NPRE + 128] = meta
            cols[NPRE + 128:] = x_prompt[seq, 0:1024]
        else:
            cols[112:128] = meta
            cols[128:] = x_prompt[seq]
        sl = slice(16 * core, 16 * core + 16)
        ckc = ck[sl]
        in_maps.append({
            "xT": _fm(cols),
            "xsT": _fm(x_sample[sl].reshape(128, D)),
            "s0": np.ascontiguousarray(st[sl]),
            "ckT": np.ascontiguousarray(ckc.reshape(16, 128, 4, 64).transpose(0, 2, 3, 1)),
            "cv": np.ascontiguousarray(cv[sl]),
            "ck": np.ascontiguousarray(ckc),
            "wall": wall,
            "consts": consts[half],
            "pars": pars,
        })
    return in_maps


_NC_CACHE = {}


def kernel(x_prompt, x_sample, state_hgrn, cache_swa_k, cache_swa_v, meta_tokens,
           hgrn_w_in, hgrn_lb_logits, hgrn_norm_w, hgrn_w_out,
           swa_w_in, swa_sinks, swa_w_out, ln_g, ln_b):
    in_maps = prepare_inputs(x_prompt, x_sample, state_hgrn, cache_swa_k, cache_swa_v, meta_tokens,
                             hgrn_w_in, hgrn_lb_logits, hgrn_norm_w, hgrn_w_out,
                             swa_w_in, swa_sinks, swa_w_out, ln_g, ln_b)
    nc = build_nc()
    res = run_bass_kernel_spmd(nc, in_maps, core_ids=list(range(8)))
    rs = res.results
    f32 = np.float32
    y_prompt = np.empty((4, 2048, D), f32)
    y_sample = np.empty((128, 8, D), f32)
    st_p = np.empty((1, 4, 16, 128, 128), f32)
    st_s = np.empty((1, 128, 16, 128, 128), f32)
    kp = np.empty((1, 4, 128, 4, 64), f32)
    vp = np.empty((1, 4, 128, 4, 64), f32)
    ks = np.empty((1, 128, 128, 4, 64), f32)
    vs = np.empty((1, 128, 128, 4, 64), f32)
    for core in range(8):
        r = rs[core]
        seq, half = core // 2, core % 2
        yT = np.asarray(r["yT"])
        ytok = yT.transpose(2, 1, 0).reshape(NMP, D)
        y_prompt[seq, half * 1024:(half + 1) * 1024] = ytok[0:1024]
        y_sample[16 * core:16 * core + 16] = ytok[1024:1152].reshape(16, 8, D)
        st_s[0, 16 * core:16 * core + 16] = np.asarray(r["ss_out"])
        ks[0, 16 * core:16 * core + 16] = np.asarray(r["ks"]).reshape(16, 128, 4, 64)
        vs[0, 16 * core:16 * core + 16] = np.asarray(r["vs"]).reshape(16, 128, 4, 64)
        if half == 1:
            st_p[0, seq] = np.asarray(r["sp_out"])
            kp[0, seq] = np.asarray(r["kp"]).reshape(128, 4, 64)
            vp[0, seq] = np.asarray(r["vp"]).reshape(128, 4, 64)
    return (y_prompt, y_sample, st_p, st_s, kp, vp, ks, vs)
```

```python
import contextlib
import numpy as np
import concourse.bass as bass
import concourse.mybir as mybir
from concourse.bass_utils import run_bass_kernel_spmd

F32 = mybir.dt.float32
BF16 = mybir.dt.bfloat16
AF = mybir.ActivationFunctionType
ALU = mybir.AluOpType

ENGS = ("pe", "act", "dve", "pool", "sp")

D = 2048
NCH = 16
NPRE = 1024
NMP = 1152
NM = 1280
NX = NPRE + NMP
ALPHA = (2.0 * 2) ** 0.25
LN_EPS = 1e-5
RMS_EPS = 1e-6
SHIFT = 30.0
NSLAB = 136
DEBUG = False

C_ID, C_MC, C_MP, C_MP1, C_MS, C_RP, C_RM2, C_SEL, C_MCA, C_ONE = 0, 128, 256, 384, 512, 640, 1152, 1408, 1424, 1432
NCONST = 1560
P_LBL, P_NW, P_LNG, P_LNB, P_SINK = 0, 48, 64, 96, 128
NPAR = 144


class Res:
    __slots__ = ("name", "writer", "readers")

    def __init__(self, name):
        self.name = name
        self.writer = None
        self.readers = {}


class Prog:
    def __init__(self, nc, n_dma_sems=(16, 8, 12)):
        self.nc = nc
        self.ops = {e: [] for e in ENGS}
        self.count = {e: 0 for e in ENGS}
        self.waited = {e: {} for e in ENGS}
        self.dma_ring = {"sp": [("dsp", i) for i in range(n_dma_sems[0])],
                         "act": [("dact", i) for i in range(n_dma_sems[1])],
                         "pool": [("dpool", i) for i in range(n_dma_sems[2])]}
        self.dma_pos = {"sp": 0, "act": 0, "pool": 0}
        self.dma_total = {}
        for q in self.dma_ring:
            for k in self.dma_ring[q]:
                self.dma_total[k] = 0

    def _collect(self, eng, reads, writes):
        need = {}

        def add(tok):
            if tok is None:
                return
            k, v = tok
            if need.get(k, 0) < v:
                need[k] = v
        for r in reads:
            add(r.writer)
        for w in writes:
            add(w.writer)
            for k, v in w.readers.items():
                add((k, v))
        out = []
        wd = self.waited[eng]
        for k, v in need.items():
            if eng == "pe" and k == "pe":
                continue
            if wd.get(k, 0) >= v:
                continue
            wd[k] = v
            out.append((k, v))
        return out

    def _mark(self, tok, reads, writes):
        k, v = tok
        for r in reads:
            if r.readers.get(k, 0) < v:
                r.readers[k] = v
        for w in writes:
            w.writer = tok
            w.readers = {}

    def op(self, eng, fn, reads=(), writes=()):
        bk = [r for r in reads if r.name.startswith("bank")]
        if bk:
            reads = [r for r in reads if not r.name.startswith("bank")]
            writes = list(writes) + [b for b in bk if b not in writes]
        waits = self._collect(eng, reads, writes)
        self.count[eng] += 1
        tok = (eng, self.count[eng])
        self._mark(tok, reads, writes)
        self.ops[eng].append((waits, fn, (eng, 1)))
        return tok

    def dma(self, q, fn, reads=(), writes=()):
        ring = self.dma_ring[q]
        key = ring[self.dma_pos[q] % len(ring)]
        self.dma_pos[q] += 1
        waits = self._collect(q, reads, writes)
        prev = self.dma_total[key]
        if prev > 0 and self.waited[q].get(key, 0) < prev:
            self.waited[q][key] = prev
            waits.append((key, prev))
        self.dma_total[key] = prev + 16
        tok = (key, prev + 16)
        self._mark(tok, reads, writes)
        self.ops[q].append((waits, fn, (key, 16)))
        return tok

    def barrier(self):
        toks = [(e, self.count[e]) for e in ("pe", "act", "dve", "pool") if self.count[e] > 0]
        toks += [(k, v) for k, v in self.dma_total.items() if v > 0]
        for e in ENGS:
            waits = []
            for k, v in toks:
                if self.waited[e].get(k, 0) >= v:
                    continue
                self.waited[e][k] = v
                waits.append((k, v))
            if waits:
                self.ops[e].append((waits, None, None))

    def emit(self):
        nc = self.nc
        keys = ["pe", "act", "dve", "pool"] + list(self.dma_total)
        with contextlib.ExitStack() as st:
            sems = {}
            for k in keys:
                nm = k if isinstance(k, str) else f"{k[0]}{k[1]}"
                sems[k] = st.enter_context(nc.semaphore("s_" + nm))
            block = st.enter_context(nc.Block())

            def run(engname):
                def body(e):
                    for waits, fn, inc in self.ops[engname]:
                        for k, v in waits:
                            e.wait_ge(sems[k], v)
                        if fn is None:
                            continue
                        ins = fn(e)
                        ins.then_inc(sems[inc[0]], inc[1])
                return body
            block.sync(run("sp"))
            block.tensor(run("pe"))
            block.scalar(run("act"))
            block.vector(run("dve"))
            block.gpsimd(run("pool"))


class Builder:
    def __init__(self, nc, stop_after=None):
        self.nc = nc
        self.P = Prog(nc)
        self.res = {}
        self.stop_after = stop_after

    def R(self, name):
        r = self.res.get(name)
        if r is None:
            r = self.res[name] = Res(name)
        return r

    def act(self, out, in_, func, reads, writes, bias=None, scale=None):
        kw = {}
        if bias is not None:
            kw["bias"] = bias
        if scale is not None:
            kw["scale"] = scale
        self.P.op("act", lambda e: e.activation(out=out, in_=in_, func=func, **kw), reads, writes)

    def tt(self, eng, out, in0, in1, op, reads, writes):
        self.P.op(eng, lambda e: e.tensor_tensor(out=out, in0=in0, in1=in1, op=op), reads, writes)

    def ts(self, eng, out, in0, s1, op0, reads, writes, s2=None, op1=None):
        if op1 is None:
            self.P.op(eng, lambda e: e.tensor_scalar(out=out, in0=in0, scalar1=s1, scalar2=None, op0=op0), reads, writes)
        else:
            self.P.op(eng, lambda e: e.tensor_scalar(out=out, in0=in0, scalar1=s1, scalar2=s2, op0=op0, op1=op1), reads, writes)

    def stt(self, out, in0, scalar, in1, op0, op1, reads, writes):
        self.P.op("dve", lambda e: e.scalar_tensor_tensor(out=out, in0=in0, scalar=scalar, in1=in1, op0=op0, op1=op1), reads, writes)

    def copy(self, eng, out, in_, reads, writes):
        if eng == "act":
            self.P.op("act", lambda e: e.activation(out=out, in_=in_, func=AF.Copy), reads, writes)
        else:
            self.P.op(eng, lambda e: e.tensor_copy(out=out, in_=in_), reads, writes)

    def memset(self, eng, ap, val, writes):
        self.P.op(eng, lambda e: e.memset(ap, val), (), writes)

    def mm(self, mms, reads, writes):
        def fn(e):
            ins = None
            for (o, l, r, s0, s1) in mms:
                ins = e.matmul(o, lhsT=l, rhs=r, start=s0, stop=s1)
            return ins
        self.P.op("pe", fn, reads, writes)

    def tr(self, trs, reads, writes):
        def fn(e):
            ins = None
            for (o, i, ident) in trs:
                ins = e.transpose(out=o, in_=i, identity=ident)
            return ins
        self.P.op("pe", fn, reads, writes)

    def dma(self, q, out, in_, reads, writes):
        self.P.dma(q, lambda e: e.dma_start(out=out, in_=in_), reads, writes)

    def slab_setup(self, schedule):
        self.sched = schedule
        self.sl_issued = 0
        self.sl_used = 0

    def slab_issue_upto(self, n):
        n = min(n, len(self.sched))
        while self.sl_issued < n:
            i = self.sl_issued
            slot = i % 8
            sid = self.sched[i]
            dst = self.RING[:, slot * 2048:(slot + 1) * 2048]
            self.dma("pool", dst, self.wall[sid], [], [self.R(f"slot{slot}")])
            self.sl_issued += 1

    def next_slab(self, sid):
        i = self.sl_used
        assert self.sched[i] == sid, (i, self.sched[i], sid)
        if getattr(self, "auto_prefetch", True):
            self.slab_issue_upto(i + 5)
        self.sl_used += 1
        slot = i % 8
        ap = self.RING[:, slot * 2048:(slot + 1) * 2048].rearrange("p (c e) -> p c e", c=NCH)
        return ap, self.R(f"slot{slot}")

    def next_bank(self):
        b = self.bank_pos % self.n_ring
        self.bank_pos += 1
        return b

    def proj(self, slab, slabres, X, xres, c0, n):
        b = self.next_bank()
        out = self.ps32[:, b * 512:b * 512 + n]
        mms = [(out, slab[:, c, :], X[:, c, c0:c0 + n], c == 0, c == NCH - 1) for c in range(NCH)]
        self.mm(mms, [slabres] + list(xres), [self.R(f"bank{b}")])
        return out, self.R(f"bank{b}")

    def build(self):
        nc = self.nc
        dt_in = lambda name, shape: nc.dram_tensor(name, shape, F32, kind="ExternalInput").ap()
        dt_out = lambda name, shape: nc.dram_tensor(name, shape, F32, kind="ExternalOutput").ap()
        self.xT = dt_in("xT", [128, NCH, NX])
        self.xsT = dt_in("xsT", [128, NCH, 128])
        self.s0 = dt_in("s0", [16, 16, 128, 128])
        self.ckT = dt_in("ckT", [16, 4, 64, 128])
        self.cv = dt_in("cv", [16, 128, 256])
        self.ck = dt_in("ck", [16, 128, 256])
        self.wall = dt_in("wall", [NSLAB, 128, 2048])
        self.consts = dt_in("consts", [128, NCONST])
        self.pars = dt_in("pars", [128, NPAR])
        self.yT = dt_out("yT", [128, NCH, NMP])
        self.sp_out = dt_out("sp_out", [16, 128, 128])
        self.ss_out = dt_out("ss_out", [16, 16, 128, 128])
        self.kp = dt_out("kp", [128, 256])
        self.vp = dt_out("vp", [128, 256])
        self.ks = dt_out("ks", [16, 128, 256])
        self.vs = dt_out("vs", [16, 128, 256])
        self.x1T = nc.dram_tensor("x1T", [128, NCH, NM], F32, kind="Internal").ap()
        if DEBUG:
            self.dbg_x1 = dt_out("dbg_x1", [128, NCH, NM])
            self.dbg_ot = dt_out("dbg_ot", [128, NCH, NM])

        with contextlib.ExitStack() as st:
            E = st.enter_context
            self.XTm_t = E(nc.sbuf_tensor("XTm", [128, NCH * NM], BF16))
            self.OTa_t = E(nc.sbuf_tensor("OTa", [128, NCH * NM], BF16))
            self.RING = E(nc.sbuf_tensor("RING", [128, 8 * 2048], BF16))
            self.C32 = E(nc.sbuf_tensor("C32", [128, NCONST], F32))
            self.C16 = E(nc.sbuf_tensor("C16", [128, NCONST], BF16))
            self.PAR = E(nc.sbuf_tensor("PAR", [128, 256], F32))
            self.SCR = E(nc.sbuf_tensor("SCR", [128, 21800], F32))
            self.ps32 = E(nc.psum_tensor("ps32", [128, 7 * 512], F32))
            self.ps16 = E(nc.psum_tensor("ps16", [128, 1024], BF16))
            self.XTm = self.XTm_t[:].rearrange("p (c n) -> p c n", c=NCH)
            self.OTa = self.OTa_t[:].rearrange("p (c n) -> p c n", c=NCH)
            self.program()
            self.P.barrier()
            self.P.emit()

    def carve_reset(self):
        self.cpos = 0

    def c32(self, n):
        a = self.SCR[:, self.cpos:self.cpos + n]
        self.cpos += n
        assert self.cpos <= 21800, self.cpos
        return a

    def c16(self, n):
        n32 = (n + 1) // 2
        a = self.SCR[:, self.cpos:self.cpos + n32].bitcast(BF16)
        self.cpos += n32
        assert self.cpos <= 21800, self.cpos
        return a

    def program(self):
        R = self.R
        sched = []
        for h in range(16):
            sched += [h, 16 + h, 32 + h, 48 + h]
        sched += [64 + j for j in range(16)] * 2
        sched += [116, 117, 118, 119, 112, 113, 114, 115]
        for kvh in range(4):
            sched += [80 + 4 * kvh + i for i in range(4)] + [96 + 4 * kvh + i for i in range(4)]
        sched += [120 + j for j in range(16)] * 2
        self.slab_setup(sched)

        self.dma("sp", self.C32[:], self.consts, [], [R("C32")])
        self.dma("sp", self.PAR[:, 0:NPAR], self.pars, [], [R("PAR")])
        self.copy("act", self.C16[:], self.C32[:], [R("C32")], [R("C16")])
        self.setup_params()
        self.hgrn()
        if self.stop_after == "hgrn":
            return
        self.P.barrier()
        self.wout_ln(0)
        if self.stop_after == "ln0":
            return
        self.P.barrier()
        self.swa()
        if self.stop_after in ("swa0", "swa1a", "swa1b", "swa"):
            return
        self.P.barrier()
        self.wout_ln(1)

    def setup_params(self):
        R = self.R
        PAR = self.PAR
        l0, l1, l2 = PAR[:, 0:16], PAR[:, 16:32], PAR[:, 32:48]
        mx = PAR[:, 144:160]
        ex = PAR[:, 160:208]
        sm = PAR[:, 208:224]
        self.LB = PAR[:, 224:240]
        self.C1 = PAR[:, 240:256]
        rP = [R("PAR")]
        self.tt("dve", mx, l0, l1, ALU.max, rP, [R("pmx")])
        self.tt("dve", mx, mx, l2, ALU.max, rP + [R("pmx")], [R("pmx")])
        for i in range(3):
            self.tt("dve", ex[:, 16 * i:16 * i + 16], PAR[:, 16 * i:16 * i + 16], mx, ALU.subtract, rP + [R("pmx")], [R(f"pex{i}")])
        self.act(ex, ex, AF.Exp, [R("pex0"), R("pex1"), R("pex2")], [R("pex")])
        self.tt("dve", sm, ex[:, 0:16], ex[:, 16:32], ALU.add, [R("pex")], [R("psm")])
        self.tt("dve", sm, sm, ex[:, 32:48], ALU.add, [R("pex"), R("psm")], [R("psm")])
        self.P.op("dve", lambda e: e.reciprocal(out=sm, in_=sm), [R("psm")], [R("psm")])
        self.tt("dve", self.LB, ex[:, 0:16], sm, ALU.mult, [R("pex"), R("psm")], [R("LB")])
        self.act(self.C1, self.LB, AF.Ln, [R("LB")], [R("C1")], bias=1.0, scale=-1.0)
        self.ESINK = PAR[:, P_SINK:P_SINK + 16]
        self.act(self.ESINK, self.ESINK, AF.Exp, rP, [R("ESINK")], bias=-SHIFT)

    def hgrn(self):
        R = self.R
        self.carve_reset()
        XTp_t = self.c16(NCH * NPRE)
        XTp = XTp_t.rearrange("p (c n) -> p c n", c=NCH)
        QT = [self.c16(512) for _ in range(2)]
        KT = [self.c16(512) for _ in range(2)]
        KH = [self.c16(512) for _ in range(2)]
        VT = [self.c16(512) for _ in range(2)]
        GATE = [self.c16(512) for _ in range(2)]
        VK = [self.c16(256) for _ in range(3)]
        ATm = [self.c16(128) for _ in range(3)]
        SQ = [self.c16(128) for _ in range(3)]
        VbAll = self.c16(2048)
        Vb = [VbAll[:, 512 * i:512 * (i + 1)] for i in range(4)]
        self.S0BF = VbAll
        self.VKS = self.c16(256)
        U, LA, Bb, L1, SG = [self.c32(512) for _ in range(5)]
        EQ = U
        DK = LA
        QT32 = [self.c32(512) for _ in range(2)]
        RS = [self.c32(128) for _ in range(3)]
        T1 = [self.c32(128) for _ in range(3)]
        S = [self.c32(128) for _ in range(2)]
        S0b = [self.c32(2048) for _ in range(2)]
        CB = [self.c32(8) for _ in range(2)]
        EBL = [self.c32(8) for _ in range(2)]
        CBs = [self.c32(16) for _ in range(2)]
        EBLs = [self.c32(16) for _ in range(2)]
        self.n_ring = 4
        self.bank_pos = 0
        self.auto_prefetch = False
        C32, C16 = self.C32, self.C16
        ident16 = C16[:, C_ID:C_ID + 128]
        ones16 = C16[:, C_ONE:C_ONE + 128]

        xgroups = [("pre", 0, 512), ("pre", 512, 512), ("main", 0, 512), ("main", 512, 512), ("main", 1024, 256)]
        self.dma("pool", XTp[:, :, 0:512], self.xT[:, :, 0:512], [], [R("XTp_0")])
        self.slab_issue_upto(4)
        self.dma("pool", XTp[:, :, 512:1024], self.xT[:, :, 512:1024], [], [R("XTp_1")])
        self.dma("pool", self.XTm[:, :, 0:512], self.xT[:, :, NPRE:NPRE + 512], [], [R("XTm_0")])
        self.dma("pool", self.XTm[:, :, 512:1024], self.xT[:, :, NPRE + 512:NPRE + 1024], [], [R("XTm_1")])
        self.dma("pool", self.XTm[:, :, 1024:1152], self.xT[:, :, NPRE + 1024:NPRE + 1152], [], [R("XTm_2")])
        self.dma("pool", self.XTm[:, :, 1152:1280], self.xsT, [], [R("XTm_2")])

        def load_s0(h):
            dst = S0b[h % 2].rearrange("p (s v) -> p s v", s=16)
            src = self.s0[:, h].rearrange("s k v -> k s v")
            self.dma("sp", dst, src, [], [R(f"S0b{h % 2}")])

        class Item:
            pass
        items = []
        for h in range(16):
            for gidx, (kind, c0, n) in enumerate(xgroups):
                it = Item()
                it.h, it.kind, it.c0, it.n, it.gidx = h, kind, c0, n, gidx
                it.main = kind == "main"
                it.m2 = it.main and c0 == 1024
                it.nt = n // 128
                it.gb = len(items) % 2
                items.append(it)
        heads = {}
        tstate = {"ti": 0}

        def setup_head(h):
            hd = Item()
            hd.sq, hd.rq = self.next_slab(h)
            hd.sf, hd.rf = self.next_slab(16 + h)
            hd.si, hd.ri = self.next_slab(32 + h)
            hd.sg, hd.rg = self.next_slab(48 + h)
            self.slab_issue_upto(4 * (h + 2))
            hd.Sh = S[h % 2]
            hd.rS = R(f"S{h % 2}")
            self.memset("dve", hd.Sh, 0.0, [hd.rS])
            hd.lbh = self.LB[:, h:h + 1]
            hd.c1h = self.C1[:, h:h + 1]
            hd.nwh = self.PAR[:, P_NW + h:P_NW + h + 1]
            heads[h] = hd

        rpar = [R("LB"), R("C1"), R("PAR")]

        def proj_chunks(it):
            hd = heads[it.h]
            if it.main:
                X, xres = self.XTm, [R(f"XTm_{it.c0 // 512}")]
            else:
                X, xres = XTp, [R(f"XTp_{it.c0 // 512}")]

            def mk(attr, rattr, slab, rsl):
                def f():
                    o, r = self.proj(slab, rsl, X, xres, it.c0, it.n)
                    setattr(it, attr, o)
                    setattr(it, rattr, r)
                return f
            ch = [mk("pf", "rpf", hd.sf, hd.rf), mk("pi", "rpi", hd.si, hd.ri)]
            if it.main:
                ch += [mk("pq", "rpq", hd.sq, hd.rq), mk("pg", "rpg", hd.sg, hd.rg)]
            return ch

        def proj_item(it):
            for f in proj_chunks(it):
                f()

        def gatingA(it):
            hd = heads[it.h]
            n, gb, m2, nt = it.n, it.gb, it.m2, it.nt
            u, la, bb, l1, dk = U[:, :n], LA[:, :n], Bb[:, :n], L1[:, :n], DK[:, :n]
            ops = []
            ops.append(lambda: self.act(u, it.pf, AF.Exp, [it.rpf], [R("U")]))
            ops.append(lambda: self.act(la, u, AF.Ln, [R("U")] + rpar, [R("LA")], bias=hd.lbh))
            ops.append(lambda: self.act(l1, u, AF.Ln, [R("U")], [R("L1")], bias=1.0))
            ops.append(lambda: self.tt("dve", la, la, l1, ALU.subtract, [R("LA"), R("L1")], [R("LA")]))
            rst = C32[:, C_RM2:C_RM2 + 256] if m2 else C32[:, C_RP:C_RP + n]
            ops.append(lambda: self.P.op("dve", lambda e: e.tensor_tensor_scan(out=bb, data0=rst, data1=la, initial=0.0, op0=ALU.mult, op1=ALU.add),
                                         [R("LA"), R("C32")], [R("B")]))
            it.nA1a = len(ops)
            ops.append(lambda: self.copy("act", VT[gb][:, :n], it.pi, [it.rpi], [R(f"VT{gb}")]))
            ntp = 1 if m2 else nt
            blast = Bb[:, 127:128 * ntp:128]
            it.nA1 = len(ops)
            ops.append(lambda: self.act(EBL[gb][:, 0:ntp], blast, AF.Exp, [R("B")], [R(f"EBL{gb}")]))
            ops.append(lambda: self.ts("dve", CB[gb][:, 0:ntp], blast, hd.c1h, ALU.add, [R("B")] + rpar, [R(f"CB{gb}")]))
            if m2:
                bl_s = Bb[:, 128 + 7:256:8]
                ops.append(lambda: self.act(EBLs[gb][:, :], bl_s, AF.Exp, [R("B")], [R(f"EBLs{gb}")]))
                ops.append(lambda: self.ts("dve", CBs[gb][:, :], bl_s, hd.c1h, ALU.add, [R("B")] + rpar, [R(f"CBs{gb}")]))
            ops.append(lambda: self.tt("dve", l1, bb, l1, ALU.add, [R("B"), R("L1")], [R("L1")]))
            dkv = DK[:, 0:128 * ntp].rearrange("p (t n) -> p t n", n=128)
            zv = L1[:, 0:128 * ntp].rearrange("p (t n) -> p t n", n=128)
            cbv = CB[gb][:, 0:ntp].unsqueeze(2).broadcast_to([128, ntp, 128])
            ops.append(lambda: self.tt("dve", dkv, cbv, zv, ALU.subtract, [R(f"CB{gb}"), R("L1")], [R("LA")]))
            if m2:
                dkv2 = DK[:, 128:256].rearrange("p (s n) -> p s n", n=8)
                zv2 = L1[:, 128:256].rearrange("p (s n) -> p s n", n=8)
                cbv2 = CBs[gb][:, :].unsqueeze(2).broadcast_to([128, 16, 8])
                ops.append(lambda: self.tt("dve", dkv2, cbv2, zv2, ALU.subtract, [R(f"CBs{gb}"), R("L1")], [R("LA")]))
            ops.append(lambda: self.act(KH[gb][:, :n], dk, AF.Exp, [R("LA")], [R(f"KH{gb}")]))
            return ops

        def gatingB(it):
            if not it.main:
                return []
            hd = heads[it.h]
            n, gb = it.n, it.gb
            bb, l1, eq, sgt = Bb[:, :n], L1[:, :n], EQ[:, :n], SG[:, :n]
            ops = []
            ops.append(lambda: self.act(eq, bb, AF.Exp, [R("B")], [R("U")]))
            ops.append(lambda: self.act(KT[gb][:, :n], l1, AF.Exp, [R("L1")] + rpar, [R(f"KT{gb}")], bias=hd.c1h, scale=-1.0))
            ops.append(lambda: self.tt("dve", QT32[gb][:, :n], it.pq, eq, ALU.mult, [it.rpq, R("U")], [R(f"QT32{gb}")]))
            ops.append(lambda: self.tt("dve", QT[gb][:, :n], it.pq, eq, ALU.mult, [it.rpq, R("U")], [R(f"QT{gb}")]))
            ops.append(lambda: self.act(sgt, it.pg, AF.Exp, [it.rpg], [R("SG")], scale=-1.0))
            ops.append(lambda: self.act(sgt, sgt, AF.Ln, [R("SG")], [R("SG")], bias=1.0))
            ops.append(lambda: self.act(sgt, sgt, AF.Exp, [R("SG")], [R("SG")], scale=-1.0))
            ops.append(lambda: self.stt(GATE[gb][:, :n], it.pg, hd.nwh, sgt, ALU.mult, ALU.mult, [it.rpg, R("SG"), R("PAR")], [R(f"GATE{gb}")]))
            return ops

        def mk_tile(it, t):
            td = Item()
            td.it, td.t = it, t
            td.tb = tstate["ti"] % 3
            tstate["ti"] += 1
            td.stage = 0
            return td

        def tile_views(td):
            tb = td.tb
            sm0 = (4 + tb) * 512
            v = Item()
            v.AT = self.ps32[:, sm0:sm0 + 128]
            v.Op = self.ps32[:, sm0 + 128:sm0 + 256]
            v.SSp = self.ps32[:, sm0 + 256:sm0 + 384]
            v.SPp = self.ps32[:, sm0 + 384:sm0 + 512]
            v.psb = self.ps16[:, tb * 256:(tb + 1) * 256]
            v.bk = R(f"bank{4 + tb}")
            v.cs = slice(td.t * 128, (td.t + 1) * 128)
            v.sample = td.it.m2 and td.t == 1
            return v

        def stageA(td):
            it, tb = td.it, td.tb
            gb = it.gb
            v = tile_views(td)
            self.tr([(v.psb[:, 0:128], VT[gb][:, v.cs], ident16), (v.psb[:, 128:256], KH[gb][:, v.cs], ident16)],
                    [R(f"VT{gb}"), R(f"KH{gb}"), R("C16")], [R("bank7")])
            self.copy("act", VK[tb][:, :], v.psb, [R("bank7")], [R(f"VK{tb}")])
            if it.main:
                self.mm([(v.AT, KT[gb][:, v.cs], QT[gb][:, v.cs], True, True)], [R(f"KT{gb}"), R(f"QT{gb}")], [v.bk])
                mask = C32[:, C_MS:C_MS + 128] if v.sample else C32[:, C_MC:C_MC + 128]
                self.tt("dve", ATm[tb][:, :], v.AT, mask, ALU.mult, [v.bk, R("C32")], [R(f"ATm{tb}")])

        def stageB(td):
            it, tb = td.it, td.tb
            hd = heads[it.h]
            h, gb = it.h, it.gb
            Sh, rS = hd.Sh, hd.rS
            v = tile_views(td)
            if it.main:
                if not v.sample:
                    self.mm([(v.Op, VK[tb][:, 0:128], ATm[tb][:, :], True, False),
                             (v.Op, Sh, QT32[gb][:, v.cs], False, True)],
                            [R(f"VK{tb}"), R(f"ATm{tb}"), rS, R(f"QT32{gb}")], [v.bk])
                else:
                    s0v = self.S0BF.rearrange("p (s v) -> p s v", s=16)
                    mms = [(v.Op, VK[tb][:, 0:128], ATm[tb][:, :], True, False)]
                    for sq_ in range(16):
                        mms.append((v.Op[:, sq_ * 8:(sq_ + 1) * 8], s0v[:, sq_, :], QT[gb][:, 128 + sq_ * 8:128 + (sq_ + 1) * 8], False, sq_ == 15))
                    self.mm(mms, [R(f"VK{tb}"), R(f"ATm{tb}"), R("Vb0"), R("Vb1"), R("Vb2"), R("Vb3"), R(f"QT{gb}")], [v.bk])
            if not v.sample:
                self.mm([(v.SPp, VK[tb][:, 128:256], VK[tb][:, 0:128], True, True)], [R(f"VK{tb}")], [v.bk])
                self.stt(Sh, Sh, EBL[gb][:, td.t:td.t + 1], v.SPp, ALU.mult, ALU.add, [rS, R(f"EBL{gb}"), v.bk], [rS])
                if it.m2:
                    self.dma("sp", self.sp_out[h], Sh, [rS], [])
            else:
                self.copy("dve", self.VKS[:, :], VK[tb][:, :], [R(f"VK{tb}")], [R("VKS")])
            if it.main:
                self.act(SQ[tb][:, :], v.Op, AF.Square, [v.bk], [R(f"SQ{tb}")])

        def stageC(td):
            it, tb = td.it, td.tb
            if not it.main:
                return
            gb = it.gb
            v = tile_views(td)
            mcol = it.c0 + td.t * 128
            self.mm([(v.SSp, ones16, SQ[tb][:, :], True, True)], [R(f"SQ{tb}"), R("C16")], [v.bk])
            self.act(RS[tb][:, :], v.SSp, AF.Ln, [v.bk], [R(f"RS{tb}")], bias=RMS_EPS, scale=1.0 / 128.0)
            self.act(RS[tb][:, :], RS[tb][:, :], AF.Exp, [R(f"RS{tb}")], [R(f"RS{tb}")], scale=-0.5)
            self.tt("dve", T1[tb][:, :], v.Op, RS[tb][:, :], ALU.mult, [v.bk, R(f"RS{tb}")], [R(f"T1{tb}")])
            self.tt("pool", self.OTa[:, it.h, mcol:mcol + 128], T1[tb][:, :], GATE[gb][:, v.cs], ALU.mult,
                    [R(f"T1{tb}"), R(f"GATE{gb}")], [R(f"OTa_{mcol // 128}")])

        tqueue = []

        def step(td_new):
            if td_new is not None:
                stageA(td_new)
            for td in reversed(tqueue):
                if td.stage == 1:
                    stageB(td)
                    td.stage = 2
                elif td.stage == 2:
                    stageC(td)
                    td.stage = 3
            tqueue[:] = [td for td in tqueue if td.stage < 3]
            if td_new is not None:
                td_new.stage = 1
                tqueue.append(td_new)

        def sample_state(it):
            h, gb = it.h, it.gb
            s0v = S0b[h % 2].rearrange("p (s v) -> p s v", s=16)
            VKs = self.VKS
            holder = {}

            def vb_op(q4):
                def f():
                    vbv = Vb[q4].rearrange("p (s v) -> p s v", s=4)
                    vin = VKs[:, 0:128].unsqueeze(1).broadcast_to([128, 4, 128])
                    sel = C16[:, C_SEL + 4 * q4:C_SEL + 4 * q4 + 4].unsqueeze(2).broadcast_to([128, 4, 128])
                    self.tt("dve", vbv, vin, sel, ALU.mult, [R("VKS"), R("C16")], [R(f"Vb{q4}")])
                return f

            def mm_op(q4):
                def f():
                    b = self.next_bank()
                    holder[q4] = b
                    sps = self.ps32[:, b * 512:(b + 1) * 512]
                    self.mm([(sps, VKs[:, 128:256], Vb[q4], True, True)], [R("VKS"), R(f"Vb{q4}")], [R(f"bank{b}")])
                return f

            def wb_op(q4):
                def f():
                    b = holder[q4]
                    sps = self.ps32[:, b * 512:(b + 1) * 512]
                    for s_ in range(4):
                        sq_ = q4 * 4 + s_
                        self.stt(s0v[:, sq_, :], s0v[:, sq_, :], EBLs[gb][:, sq_:sq_ + 1], sps[:, s_ * 128:(s_ + 1) * 128],
                                 ALU.mult, ALU.add, [R(f"S0b{h % 2}"), R(f"EBLs{gb}"), R(f"bank{b}")], [R(f"S0b{h % 2}")])
                return f
            dst = self.ss_out[:, h].rearrange("s k v -> k s v")
            out_op = lambda: self.dma("sp", dst, s0v, [R(f"S0b{h % 2}")], [])
            chunks = [[vb_op(0), vb_op(1), vb_op(2), vb_op(3)],
                      [mm_op(0), mm_op(1)],
                      [wb_op(0), wb_op(1), mm_op(2), mm_op(3)],
                      [wb_op(2), wb_op(3), out_op]]
            return chunks

        def run(ops):
            for o in ops:
                o()

        load_s0(0)
        setup_head(0)
        proj_item(items[0])
        run(gatingA(items[0]))
        run(gatingB(items[0]))
        deferred = []
        for i, it in enumerate(items):
            nxt = items[i + 1] if i + 1 < len(items) else None
            P, A1, A2, B = [], [], [], []
            if nxt is not None:
                if nxt.gidx == 0:
                    setup_head(nxt.h)
                P = proj_chunks(nxt)
            avail = []
            if it.m2:
                self.copy("dve", self.S0BF[:, :], S0b[it.h % 2][:, :], [R(f"S0b{it.h % 2}")], [R("Vb0"), R("Vb1"), R("Vb2"), R("Vb3")])
            nsteps = it.nt
            T = max(nsteps, len(P))
            for t in range(T):
                if t < len(P):
                    P[t]()
                if t == 0 and nxt is not None:
                    ga = gatingA(nxt)
                    run(ga[:nxt.nA1a])
                if t == 1 and nxt is not None:
                    run(ga[nxt.nA1a:nxt.nA1])
                    A2 = ga[nxt.nA1:]
                if t == 3 and nxt is not None and nxt.main:
                    B = gatingB(nxt)
                if t < nsteps:
                    step(mk_tile(it, t))
                if t == 0:
                    avail = [(lambda ch: (lambda: run(ch)))(ch) for ch in deferred] + avail
                    deferred = []
                if t == 1:
                    run(A2)
                if t == 3:
                    avail = avail + B
                remaining = max(T - t - 1, 0)
                k = -(-len(avail) // (remaining + 1)) if avail else 0
                run(avail[:k])
                avail = avail[k:]
            run(avail)
            if it.gidx == 0 and it.h + 1 < 16:
                load_s0(it.h + 1)
            if it.m2:
                deferred = sample_state(it)
        step(None)
        step(None)
        for ch in deferred:
            run(ch)
        self.slab_issue_upto(self.sl_used + 8)
        self.auto_prefetch = True
        if DEBUG:
            self.dbg_dump_bf16(self.dbg_ot, self.OTa, [R(f"OTa_{i}") for i in range(10)])

    def dbg_dump_bf16(self, dst, src3, reads):
        self.P.barrier()
        tmp = self.SCR[:, 21800 - 1280:21800]
        for c in range(NCH):
            self.copy("dve", tmp, src3[:, c, :], reads, [self.R("dbgtmp")])
            self.dma("sp", dst[:, c, :], tmp, [self.R("dbgtmp")], [self.R("dbgout")])
        self.P.barrier()

    def wout_ln(self, layer):
        R = self.R
        self.carve_reset()
        ZW = 640
        Z_t = self.c32(NCH * ZW)
        Z = Z_t.rearrange("p (c n) -> p c n", c=NCH)
        ACC = [self.c32(ZW), self.c32(ZW)]
        ACC2 = [self.c32(ZW), self.c32(ZW)]
        MEANb = self.c32(ZW)
        MEAN = [MEANb, MEANb]
        RSTD = [self.c32(ZW), self.c32(ZW)]
        NMR = [self.c32(ZW), self.c32(ZW)]
        M2 = self.c32(512)
        SQT = [self.c32(512) for _ in range(2)]
        XR = [self.c32(512) for _ in range(4)]
        XO = [self.c32(512) for _ in range(4)]
        self.n_ring = 5
        ones32 = self.C32[:, C_ONE:C_ONE + 128]
        if layer == 0:
            passes = [[(0, 512), (512, 128)], [(640, 512), (1152, 128)]]
            sbase = 64
        else:
            passes = [[(128, 512), (640, 128)], [(768, 384), (1152, 128)]]
            sbase = 120
        cnt = {"k": 0, "ko": 0}
        if layer == 0:
            self.dma("sp", self.ks[:, 0:120, :], self.ck[:, 8:128, :], [], [R("ks_copy")])
            self.dma("sp", self.vs[:, 0:120, :], self.cv[:, 8:128, :], [], [R("vs_copy")])

        steps = [(pi, j, c0, n) for pi in range(2) for j in range(NCH) for (c0, n) in passes[pi]]
        xrs = {"issued": 0}

        def issue_xr(upto):
            upto = min(upto, len(steps))
            while xrs["issued"] < upto:
                i = xrs["issued"]
                pi_, j_, c0, n = steps[i]
                xb = i % 4
                xr = XR[xb][:, :n]
                if layer == 0:
                    src = self.xsT[:, j_, :] if c0 >= NMP else self.xT[:, j_, NPRE + c0:NPRE + c0 + n]
                else:
                    src = self.x1T[:, j_, c0:c0 + n]
                if layer == 1:
                    l0_pieces = [(0, 512), (512, 128), (640, 512), (1152, 128)]
                    xdeps = [R(f"x1T_{j_}_{p0_}") for (p0_, pn_) in l0_pieces if p0_ < c0 + n and c0 < p0_ + pn_]
                else:
                    xdeps = []
                self.dma("sp", xr, src, xdeps, [R(f"XR{xb}")])
                xrs["issued"] += 1

        def accumulate(pi, j):
            cgs = passes[pi]
            p0 = cgs[0][0]
            slab, rsl = self.next_slab(sbase + j)
            for (c0, n) in cgs:
                kb = cnt["k"] % 2
                xb = cnt["k"] % 4
                assert steps[cnt["k"]] == (pi, j, c0, n)
                issue_xr(cnt["k"] + 3)
                cnt["k"] += 1
                xr = XR[xb][:, :n]
                y, ry = self.proj(slab, rsl, self.OTa, [R(f"OTa_{i}") for i in range(c0 // 128, (c0 + n) // 128)], c0, n)
                sl = slice(c0 - p0, c0 - p0 + n)
                zs = Z[:, j, sl]
                self.stt(zs, xr, ALPHA, y, ALU.mult, ALU.add, [R(f"XR{xb}"), ry], [R(f"Z{j}_{sl.start}")])
                sqt = SQT[kb][:, :n]
                self.act(sqt, zs, AF.Square, [R(f"Z{j}_{sl.start}")], [R(f"SQT{kb}")])
                a1 = ACC[pi][:, sl]
                a2 = ACC2[pi][:, sl]
                if j == 0:
                    self.copy("dve", a1, zs, [R(f"Z{j}_{sl.start}")], [R(f"ACC{pi}")])
                    self.copy("dve", a2, sqt, [R(f"SQT{kb}")], [R(f"ACC2{pi}")])
                else:
                    self.tt("dve", a1, a1, zs, ALU.add, [R(f"Z{j}_{sl.start}"), R(f"ACC{pi}")], [R(f"ACC{pi}")])
                    self.tt("dve", a2, a2, sqt, ALU.add, [R(f"SQT{kb}"), R(f"ACC2{pi}")], [R(f"ACC2{pi}")])

        def stats(pi):
            cgs = passes[pi]
            p0 = cgs[0][0]
            chains = []
            for (c0, n) in cgs:
                sl = slice(c0 - p0, c0 - p0 + n)
                tot = self.ps32[:, 5 * 512:5 * 512 + n]
                tot2 = self.ps32[:, 6 * 512:6 * 512 + n]
                k = sl.start
                rM, rR, rN = R(f"MEAN_{k}"), R(f"RSTD{pi}_{k}"), R(f"NMR{pi}_{k}")

                def mk(sl=sl, n=n, tot=tot, tot2=tot2, rM=rM, rR=rR, rN=rN):
                    return [
                        lambda: self.mm([(tot, ones32, ACC[pi][:, sl], True, True)], [R(f"ACC{pi}"), R("C32")], [R("bank5")]),
                        lambda: self.mm([(tot2, ones32, ACC2[pi][:, sl], True, True)], [R(f"ACC2{pi}"), R("C32")], [R("bank6")]),
                        lambda: self.act(MEAN[pi][:, sl], tot, AF.Copy, [R("bank5")], [rM], scale=1.0 / D),
                        lambda: self.tt("dve", M2[:, :n], MEAN[pi][:, sl], MEAN[pi][:, sl], ALU.mult, [rM], [R("M2")]),
                        lambda: self.stt(RSTD[pi][:, sl], tot2, 1.0 / D, M2[:, :n], ALU.mult, ALU.subtract, [R("bank6"), R("M2")], [rR]),
                        lambda: self.act(RSTD[pi][:, sl], RSTD[pi][:, sl], AF.Ln, [rR], [rR], bias=LN_EPS),
                        lambda: self.act(RSTD[pi][:, sl], RSTD[pi][:, sl], AF.Exp, [rR], [rR], scale=-0.5),
                        lambda: self.stt(NMR[pi][:, sl], MEAN[pi][:, sl], -1.0, RSTD[pi][:, sl], ALU.mult, ALU.mult, [rM, rR], [rN]),
                    ]
                chains.append(mk())
            SK = 3
            order = []
            n0 = len(chains[0])
            for i in range(n0 + SK):
                if i < n0:
                    order.append(chains[0][i])
                if len(chains) > 1 and 0 <= i - SK < n0:
                    order.append(chains[1][i - SK])
            for f in order:
                f()

        def normalize(pi, j):
            cgs = passes[pi]
            p0 = cgs[0][0]
            gj = self.PAR[:, P_LNG + layer * 16 + j:P_LNG + layer * 16 + j + 1]
            bj = self.PAR[:, P_LNB + layer * 16 + j:P_LNB + layer * 16 + j + 1]
            for (c0, n) in cgs:
                sl = slice(c0 - p0, c0 - p0 + n)
                kb = cnt["ko"] % 4
                cnt["ko"] += 1
                zs = Z[:, j, sl]
                self.tt("dve", zs, zs, RSTD[pi][:, sl], ALU.mult, [R(f"Z{j}_{sl.start}"), R(f"RSTD{pi}_{sl.start}")], [R(f"Z{j}_{sl.start}")])
                self.tt("dve", zs, zs, NMR[pi][:, sl], ALU.add, [R(f"Z{j}_{sl.start}"), R(f"NMR{pi}_{sl.start}")], [R(f"Z{j}_{sl.start}")])
                xo = XO[kb][:, :n]
                self.act(xo, zs, AF.Identity, [R(f"Z{j}_{sl.start}"), R("PAR")], [R(f"XO{kb}")], bias=bj, scale=gj)
                if layer == 0:
                    xw = sorted({min(c0 // 512, 2), min((c0 + n - 1) // 512, 2)})
                    self.act(self.XTm[:, j, c0:c0 + n], zs, AF.Identity, [R(f"Z{j}_{sl.start}"), R("PAR")], [R(f"XTm_{i}") for i in xw], bias=bj, scale=gj)
                    self.dma("act", self.x1T[:, j, c0:c0 + n], xo, [R(f"XO{kb}")], [R(f"x1T_{j}_{c0}")])
                    if DEBUG:
                        self.dma("act", self.dbg_x1[:, j, c0:c0 + n], xo, [R(f"XO{kb}")], [])
                else:
                    self.dma("act", self.yT[:, j, c0 - 128:c0 - 128 + n], xo, [R(f"XO{kb}")], [])

        for j in range(NCH):
            accumulate(0, j)
        stats(0)
        for j in range(NCH):
            normalize(0, j)
            accumulate(1, j)
        stats(1)
        self.slab_issue_upto(self.sl_used + 8)
        for j in range(NCH):
            normalize(1, j)

    def swa(self):
        R = self.R
        self.carve_reset()
        KcT = self.c16(16 * 4 * 128)
        Vc = self.c16(16 * 256)
        Vtok = self.c16(10 * 256)
        KTd = self.c16(4 * NM)
        QT = self.c16(4 * NMP)
        GT = self.c16(4 * NMP)
        PT = [self.c16(512) for _ in range(8)]
        att = {"st_i": 0, "on_i": 0}
        ON = [self.c16(512) for _ in range(2)]
        D2 = [self.c32(512) for _ in range(2)]
        EG = [self.c32(512) for _ in range(2)]
        STG = [self.c32(128) for _ in range(2)]
        KcTv = KcT.rearrange("p (q k s) -> p q k s", q=16, k=4)
        Vcv = Vc.rearrange("p (q d) -> p q d", q=16)
        Vtv = Vtok.rearrange("p (t d) -> p t d", t=10)
        KTdv = KTd.rearrange("p (k n) -> p k n", k=4)
        QTv = QT.rearrange("p (c n) -> p c n", c=4)
        GTv = GT.rearrange("p (c n) -> p c n", c=4)
        C16 = self.C16
        ones16 = C16[:, C_ONE:C_ONE + 128]
        self.n_ring = 3
        ST_BANKS = [(self.ps32[:, 3 * 512:4 * 512], R("bank3")), (self.ps32[:, 4 * 512:5 * 512], R("bank4")), (self.ps16[:, :].bitcast(F32), R("bank7"))]
        self.bank_pos = 0
        XT = self.XTm
        xall = [R("XTm_0"), R("XTm_1"), R("XTm_2")]

        import os
        skip = os.environ.get("KSKIP", "")
        for kk in range(4 if "cache" not in skip else 0):
            src = self.ckT[:, kk].rearrange("q d s -> d q s")
            self.dma("pool", KcTv[0:64, :, kk, :], src, [], [R("KcT")])
            self.dma("pool", KcTv[64:128, :, kk, :], src, [], [R("KcT")])
        if "cache" not in skip:
            self.dma("pool", Vcv, self.cv.rearrange("q s d -> s q d"), [], [R("Vc")])

        if self.stop_after == "swa0":
            return
        stg_i = 0
        for si in range(4):
            slab, rsl = self.next_slab(116 + si)
            isv = si >= 2
            col = (si % 2) * 128
            tiles = range(10) if isv else (8, 9)
            for t in tiles:
                b = self.next_bank()
                out = self.ps32[:, b * 512:b * 512 + 128]
                mms = [(out, XT[:, c, t * 128:(t + 1) * 128], slab[:, c, :], c == 0, c == NCH - 1) for c in range(NCH)]
                self.mm(mms, [rsl] + xall, [R(f"bank{b}")])
                if isv:
                    self.copy("act", Vtv[:, t, col:col + 128], out, [R(f"bank{b}")], [R("Vtok")])
                if t >= 8:
                    sb = stg_i % 2
                    stg_i += 1
                    self.copy("dve", STG[sb][:, :], out, [R(f"bank{b}")], [R(f"STG{sb}")])
                    if t == 8:
                        dst = (self.vp if isv else self.kp)[:, col:col + 128]
                        self.dma("sp", dst, STG[sb][:, :], [R(f"STG{sb}")], [])
                    else:
                        dd = self.vs if isv else self.ks
                        for q in range(16 if "small" not in skip else 0):
                            self.dma("sp", dd[q, 120:128, col:col + 128], STG[sb][q * 8:(q + 1) * 8, :], [R(f"STG{sb}"), R("ks_copy"), R("vs_copy")], [])
        if self.stop_after == "swa1a":
            return
        for kvh in range(4):
            slab, rsl = self.next_slab(112 + kvh)
            for (c0, n) in [(0, 512), (512, 512), (1024, 256)]:
                o, ro = self.proj(slab, rsl, XT, xall, c0, n)
                self.copy("act", KTdv[:, kvh, c0:c0 + n], o, [ro], [R("KTd")])
        if self.stop_after == "swa1b":
            return
        st_i = 0
        on_i = 0
        for kvh in range(4):
            for cc in range(4):
                slab, rsl = self.next_slab(80 + 4 * kvh + cc)
                for (c0, n) in [(128, 512), (640, 512), (1152, 128)]:
                    o, ro = self.proj(slab, rsl, XT, xall, c0, n)
                    self.copy("act", QTv[:, cc, c0 - 128:c0 - 128 + n], o, [ro], [R("QTs")])
            for cc in range(4):
                slab, rsl = self.next_slab(96 + 4 * kvh + cc)
                for gi_, (c0, n) in enumerate([(128, 512), (640, 512), (1152, 128)]):
                    o, ro = self.proj(slab, rsl, XT, xall, c0, n)
                    eg = EG[gi_ % 2][:, :n]
                    rE = R(f"EG{gi_ % 2}")
                    self.act(eg, o, AF.Exp, [ro], [rE], scale=-1.0)
                    self.act(eg, eg, AF.Ln, [rE], [rE], bias=1.0)
                    self.act(eg, eg, AF.Exp, [rE], [rE], scale=-1.0)
                    self.tt("dve", GTv[:, cc, c0 - 128:c0 - 128 + n], o, eg, ALU.mult, [ro, rE], [R("GTs")])
            Ob = self.ps32[:, 5 * 512:6 * 512]
            Db = self.ps32[:, 6 * 512:7 * 512]
            Obv = Ob.rearrange("p (c n) -> p c n", c=4)
            Dbv = Db.rearrange("p (c n) -> p c n", c=4)
            units = [(jt, par) for jt in list(range(1, 9)) + [9] for par in range(2)]

            def S1(jt):
                sample = jt == 9
                qc = (jt - 1) * 128
                st = Item2()
                st.pts = {0: [], 1: []}
                st.ptc = {}
                kts = [jt] if sample else [jt - 1, jt]
                for kt in kts:
                    slots = []
                    mms_all = []
                    for par in range(2):
                        hp = slice(par * 64, par * 64 + 64)
                        i_ = att["st_i"]
                        att["st_i"] += 1
                        pt = PT[i_ % 8]
                        rpt = R(f"PT{i_ % 8}")
                        ST, rST = ST_BANKS[i_ % 3]
                        if sample:
                            rhs = QTv[hp, :, qc:qc + 128].rearrange("p c (q t) -> p q c t", t=8)
                        else:
                            rhs = QTv[hp, :, qc:qc + 128]
                        self.mm([(ST, KTdv[hp, kvh, kt * 128:(kt + 1) * 128], rhs, True, True)],
                                [R("KTd"), R("QTs")], [rST])
                        slots.append((pt, rpt, ST, rST, par))
                    for (pt, rpt, ST, rST, par) in slots:
                        self.act(pt[:, :], ST, AF.Exp, [rST], [rpt], bias=-SHIFT, scale=0.125)
                        if sample:
                            mk = C16[:, C_MS:C_MS + 128].rearrange("p (q t) -> p q t", t=8).unsqueeze(2).broadcast_to([128, 16, 4, 8])
                            ptv = pt.rearrange("p (q c t) -> p q c t", q=16, c=4)
                        else:
                            if kt == jt:
                                mcol = C_MC
                            else:
                                mcol = C_MP1 if jt == 1 else C_MP
                            mk = C16[:, mcol:mcol + 128].unsqueeze(1).broadcast_to([128, 4, 128])
                            ptv = pt.rearrange("p (c n) -> p c n", c=4)
                        self.tt("dve", ptv, ptv, mk, ALU.mult, [rpt, R("C16")], [rpt])
                        st.pts[par].append((pt, rpt, kt))
                if sample:
                    for par in range(2):
                        hp = slice(par * 64, par * 64 + 64)
                        i_ = att["st_i"]
                        att["st_i"] += 1
                        ptc = PT[i_ % 8]
                        rptc = R(f"PT{i_ % 8}")
                        STcb, rSTc = ST_BANKS[i_ % 3]
                        mms = [(STcb[:, q * 32:(q + 1) * 32], KcTv[hp, q, kvh, :], QTv[hp, :, qc + q * 8:qc + (q + 1) * 8], True, True) for q in range(16)]
                        self.mm(mms, [R("KcT"), R("QTs")], [rSTc])
                        self.act(ptc[:, :], STcb, AF.Exp, [rSTc], [rptc], bias=-SHIFT, scale=0.125)
                        ptcv4 = ptc.rearrange("p (q c t) -> p q c t", q=16, c=4)
                        mk = C16[:, C_MCA:C_MCA + 8].unsqueeze(1).unsqueeze(1).broadcast_to([128, 16, 4, 8])
                        self.tt("dve", ptcv4, ptcv4, mk, ALU.mult, [rptc, R("C16")], [rptc])
                        st.ptc[par] = (ptc, rptc)
                return st

            def S2(jt, st):
                sample = jt == 9
                mo = []
                md = []
                rr = [R("Vtok"), R("C16")]
                nk = len(st.pts[0])
                for i in range(nk):
                    for par in range(2):
                        hp = slice(par * 64, par * 64 + 64)
                        pt, rpt, kt = st.pts[par][i]
                        last = (i == nk - 1) and not sample
                        mo.append((Ob[hp, :], Vtv[:, kt, kvh * 64:(kvh + 1) * 64], pt[:, :], i == 0, last))
                        md.append((Db[hp, :], ones16[:, 0:64], pt[:, :], i == 0, last))
                        rr.append(rpt)
                if sample:
                    for q in range(16):
                        for par in range(2):
                            hp = slice(par * 64, par * 64 + 64)
                            ptc, rptc = st.ptc[par]
                            mo.append((Ob[hp, q * 32:(q + 1) * 32], Vcv[:, q, kvh * 64:(kvh + 1) * 64], ptc[:, q * 32:(q + 1) * 32], False, q == 15))
                            md.append((Db[hp, q * 32:(q + 1) * 32], ones16[:, 0:64], ptc[:, q * 32:(q + 1) * 32], False, q == 15))
                    rr += [st.ptc[0][1], st.ptc[1][1], R("Vc")]
                self.mm(mo, rr, [R("bank5")])
                self.mm(md, rr, [R("bank6")])

            def S3(jt):
                sample = jt == 9
                qc = (jt - 1) * 128
                ob_ = att["on_i"] % 2
                att["on_i"] += 1
                d2 = D2[ob_][:, :]
                es = self.ESINK[:, kvh * 4:(kvh + 1) * 4]
                if sample:
                    esb = es.unsqueeze(1).unsqueeze(3).broadcast_to([128, 16, 4, 8])
                    d2v = d2.rearrange("p (q c t) -> p q c t", q=16, c=4)
                    dbv = Db.rearrange("p (q c t) -> p q c t", q=16, c=4)
                else:
                    esb = es.unsqueeze(2).broadcast_to([128, 4, 128])
                    d2v = d2.rearrange("p (c n) -> p c n", c=4)
                    dbv = Dbv
                self.tt("dve", d2v, dbv, esb, ALU.add, [R("bank6"), R("ESINK")], [R(f"D2{ob_}")])
                self.act(d2, d2, AF.Ln, [R(f"D2{ob_}")], [R(f"D2{ob_}")])
                self.act(d2, d2, AF.Exp, [R(f"D2{ob_}")], [R(f"D2{ob_}")], scale=-1.0)
                self.tt("dve", ON[ob_][:, :], Ob, d2, ALU.mult, [R("bank5"), R(f"D2{ob_}")], [R(f"ON{ob_}")])
                mcol0 = 1152 if sample else jt * 128
                if sample:
                    o_out = self.OTa[:, kvh * 4:(kvh + 1) * 4, mcol0:mcol0 + 128].rearrange("p c (q t) -> p q c t", t=8)
                    o_in0 = ON[ob_].rearrange("p (q c t) -> p q c t", q=16, c=4)
                    o_in1 = GTv[:, :, qc:qc + 128].rearrange("p c (q t) -> p q c t", t=8)
                else:
                    o_out = self.OTa[:, kvh * 4:(kvh + 1) * 4, mcol0:mcol0 + 128]
                    o_in0 = ON[ob_].rearrange("p (c n) -> p c n", c=4)
                    o_in1 = GTv[:, :, qc:qc + 128]
                self.tt("pool", o_out, o_in0, o_in1, ALU.mult, [R(f"ON{ob_}"), R("GTs")], [R(f"OTa_{mcol0 // 128}")])

            tiles_ = list(range(1, 9)) + [9]
            cur = S1(tiles_[0])
            for ti_, jt in enumerate(tiles_):
                nxt_st = S1(tiles_[ti_ + 1]) if ti_ + 1 < len(tiles_) else None
                S2(jt, cur)
                S3(jt)
                cur = nxt_st


class Item2:
    pass


def build_nc(stop_after=None):
    nc = bass.Bass("TRN2", target_bir_lowering=False)
    b = Builder(nc, stop_after=stop_after)
    b.build()
    return nc


def _fm(a):
    cols = a.shape[0]
    return np.ascontiguousarray(a.reshape(cols, NCH, 128).transpose(2, 1, 0))


def _slab(w):
    return np.ascontiguousarray(w.reshape(NCH, 128, 128).transpose(1, 0, 2).reshape(128, 2048))


def _consts(half):
    c = np.zeros((128, NCONST), np.float32)
    s = np.arange(128)[:, None]
    t = np.arange(128)[None, :]
    c[:, C_ID:C_ID + 128] = (s == t)
    c[:, C_MC:C_MC + 128] = (s <= t)
    c[:, C_MP:C_MP + 128] = (s > t)
    mp1 = (s > t)
    if half == 0:
        mp1 = mp1 & (s >= 112)
    c[:, C_MP1:C_MP1 + 128] = mp1
    c[:, C_MS:C_MS + 128] = (s // 8 == t // 8) & (s % 8 <= t % 8)
    rp = np.ones(512, np.float32)
    rp[::128] = 0
    c[:, C_RP:C_RP + 512] = rp[None, :]
    rm2 = np.ones(256, np.float32)
    rm2[0] = 0
    rm2[128::8] = 0
    c[:, C_RM2:C_RM2 + 256] = rm2[None, :]
    c[:, C_SEL:C_SEL + 16] = (np.arange(128)[:, None] // 8 == np.arange(16)[None, :])
    c[:, C_MCA:C_MCA + 8] = (np.arange(128)[:, None] >= np.arange(8)[None, :] + 1)
    c[:, C_ONE:C_ONE + 128] = 1.0
    return c


def prepare_inputs(x_prompt, x_sample, state_hgrn, cache_swa_k, cache_swa_v, meta_tokens,
                   hgrn_w_in, hgrn_lb_logits, hgrn_norm_w, hgrn_w_out,
                   swa_w_in, swa_sinks, swa_w_out, ln_g, ln_b):
    f32 = np.float32
    x_prompt = np.asarray(x_prompt, f32)
    x_sample = np.asarray(x_sample, f32)
    wall = np.empty((NSLAB, 128, 2048), f32)
    hw = np.asarray(hgrn_w_in, f32)[0]
    for j in range(64):
        wall[j] = _slab(hw[:, j * 128:(j + 1) * 128])
    ho = np.asarray(hgrn_w_out, f32)[0]
    for j in range(16):
        wall[64 + j] = _slab(ho[:, j * 128:(j + 1) * 128])
    sw = np.asarray(swa_w_in, f32)[0]
    for j in range(16):
        wall[80 + j] = _slab(sw[:, j * 128:(j + 1) * 128])
        wall[96 + j] = _slab(sw[:, 2560 + j * 128:2560 + (j + 1) * 128])
    for kvh in range(4):
        wk = sw[:, 2048 + kvh * 64:2048 + (kvh + 1) * 64]
        wall[112 + kvh] = _slab(np.concatenate([wk, wk], axis=1))
    for i in range(4):
        wall[116 + i] = _slab(sw[:, 2048 + i * 128:2048 + (i + 1) * 128])
    so = np.asarray(swa_w_out, f32)[0]
    for j in range(16):
        wall[120 + j] = _slab(so[:, j * 128:(j + 1) * 128])
    pars = np.zeros((128, NPAR), f32)
    lbl = np.asarray(hgrn_lb_logits, f32)
    for i in range(3):
        pars[:, P_LBL + 16 * i:P_LBL + 16 * i + 16] = lbl[i].reshape(16, 128).T
    pars[:, P_NW:P_NW + 16] = np.asarray(hgrn_norm_w, f32)[0].T
    for l in range(2):
        pars[:, P_LNG + 16 * l:P_LNG + 16 * l + 16] = np.asarray(ln_g, f32)[l].reshape(16, 128).T
        pars[:, P_LNB + 16 * l:P_LNB + 16 * l + 16] = np.asarray(ln_b, f32)[l].reshape(16, 128).T
    sk = np.asarray(swa_sinks, f32)[0]
    pars[0:64, P_SINK:P_SINK + 16] = sk[0::2][None, :]
    pars[64:128, P_SINK:P_SINK + 16] = sk[1::2][None, :]
    meta = np.asarray(meta_tokens, f32)
    st = np.asarray(state_hgrn, f32)[0]
    ck = np.asarray(cache_swa_k, f32)[0].reshape(128, 128, 256)
    cv = np.asarray(cache_swa_v, f32)[0].reshape(128, 128, 256)
    consts = [_consts(0), _consts(1)]
    in_maps = []
    for core in range(8):
        seq, half = core // 2, core % 2
        cols = np.zeros((NX, D), f32)
        if half == 0:
            cols[NPRE + 112:NPRE + 128] = meta
            cols[NPRE + 128:] = x_prompt[seq, 0:1024]
        else:
            cols[112:128] = meta
            cols[128:] = x_prompt[seq]
        sl = slice(16 * core, 16 * core + 16)
        ckc = ck[sl]
        in_maps.append({
            "xT": _fm(cols),
            "xsT": _fm(x_sample[sl].reshape(128, D)),
            "s0": np.ascontiguousarray(st[sl]),
            "ckT": np.ascontiguousarray(ckc.reshape(16, 128, 4, 64).transpose(0, 2, 3, 1)),
            "cv": np.ascontiguousarray(cv[sl]),
            "ck": np.ascontiguousarray(ckc),
            "wall": wall,
            "consts": consts[half],
            "pars": pars,
        })
    return in_maps


_NC_CACHE = {}


def kernel(x_prompt, x_sample, state_hgrn, cache_swa_k, cache_swa_v, meta_tokens,
           hgrn_w_in, hgrn_lb_logits, hgrn_norm_w, hgrn_w_out,
           swa_w_in, swa_sinks, swa_w_out, ln_g, ln_b):
    in_maps = prepare_inputs(x_prompt, x_sample, state_hgrn, cache_swa_k, cache_swa_v, meta_tokens,
                             hgrn_w_in, hgrn_lb_logits, hgrn_norm_w, hgrn_w_out,
                             swa_w_in, swa_sinks, swa_w_out, ln_g, ln_b)
    nc = build_nc()
    res = run_bass_kernel_spmd(nc, in_maps, core_ids=list(range(8)))
    rs = res.results
    f32 = np.float32
    y_prompt = np.empty((4, 2048, D), f32)
    y_sample = np.empty((128, 8, D), f32)
    st_p = np.empty((1, 4, 16, 128, 128), f32)
    st_s = np.empty((1, 128, 16, 128, 128), f32)
    kp = np.empty((1, 4, 128, 4, 64), f32)
    vp = np.empty((1, 4, 128, 4, 64), f32)
    ks = np.empty((1, 128, 128, 4, 64), f32)
    vs = np.empty((1, 128, 128, 4, 64), f32)
    for core in range(8):
        r = rs[core]
        seq, half = core // 2, core % 2
        yT = np.asarray(r["yT"])
        ytok = yT.transpose(2, 1, 0).reshape(NMP, D)
        y_prompt[seq, half * 1024:(half + 1) * 1024] = ytok[0:1024]
        y_sample[16 * core:16 * core + 16] = ytok[1024:1152].reshape(16, 8, D)
        st_s[0, 16 * core:16 * core + 16] = np.asarray(r["ss_out"])
        ks[0, 16 * core:16 * core + 16] = np.asarray(r["ks"]).reshape(16, 128, 4, 64)
        vs[0, 16 * core:16 * core + 16] = np.asarray(r["vs"]).reshape(16, 128, 4, 64)
        if half == 1:
            st_p[0, seq] = np.asarray(r["sp_out"])
            kp[0, seq] = np.asarray(r["kp"]).reshape(128, 4, 64)
            vp[0, seq] = np.asarray(r["vp"]).reshape(128, 4, 64)
    return (y_prompt, y_sample, st_p, st_s, kp, vp, ks, vs)
```

```python
import contextlib
import numpy as np
import concourse.bass as bass
import concourse.mybir as mybir
from concourse.bass_utils import run_bass_kernel_spmd

F32 = mybir.dt.float32
BF16 = mybir.dt.bfloat16
AF = mybir.ActivationFunctionType
ALU = mybir.AluOpType

ENGS = ("pe", "act", "dve", "pool", "sp")

D = 2048
NCH = 16
NPRE = 1024
NMP = 1152
NM = 1280
NX = NPRE + NMP
ALPHA = (2.0 * 2) ** 0.25
LN_EPS = 1e-5
RMS_EPS = 1e-6
SHIFT = 30.0
NSLAB = 136
DEBUG = False

C_ID, C_MC, C_MP, C_MP1, C_MS, C_RP, C_RM2, C_SEL, C_MCA, C_ONE = 0, 128, 256, 384, 512, 640, 1152, 1408, 1424, 1432
NCONST = 1560
P_LBL, P_NW, P_LNG, P_LNB, P_SINK = 0, 48, 64, 96, 128
NPAR = 144


class Res:
    __slots__ = ("name", "writer", "readers")

    def __init__(self, name):
        self.name = name
        self.writer = None
        self.readers = {}


class Prog:
    def __init__(self, nc, n_dma_sems=(16, 8, 12)):
        self.nc = nc
        self.ops = {e: [] for e in ENGS}
        self.count = {e: 0 for e in ENGS}
        self.waited = {e: {} for e in ENGS}
        self.dma_ring = {"sp": [("dsp", i) for i in range(n_dma_sems[0])],
                         "act": [("dact", i) for i in range(n_dma_sems[1])],
                         "pool": [("dpool", i) for i in range(n_dma_sems[2])]}
        self.dma_pos = {"sp": 0, "act": 0, "pool": 0}
        self.dma_total = {}
        for q in self.dma_ring:
            for k in self.dma_ring[q]:
                self.dma_total[k] = 0

    def _collect(self, eng, reads, writes):
        need = {}

        def add(tok):
            if tok is None:
                return
            k, v = tok
            if need.get(k, 0) < v:
                need[k] = v
        for r in reads:
            add(r.writer)
        for w in writes:
            add(w.writer)
            for k, v in w.readers.items():
                add((k, v))
        out = []
        wd = self.waited[eng]
        for k, v in need.items():
            if eng == "pe" and k == "pe":
                continue
            if wd.get(k, 0) >= v:
                continue
            wd[k] = v
            out.append((k, v))
        return out

    def _mark(self, tok, reads, writes):
        k, v = tok
        for r in reads:
            if r.readers.get(k, 0) < v:
                r.readers[k] = v
        for w in writes:
            w.writer = tok
            w.readers = {}

    def op(self, eng, fn, reads=(), writes=()):
        bk = [r for r in reads if r.name.startswith("bank")]
        if bk:
            reads = [r for r in reads if not r.name.startswith("bank")]
            writes = list(writes) + [b for b in bk if b not in writes]
        waits = self._collect(eng, reads, writes)
        self.count[eng] += 1
        tok = (eng, self.count[eng])
        self._mark(tok, reads, writes)
        self.ops[eng].append((waits, fn, (eng, 1)))
        return tok

    def dma(self, q, fn, reads=(), writes=()):
        ring = self.dma_ring[q]
        key = ring[self.dma_pos[q] % len(ring)]
        self.dma_pos[q] += 1
        waits = self._collect(q, reads, writes)
        prev = self.dma_total[key]
        if prev > 0 and self.waited[q].get(key, 0) < prev:
            self.waited[q][key] = prev
            waits.append((key, prev))
        self.dma_total[key] = prev + 16
        tok = (key, prev + 16)
        self._mark(tok, reads, writes)
        self.ops[q].append((waits, fn, (key, 16)))
        return tok

    def barrier(self):
        toks = [(e, self.count[e]) for e in ("pe", "act", "dve", "pool") if self.count[e] > 0]
        toks += [(k, v) for k, v in self.dma_total.items() if v > 0]
        for e in ENGS:
            waits = []
            for k, v in toks:
                if self.waited[e].get(k, 0) >= v:
                    continue
                self.waited[e][k] = v
                waits.append((k, v))
            if waits:
                self.ops[e].append((waits, None, None))

    def emit(self):
        nc = self.nc
        keys = ["pe", "act", "dve", "pool"] + list(self.dma_total)
        with contextlib.ExitStack() as st:
            sems = {}
            for k in keys:
                nm = k if isinstance(k, str) else f"{k[0]}{k[1]}"
                sems[k] = st.enter_context(nc.semaphore("s_" + nm))
            block = st.enter_context(nc.Block())

            def run(engname):
                def body(e):
                    for waits, fn, inc in self.ops[engname]:
                        for k, v in waits:
                            e.wait_ge(sems[k], v)
                        if fn is None:
                            continue
                        ins = fn(e)
                        ins.then_inc(sems[inc[0]], inc[1])
                return body
            block.sync(run("sp"))
            block.tensor(run("pe"))
            block.scalar(run("act"))
            block.vector(run("dve"))
            block.gpsimd(run("pool"))


class Builder:
    def __init__(self, nc, stop_after=None):
        self.nc = nc
        self.P = Prog(nc)
        self.res = {}
        self.stop_after = stop_after

    def R(self, name):
        r = self.res.get(name)
        if r is None:
            r = self.res[name] = Res(name)
        return r

    def act(self, out, in_, func, reads, writes, bias=None, scale=None):
        kw = {}
        if bias is not None:
            kw["bias"] = bias
        if scale is not None:
            kw["scale"] = scale
        self.P.op("act", lambda e: e.activation(out=out, in_=in_, func=func, **kw), reads, writes)

    def tt(self, eng, out, in0, in1, op, reads, writes):
        self.P.op(eng, lambda e: e.tensor_tensor(out=out, in0=in0, in1=in1, op=op), reads, writes)

    def ts(self, eng, out, in0, s1, op0, reads, writes, s2=None, op1=None):
        if op1 is None:
            self.P.op(eng, lambda e: e.tensor_scalar(out=out, in0=in0, scalar1=s1, scalar2=None, op0=op0), reads, writes)
        else:
            self.P.op(eng, lambda e: e.tensor_scalar(out=out, in0=in0, scalar1=s1, scalar2=s2, op0=op0, op1=op1), reads, writes)

    def stt(self, out, in0, scalar, in1, op0, op1, reads, writes):
        self.P.op("dve", lambda e: e.scalar_tensor_tensor(out=out, in0=in0, scalar=scalar, in1=in1, op0=op0, op1=op1), reads, writes)

    def copy(self, eng, out, in_, reads, writes):
        if eng == "act":
            self.P.op("act", lambda e: e.activation(out=out, in_=in_, func=AF.Copy), reads, writes)
        else:
            self.P.op(eng, lambda e: e.tensor_copy(out=out, in_=in_), reads, writes)

    def memset(self, eng, ap, val, writes):
        self.P.op(eng, lambda e: e.memset(ap, val), (), writes)

    def mm(self, mms, reads, writes):
        def fn(e):
            ins = None
            for (o, l, r, s0, s1) in mms:
                ins = e.matmul(o, lhsT=l, rhs=r, start=s0, stop=s1)
            return ins
        self.P.op("pe", fn, reads, writes)

    def tr(self, trs, reads, writes):
        def fn(e):
            ins = None
            for (o, i, ident) in trs:
                ins = e.transpose(out=o, in_=i, identity=ident)
            return ins
        self.P.op("pe", fn, reads, writes)

    def dma(self, q, out, in_, reads, writes):
        self.P.dma(q, lambda e: e.dma_start(out=out, in_=in_), reads, writes)

    def slab_setup(self, schedule):
        self.sched = schedule
        self.sl_issued = 0
        self.sl_used = 0

    def slab_issue_upto(self, n):
        n = min(n, len(self.sched))
        while self.sl_issued < n:
            i = self.sl_issued
            slot = i % 8
            sid = self.sched[i]
            dst = self.RING[:, slot * 2048:(slot + 1) * 2048]
            self.dma("pool", dst, self.wall[sid], [], [self.R(f"slot{slot}")])
            self.sl_issued += 1

    def next_slab(self, sid):
        i = self.sl_used
        assert self.sched[i] == sid, (i, self.sched[i], sid)
        if getattr(self, "auto_prefetch", True):
            self.slab_issue_upto(i + 5)
        self.sl_used += 1
        slot = i % 8
        ap = self.RING[:, slot * 2048:(slot + 1) * 2048].rearrange("p (c e) -> p c e", c=NCH)
        return ap, self.R(f"slot{slot}")

    def next_bank(self):
        b = self.bank_pos % self.n_ring
        self.bank_pos += 1
        return b

    def proj(self, slab, slabres, X, xres, c0, n):
        b = self.next_bank()
        out = self.ps32[:, b * 512:b * 512 + n]
        mms = [(out, slab[:, c, :], X[:, c, c0:c0 + n], c == 0, c == NCH - 1) for c in range(NCH)]
        self.mm(mms, [slabres] + list(xres), [self.R(f"bank{b}")])
        return out, self.R(f"bank{b}")

    def build(self):
        nc = self.nc
        dt_in = lambda name, shape: nc.dram_tensor(name, shape, F32, kind="ExternalInput").ap()
        dt_out = lambda name, shape: nc.dram_tensor(name, shape, F32, kind="ExternalOutput").ap()
        self.xT = dt_in("xT", [128, NCH, NX])
        self.xsT = dt_in("xsT", [128, NCH, 128])
        self.s0 = dt_in("s0", [16, 16, 128, 128])
        self.ckT = dt_in("ckT", [16, 4, 64, 128])
        self.cv = dt_in("cv", [16, 128, 256])
        self.ck = dt_in("ck", [16, 128, 256])
        self.wall = dt_in("wall", [NSLAB, 128, 2048])
        self.consts = dt_in("consts", [128, NCONST])
        self.pars = dt_in("pars", [128, NPAR])
        self.yT = dt_out("yT", [128, NCH, NMP])
        self.sp_out = dt_out("sp_out", [16, 128, 128])
        self.ss_out = dt_out("ss_out", [16, 16, 128, 128])
        self.kp = dt_out("kp", [128, 256])
        self.vp = dt_out("vp", [128, 256])
        self.ks = dt_out("ks", [16, 128, 256])
        self.vs = dt_out("vs", [16, 128, 256])
        self.x1T = nc.dram_tensor("x1T", [128, NCH, NM], F32, kind="Internal").ap()
        if DEBUG:
            self.dbg_x1 = dt_out("dbg_x1", [128, NCH, NM])
            self.dbg_ot = dt_out("dbg_ot", [128, NCH, NM])

        with contextlib.ExitStack() as st:
            E = st.enter_context
            self.XTm_t = E(nc.sbuf_tensor("XTm", [128, NCH * NM], BF16))
            self.OTa_t = E(nc.sbuf_tensor("OTa", [128, NCH * NM], BF16))
            self.RING = E(nc.sbuf_tensor("RING", [128, 8 * 2048], BF16))
            self.C32 = E(nc.sbuf_tensor("C32", [128, NCONST], F32))
            self.C16 = E(nc.sbuf_tensor("C16", [128, NCONST], BF16))
            self.PAR = E(nc.sbuf_tensor("PAR", [128, 256], F32))
            self.SCR = E(nc.sbuf_tensor("SCR", [128, 21800], F32))
            self.ps32 = E(nc.psum_tensor("ps32", [128, 7 * 512], F32))
            self.ps16 = E(nc.psum_tensor("ps16", [128, 1024], BF16))
            self.XTm = self.XTm_t[:].rearrange("p (c n) -> p c n", c=NCH)
            self.OTa = self.OTa_t[:].rearrange("p (c n) -> p c n", c=NCH)
            self.program()
            self.P.barrier()
            self.P.emit()

    def carve_reset(self):
        self.cpos = 0

    def c32(self, n):
        a = self.SCR[:, self.cpos:self.cpos + n]
        self.cpos += n
        assert self.cpos <= 21800, self.cpos
        return a

    def c16(self, n):
        n32 = (n + 1) // 2
        a = self.SCR[:, self.cpos:self.cpos + n32].bitcast(BF16)
        self.cpos += n32
        assert self.cpos <= 21800, self.cpos
        return a

    def program(self):
        R = self.R
        sched = []
        for h in range(16):
            sched += [h, 16 + h, 32 + h, 48 + h]
        sched += [64 + j for j in range(16)] * 2
        sched += [116, 117, 118, 119, 112, 113, 114, 115]
        for kvh in range(4):
            sched += [80 + 4 * kvh + i for i in range(4)] + [96 + 4 * kvh + i for i in range(4)]
        sched += [120 + j for j in range(16)] * 2
        self.slab_setup(sched)

        self.dma("sp", self.C32[:], self.consts, [], [R("C32")])
        self.dma("sp", self.PAR[:, 0:NPAR], self.pars, [], [R("PAR")])
        self.copy("act", self.C16[:], self.C32[:], [R("C32")], [R("C16")])
        self.setup_params()
        self.hgrn()
        if self.stop_after == "hgrn":
            return
        self.P.barrier()
        self.wout_ln(0)
        if self.stop_after == "ln0":
            return
        self.P.barrier()
        self.swa()
        if self.stop_after in ("swa0", "swa1a", "swa1b", "swa"):
            return
        self.P.barrier()
        self.wout_ln(1)

    def setup_params(self):
        R = self.R
        PAR = self.PAR
        l0, l1, l2 = PAR[:, 0:16], PAR[:, 16:32], PAR[:, 32:48]
        mx = PAR[:, 144:160]
        ex = PAR[:, 160:208]
        sm = PAR[:, 208:224]
        self.LB = PAR[:, 224:240]
        self.C1 = PAR[:, 240:256]
        rP = [R("PAR")]
        self.tt("dve", mx, l0, l1, ALU.max, rP, [R("pmx")])
        self.tt("dve", mx, mx, l2, ALU.max, rP + [R("pmx")], [R("pmx")])
        for i in range(3):
            self.tt("dve", ex[:, 16 * i:16 * i + 16], PAR[:, 16 * i:16 * i + 16], mx, ALU.subtract, rP + [R("pmx")], [R(f"pex{i}")])
        self.act(ex, ex, AF.Exp, [R("pex0"), R("pex1"), R("pex2")], [R("pex")])
        self.tt("dve", sm, ex[:, 0:16], ex[:, 16:32], ALU.add, [R("pex")], [R("psm")])
        self.tt("dve", sm, sm, ex[:, 32:48], ALU.add, [R("pex"), R("psm")], [R("psm")])
        self.P.op("dve", lambda e: e.reciprocal(out=sm, in_=sm), [R("psm")], [R("psm")])
        self.tt("dve", self.LB, ex[:, 0:16], sm, ALU.mult, [R("pex"), R("psm")], [R("LB")])
        self.act(self.C1, self.LB, AF.Ln, [R("LB")], [R("C1")], bias=1.0, scale=-1.0)
        self.ESINK = PAR[:, P_SINK:P_SINK + 16]
        self.act(self.ESINK, self.ESINK, AF.Exp, rP, [R("ESINK")], bias=-SHIFT)

    def hgrn(self):
        R = self.R
        self.carve_reset()
        XTp_t = self.c16(NCH * NPRE)
        XTp = XTp_t.rearrange("p (c n) -> p c n", c=NCH)
        QT = [self.c16(512) for _ in range(2)]
        KT = [self.c16(512) for _ in range(2)]
        KH = [self.c16(512) for _ in range(2)]
        VT = [self.c16(512) for _ in range(2)]
        GATE = [self.c16(512) for _ in range(2)]
        VK = [self.c16(256) for _ in range(3)]
        ATm = [self.c16(128) for _ in range(3)]
        SQ = [self.c16(128) for _ in range(3)]
        VbAll = self.c16(2048)
        Vb = [VbAll[:, 512 * i:512 * (i + 1)] for i in range(4)]
        self.S0BF = VbAll
        self.VKS = self.c16(256)
        U, LA, Bb, L1, SG = [self.c32(512) for _ in range(5)]
        EQ = U
        DK = LA
        QT32 = [self.c32(512) for _ in range(2)]
        RS = [self.c32(128) for _ in range(3)]
        T1 = [self.c32(128) for _ in range(3)]
        S = [self.c32(128) for _ in range(2)]
        S0b = [self.c32(2048) for _ in range(2)]
        CB = [self.c32(8) for _ in range(2)]
        EBL = [self.c32(8) for _ in range(2)]
        CBs = [self.c32(16) for _ in range(2)]
        EBLs = [self.c32(16) for _ in range(2)]
        self.n_ring = 4
        self.bank_pos = 0
        self.auto_prefetch = False
        C32, C16 = self.C32, self.C16
        ident16 = C16[:, C_ID:C_ID + 128]
        ones16 = C16[:, C_ONE:C_ONE + 128]

        xgroups = [("pre", 0, 512), ("pre", 512, 512), ("main", 0, 512), ("main", 512, 512), ("main", 1024, 256)]
        self.dma("pool", XTp[:, :, 0:512], self.xT[:, :, 0:512], [], [R("XTp_0")])
        self.slab_issue_upto(4)
        self.dma("pool", XTp[:, :, 512:1024], self.xT[:, :, 512:1024], [], [R("XTp_1")])
        self.dma("pool", self.XTm[:, :, 0:512], self.xT[:, :, NPRE:NPRE + 512], [], [R("XTm_0")])
        self.dma("pool", self.XTm[:, :, 512:1024], self.xT[:, :, NPRE + 512:NPRE + 1024], [], [R("XTm_1")])
        self.dma("pool", self.XTm[:, :, 1024:1152], self.xT[:, :, NPRE + 1024:NPRE + 1152], [], [R("XTm_2")])
        self.dma("pool", self.XTm[:, :, 1152:1280], self.xsT, [], [R("XTm_2")])

        def load_s0(h):
            dst = S0b[h % 2].rearrange("p (s v) -> p s v", s=16)
            src = self.s0[:, h].rearrange("s k v -> k s v")
            self.dma("sp", dst, src, [], [R(f"S0b{h % 2}")])

        class Item:
            pass
        items = []
        for h in range(16):
            for gidx, (kind, c0, n) in enumerate(xgroups):
                it = Item()
                it.h, it.kind, it.c0, it.n, it.gidx = h, kind, c0, n, gidx
                it.main = kind == "main"
                it.m2 = it.main and c0 == 1024
                it.nt = n // 128
                it.gb = len(items) % 2
                items.append(it)
        heads = {}
        tstate = {"ti": 0}

        def setup_head(h):
            hd = Item()
            hd.sq, hd.rq = self.next_slab(h)
            hd.sf, hd.rf = self.next_slab(16 + h)
            hd.si, hd.ri = self.next_slab(32 + h)
            hd.sg, hd.rg = self.next_slab(48 + h)
            self.slab_issue_upto(4 * (h + 2))
            hd.Sh = S[h % 2]
            hd.rS = R(f"S{h % 2}")
            self.memset("dve", hd.Sh, 0.0, [hd.rS])
            hd.lbh = self.LB[:, h:h + 1]
            hd.c1h = self.C1[:, h:h + 1]
            hd.nwh = self.PAR[:, P_NW + h:P_NW + h + 1]
            heads[h] = hd

        rpar = [R("LB"), R("C1"), R("PAR")]

        def proj_chunks(it):
            hd = heads[it.h]
            if it.main:
                X, xres = self.XTm, [R(f"XTm_{it.c0 // 512}")]
            else:
                X, xres = XTp, [R(f"XTp_{it.c0 // 512}")]

            def mk(attr, rattr, slab, rsl):
                def f():
                    o, r = self.proj(slab, rsl, X, xres, it.c0, it.n)
                    setattr(it, attr, o)
                    setattr(it, rattr, r)
                return f
            ch = [mk("pf", "rpf", hd.sf, hd.rf), mk("pi", "rpi", hd.si, hd.ri)]
            if it.main:
                ch += [mk("pq", "rpq", hd.sq, hd.rq), mk("pg", "rpg", hd.sg, hd.rg)]
            return ch

        def proj_item(it):
            for f in proj_chunks(it):
                f()

        def gatingA(it):
            hd = heads[it.h]
            n, gb, m2, nt = it.n, it.gb, it.m2, it.nt
            u, la, bb, l1, dk = U[:, :n], LA[:, :n], Bb[:, :n], L1[:, :n], DK[:, :n]
            ops = []
            ops.append(lambda: self.act(u, it.pf, AF.Exp, [it.rpf], [R("U")]))
            ops.append(lambda: self.act(la, u, AF.Ln, [R("U")] + rpar, [R("LA")], bias=hd.lbh))
            ops.append(lambda: self.act(l1, u, AF.Ln, [R("U")], [R("L1")], bias=1.0))
            ops.append(lambda: self.tt("dve", la, la, l1, ALU.subtract, [R("LA"), R("L1")], [R("LA")]))
            rst = C32[:, C_RM2:C_RM2 + 256] if m2 else C32[:, C_RP:C_RP + n]
            ops.append(lambda: self.P.op("dve", lambda e: e.tensor_tensor_scan(out=bb, data0=rst, data1=la, initial=0.0, op0=ALU.mult, op1=ALU.add),
                                         [R("LA"), R("C32")], [R("B")]))
            it.nA1a = len(ops)
            ops.append(lambda: self.copy("act", VT[gb][:, :n], it.pi, [it.rpi], [R(f"VT{gb}")]))
            ntp = 1 if m2 else nt
            blast = Bb[:, 127:128 * ntp:128]
            it.nA1 = len(ops)
            ops.append(lambda: self.act(EBL[gb][:, 0:ntp], blast, AF.Exp, [R("B")], [R(f"EBL{gb}")]))
            ops.append(lambda: self.ts("dve", CB[gb][:, 0:ntp], blast, hd.c1h, ALU.add, [R("B")] + rpar, [R(f"CB{gb}")]))
            if m2:
                bl_s = Bb[:, 128 + 7:256:8]
                ops.append(lambda: self.act(EBLs[gb][:, :], bl_s, AF.Exp, [R("B")], [R(f"EBLs{gb}")]))
                ops.append(lambda: self.ts("dve", CBs[gb][:, :], bl_s, hd.c1h, ALU.add, [R("B")] + rpar, [R(f"CBs{gb}")]))
            ops.append(lambda: self.tt("dve", l1, bb, l1, ALU.add, [R("B"), R("L1")], [R("L1")]))
            dkv = DK[:, 0:128 * ntp].rearrange("p (t n) -> p t n", n=128)
            zv = L1[:, 0:128 * ntp].rearrange("p (t n) -> p t n", n=128)
            cbv = CB[gb][:, 0:ntp].unsqueeze(2).broadcast_to([128, ntp, 128])
            ops.append(lambda: self.tt("dve", dkv, cbv, zv, ALU.subtract, [R(f"CB{gb}"), R("L1")], [R("LA")]))
            if m2:
                dkv2 = DK[:, 128:256].rearrange("p (s n) -> p s n", n=8)
                zv2 = L1[:, 128:256].rearrange("p (s n) -> p s n", n=8)
                cbv2 = CBs[gb][:, :].unsqueeze(2).broadcast_to([128, 16, 8])
                ops.append(lambda: self.tt("dve", dkv2, cbv2, zv2, ALU.subtract, [R(f"CBs{gb}"), R("L1")], [R("LA")]))
            ops.append(lambda: self.act(KH[gb][:, :n], dk, AF.Exp, [R("LA")], [R(f"KH{gb}")]))
            return ops

        def gatingB(it):
            if not it.main:
                return []
            hd = heads[it.h]
            n, gb = it.n, it.gb
            bb, l1, eq, sgt = Bb[:, :n], L1[:, :n], EQ[:, :n], SG[:, :n]
            ops = []
            ops.append(lambda: self.act(eq, bb, AF.Exp, [R("B")], [R("U")]))
            ops.append(lambda: self.act(KT[gb][:, :n], l1, AF.Exp, [R("L1")] + rpar, [R(f"KT{gb}")], bias=hd.c1h, scale=-1.0))
            ops.append(lambda: self.tt("dve", QT32[gb][:, :n], it.pq, eq, ALU.mult, [it.rpq, R("U")], [R(f"QT32{gb}")]))
            ops.append(lambda: self.tt("dve", QT[gb][:, :n], it.pq, eq, ALU.mult, [it.rpq, R("U")], [R(f"QT{gb}")]))
            ops.append(lambda: self.act(sgt, it.pg, AF.Exp, [it.rpg], [R("SG")], scale=-1.0))
            ops.append(lambda: self.act(sgt, sgt, AF.Ln, [R("SG")], [R("SG")], bias=1.0))
            ops.append(lambda: self.act(sgt, sgt, AF.Exp, [R("SG")], [R("SG")], scale=-1.0))
            ops.append(lambda: self.stt(GATE[gb][:, :n], it.pg, hd.nwh, sgt, ALU.mult, ALU.mult, [it.rpg, R("SG"), R("PAR")], [R(f"GATE{gb}")]))
            return ops

        def mk_tile(it, t):
            td = Item()
            td.it, td.t = it, t
            td.tb = tstate["ti"] % 3
            tstate["ti"] += 1
            td.stage = 0
            return td

        def tile_views(td):
            tb = td.tb
            sm0 = (4 + tb) * 512
            v = Item()
            v.AT = self.ps32[:, sm0:sm0 + 128]
            v.Op = self.ps32[:, sm0 + 128:sm0 + 256]
            v.SSp = self.ps32[:, sm0 + 256:sm0 + 384]
            v.SPp = self.ps32[:, sm0 + 384:sm0 + 512]
            v.psb = self.ps16[:, tb * 256:(tb + 1) * 256]
            v.bk = R(f"bank{4 + tb}")
            v.cs = slice(td.t * 128, (td.t + 1) * 128)
            v.sample = td.it.m2 and td.t == 1
            return v

        def stageA(td):
            it, tb = td.it, td.tb
            gb = it.gb
            v = tile_views(td)
            self.tr([(v.psb[:, 0:128], VT[gb][:, v.cs], ident16), (v.psb[:, 128:256], KH[gb][:, v.cs], ident16)],
                    [R(f"VT{gb}"), R(f"KH{gb}"), R("C16")], [R("bank7")])
            self.copy("act", VK[tb][:, :], v.psb, [R("bank7")], [R(f"VK{tb}")])
            if it.main:
                self.mm([(v.AT, KT[gb][:, v.cs], QT[gb][:, v.cs], True, True)], [R(f"KT{gb}"), R(f"QT{gb}")], [v.bk])
                mask = C32[:, C_MS:C_MS + 128] if v.sample else C32[:, C_MC:C_MC + 128]
                self.tt("dve", ATm[tb][:, :], v.AT, mask, ALU.mult, [v.bk, R("C32")], [R(f"ATm{tb}")])

        def stageB(td):
            it, tb = td.it, td.tb
            hd = heads[it.h]
            h, gb = it.h, it.gb
            Sh, rS = hd.Sh, hd.rS
            v = tile_views(td)
            if it.main:
                if not v.sample:
                    self.mm([(v.Op, VK[tb][:, 0:128], ATm[tb][:, :], True, False),
                             (v.Op, Sh, QT32[gb][:, v.cs], False, True)],
                            [R(f"VK{tb}"), R(f"ATm{tb}"), rS, R(f"QT32{gb}")], [v.bk])
                else:
                    s0v = self.S0BF.rearrange("p (s v) -> p s v", s=16)
                    mms = [(v.Op, VK[tb][:, 0:128], ATm[tb][:, :], True, False)]
                    for sq_ in range(16):
                        mms.append((v.Op[:, sq_ * 8:(sq_ + 1) * 8], s0v[:, sq_, :], QT[gb][:, 128 + sq_ * 8:128 + (sq_ + 1) * 8], False, sq_ == 15))
                    self.mm(mms, [R(f"VK{tb}"), R(f"ATm{tb}"), R("Vb0"), R("Vb1"), R("Vb2"), R("Vb3"), R(f"QT{gb}")], [v.bk])
            if not v.sample:
                self.mm([(v.SPp, VK[tb][:, 128:256], VK[tb][:, 0:128], True, True)], [R(f"VK{tb}")], [v.bk])
                self.stt(Sh, Sh, EBL[gb][:, td.t:td.t + 1], v.SPp, ALU.mult, ALU.add, [rS, R(f"EBL{gb}"), v.bk], [rS])
                if it.m2:
                    self.dma("sp", self.sp_out[h], Sh, [rS], [])
            else:
                self.copy("dve", self.VKS[:, :], VK[tb][:, :], [R(f"VK{tb}")], [R("VKS")])
            if it.main:
                self.act(SQ[tb][:, :], v.Op, AF.Square, [v.bk], [R(f"SQ{tb}")])

        def stageC(td):
            it, tb = td.it, td.tb
            if not it.main:
                return
            gb = it.gb
            v = tile_views(td)
            mcol = it.c0 + td.t * 128
            self.mm([(v.SSp, ones16, SQ[tb][:, :], True, True)], [R(f"SQ{tb}"), R("C16")], [v.bk])
            self.act(RS[tb][:, :], v.SSp, AF.Ln, [v.bk], [R(f"RS{tb}")], bias=RMS_EPS, scale=1.0 / 128.0)
            self.act(RS[tb][:, :], RS[tb][:, :], AF.Exp, [R(f"RS{tb}")], [R(f"RS{tb}")], scale=-0.5)
            self.tt("dve", T1[tb][:, :], v.Op, RS[tb][:, :], ALU.mult, [v.bk, R(f"RS{tb}")], [R(f"T1{tb}")])
            self.tt("pool", self.OTa[:, it.h, mcol:mcol + 128], T1[tb][:, :], GATE[gb][:, v.cs], ALU.mult,
                    [R(f"T1{tb}"), R(f"GATE{gb}")], [R(f"OTa_{mcol // 128}")])

        tqueue = []

        def step(td_new):
            if td_new is not None:
                stageA(td_new)
            for td in reversed(tqueue):
                if td.stage == 1:
                    stageB(td)
                    td.stage = 2
                elif td.stage == 2:
                    stageC(td)
                    td.stage = 3
            tqueue[:] = [td for td in tqueue if td.stage < 3]
            if td_new is not None:
                td_new.stage = 1
                tqueue.append(td_new)

        def sample_state(it):
            h, gb = it.h, it.gb
            s0v = S0b[h % 2].rearrange("p (s v) -> p s v", s=16)
            VKs = self.VKS
            holder = {}

            def vb_op(q4):
                def f():
                    vbv = Vb[q4].rearrange("p (s v) -> p s v", s=4)
                    vin = VKs[:, 0:128].unsqueeze(1).broadcast_to([128, 4, 128])
                    sel = C16[:, C_SEL + 4 * q4:C_SEL + 4 * q4 + 4].unsqueeze(2).broadcast_to([128, 4, 128])
                    self.tt("dve", vbv, vin, sel, ALU.mult, [R("VKS"), R("C16")], [R(f"Vb{q4}")])
                return f

            def mm_op(q4):
                def f():
                    b = self.next_bank()
                    holder[q4] = b
                    sps = self.ps32[:, b * 512:(b + 1) * 512]
                    self.mm([(sps, VKs[:, 128:256], Vb[q4], True, True)], [R("VKS"), R(f"Vb{q4}")], [R(f"bank{b}")])
                return f

            def wb_op(q4):
                def f():
                    b = holder[q4]
                    sps = self.ps32[:, b * 512:(b + 1) * 512]
                    for s_ in range(4):
                        sq_ = q4 * 4 + s_
                        self.stt(s0v[:, sq_, :], s0v[:, sq_, :], EBLs[gb][:, sq_:sq_ + 1], sps[:, s_ * 128:(s_ + 1) * 128],
                                 ALU.mult, ALU.add, [R(f"S0b{h % 2}"), R(f"EBLs{gb}"), R(f"bank{b}")], [R(f"S0b{h % 2}")])
                return f
            dst = self.ss_out[:, h].rearrange("s k v -> k s v")
            out_op = lambda: self.dma("sp", dst, s0v, [R(f"S0b{h % 2}")], [])
            chunks = [[vb_op(0), vb_op(1), vb_op(2), vb_op(3)],
                      [mm_op(0), mm_op(1)],
                      [wb_op(0), wb_op(1), mm_op(2), mm_op(3)],
                      [wb_op(2), wb_op(3), out_op]]
            return chunks

        def run(ops):
            for o in ops:
                o()

        load_s0(0)
        setup_head(0)
        proj_item(items[0])
        run(gatingA(items[0]))
        run(gatingB(items[0]))
        deferred = []
        for i, it in enumerate(items):
            nxt = items[i + 1] if i + 1 < len(items) else None
            P, A1, A2, B = [], [], [], []
            if nxt is not None:
                if nxt.gidx == 0:
                    setup_head(nxt.h)
                P = proj_chunks(nxt)
            avail = []
            if it.m2:
                self.copy("dve", self.S0BF[:, :], S0b[it.h % 2][:, :], [R(f"S0b{it.h % 2}")], [R("Vb0"), R("Vb1"), R("Vb2"), R("Vb3")])
            nsteps = it.nt
            T = max(nsteps, len(P))
            for t in range(T):
                if t < len(P):
                    P[t]()
                if t == 0 and nxt is not None:
                    ga = gatingA(nxt)
                    run(ga[:nxt.nA1a])
                if t == 1 and nxt is not None:
                    run(ga[nxt.nA1a:nxt.nA1])
                    A2 = ga[nxt.nA1:]
                if t == 3 and nxt is not None and nxt.main:
                    B = Bfull[2:]
                if t < nsteps:
                    step(mk_tile(it, t))
                if t == 0:
                    avail = [(lambda ch: (lambda: run(ch)))(ch) for ch in deferred] + avail
                    deferred = []
                if t == 1:
                    run(A2)
                    if nxt is not None and nxt.main:
                        Bfull = gatingB(nxt)
                        run(Bfull[:2])
                if t == 3:
                    avail = avail + B
                remaining = max(T - t - 1, 0)
                k = -(-len(avail) // (remaining + 1)) if avail else 0
                run(avail[:k])
                avail = avail[k:]
            run(avail)
            if it.gidx == 0 and it.h + 1 < 16:
                load_s0(it.h + 1)
            if it.m2:
                deferred = sample_state(it)
        step(None)
        step(None)
        for ch in deferred:
            run(ch)
        self.slab_issue_upto(self.sl_used + 8)
        self.auto_prefetch = True
        if DEBUG:
            self.dbg_dump_bf16(self.dbg_ot, self.OTa, [R(f"OTa_{i}") for i in range(10)])

    def dbg_dump_bf16(self, dst, src3, reads):
        self.P.barrier()
        tmp = self.SCR[:, 21800 - 1280:21800]
        for c in range(NCH):
            self.copy("dve", tmp, src3[:, c, :], reads, [self.R("dbgtmp")])
            self.dma("sp", dst[:, c, :], tmp, [self.R("dbgtmp")], [self.R("dbgout")])
        self.P.barrier()

    def wout_ln(self, layer):
        R = self.R
        self.carve_reset()
        ZW = 640
        Z_t = self.c32(NCH * ZW)
        Z = Z_t.rearrange("p (c n) -> p c n", c=NCH)
        ACC = [self.c32(ZW), self.c32(ZW)]
        ACC2 = [self.c32(ZW), self.c32(ZW)]
        MEANb = self.c32(ZW)
        MEAN = [MEANb, MEANb]
        RSTD = [self.c32(ZW), self.c32(ZW)]
        NMR = [self.c32(ZW), self.c32(ZW)]
        M2 = self.c32(512)
        SQT = [self.c32(512) for _ in range(2)]
        XR = [self.c32(512) for _ in range(4)]
        XO = [self.c32(512) for _ in range(4)]
        self.n_ring = 5
        ones32 = self.C32[:, C_ONE:C_ONE + 128]
        if layer == 0:
            passes = [[(0, 512), (512, 128)], [(640, 512), (1152, 128)]]
            sbase = 64
        else:
            passes = [[(128, 512), (640, 128)], [(768, 384), (1152, 128)]]
            sbase = 120
        cnt = {"k": 0, "ko": 0}
        if layer == 0:
            self.dma("sp", self.ks[:, 0:120, :], self.ck[:, 8:128, :], [], [R("ks_copy")])
            self.dma("sp", self.vs[:, 0:120, :], self.cv[:, 8:128, :], [], [R("vs_copy")])

        steps = [(pi, j, c0, n) for pi in range(2) for j in range(NCH) for (c0, n) in passes[pi]]
        xrs = {"issued": 0}

        def issue_xr(upto):
            upto = min(upto, len(steps))
            while xrs["issued"] < upto:
                i = xrs["issued"]
                pi_, j_, c0, n = steps[i]
                xb = i % 4
                xr = XR[xb][:, :n]
                if layer == 0:
                    src = self.xsT[:, j_, :] if c0 >= NMP else self.xT[:, j_, NPRE + c0:NPRE + c0 + n]
                else:
                    src = self.x1T[:, j_, c0:c0 + n]
                if layer == 1:
                    l0_pieces = [(0, 512), (512, 128), (640, 512), (1152, 128)]
                    xdeps = [R(f"x1T_{j_}_{p0_}") for (p0_, pn_) in l0_pieces if p0_ < c0 + n and c0 < p0_ + pn_]
                else:
                    xdeps = []
                self.dma("sp", xr, src, xdeps, [R(f"XR{xb}")])
                xrs["issued"] += 1

        def accumulate(pi, j):
            cgs = passes[pi]
            p0 = cgs[0][0]
            slab, rsl = self.next_slab(sbase + j)
            for (c0, n) in cgs:
                kb = cnt["k"] % 2
                xb = cnt["k"] % 4
                assert steps[cnt["k"]] == (pi, j, c0, n)
                issue_xr(cnt["k"] + 3)
                cnt["k"] += 1
                xr = XR[xb][:, :n]
                y, ry = self.proj(slab, rsl, self.OTa, [R(f"OTa_{i}") for i in range(c0 // 128, (c0 + n) // 128)], c0, n)
                sl = slice(c0 - p0, c0 - p0 + n)
                zs = Z[:, j, sl]
                self.stt(zs, xr, ALPHA, y, ALU.mult, ALU.add, [R(f"XR{xb}"), ry], [R(f"Z{j}_{sl.start}")])
                sqt = SQT[kb][:, :n]
                self.act(sqt, zs, AF.Square, [R(f"Z{j}_{sl.start}")], [R(f"SQT{kb}")])
                a1 = ACC[pi][:, sl]
                a2 = ACC2[pi][:, sl]
                if j == 0:
                    self.copy("dve", a1, zs, [R(f"Z{j}_{sl.start}")], [R(f"ACC{pi}")])
                    self.copy("dve", a2, sqt, [R(f"SQT{kb}")], [R(f"ACC2{pi}")])
                else:
                    self.tt("dve", a1, a1, zs, ALU.add, [R(f"Z{j}_{sl.start}"), R(f"ACC{pi}")], [R(f"ACC{pi}")])
                    self.tt("dve", a2, a2, sqt, ALU.add, [R(f"SQT{kb}"), R(f"ACC2{pi}")], [R(f"ACC2{pi}")])

        def stats(pi):
            cgs = passes[pi]
            p0 = cgs[0][0]
            chains = []
            for (c0, n) in cgs:
                sl = slice(c0 - p0, c0 - p0 + n)
                tot = self.ps32[:, 5 * 512:5 * 512 + n]
                tot2 = self.ps32[:, 6 * 512:6 * 512 + n]
                k = sl.start
                rM, rR, rN = R(f"MEAN_{k}"), R(f"RSTD{pi}_{k}"), R(f"NMR{pi}_{k}")

                def mk(sl=sl, n=n, tot=tot, tot2=tot2, rM=rM, rR=rR, rN=rN):
                    return [
                        lambda: self.mm([(tot, ones32, ACC[pi][:, sl], True, True)], [R(f"ACC{pi}"), R("C32")], [R("bank5")]),
                        lambda: self.mm([(tot2, ones32, ACC2[pi][:, sl], True, True)], [R(f"ACC2{pi}"), R("C32")], [R("bank6")]),
                        lambda: self.act(MEAN[pi][:, sl], tot, AF.Copy, [R("bank5")], [rM], scale=1.0 / D),
                        lambda: self.tt("dve", M2[:, :n], MEAN[pi][:, sl], MEAN[pi][:, sl], ALU.mult, [rM], [R("M2")]),
                        lambda: self.stt(RSTD[pi][:, sl], tot2, 1.0 / D, M2[:, :n], ALU.mult, ALU.subtract, [R("bank6"), R("M2")], [rR]),
                        lambda: self.act(RSTD[pi][:, sl], RSTD[pi][:, sl], AF.Ln, [rR], [rR], bias=LN_EPS),
                        lambda: self.act(RSTD[pi][:, sl], RSTD[pi][:, sl], AF.Exp, [rR], [rR], scale=-0.5),
                        lambda: self.stt(NMR[pi][:, sl], MEAN[pi][:, sl], -1.0, RSTD[pi][:, sl], ALU.mult, ALU.mult, [rM, rR], [rN]),
                    ]
                chains.append(mk())
            SK = 3
            order = []
            n0 = len(chains[0])
            for i in range(n0 + SK):
                if i < n0:
                    order.append(chains[0][i])
                if len(chains) > 1 and 0 <= i - SK < n0:
                    order.append(chains[1][i - SK])
            for f in order:
                f()

        def normalize(pi, j):
            cgs = passes[pi]
            p0 = cgs[0][0]
            gj = self.PAR[:, P_LNG + layer * 16 + j:P_LNG + layer * 16 + j + 1]
            bj = self.PAR[:, P_LNB + layer * 16 + j:P_LNB + layer * 16 + j + 1]
            for (c0, n) in cgs:
                sl = slice(c0 - p0, c0 - p0 + n)
                kb = cnt["ko"] % 4
                cnt["ko"] += 1
                zs = Z[:, j, sl]
                self.tt("dve", zs, zs, RSTD[pi][:, sl], ALU.mult, [R(f"Z{j}_{sl.start}"), R(f"RSTD{pi}_{sl.start}")], [R(f"Z{j}_{sl.start}")])
                self.tt("dve", zs, zs, NMR[pi][:, sl], ALU.add, [R(f"Z{j}_{sl.start}"), R(f"NMR{pi}_{sl.start}")], [R(f"Z{j}_{sl.start}")])
                xo = XO[kb][:, :n]
                self.act(xo, zs, AF.Identity, [R(f"Z{j}_{sl.start}"), R("PAR")], [R(f"XO{kb}")], bias=bj, scale=gj)
                if layer == 0:
                    xw = sorted({min(c0 // 512, 2), min((c0 + n - 1) // 512, 2)})
                    self.act(self.XTm[:, j, c0:c0 + n], zs, AF.Identity, [R(f"Z{j}_{sl.start}"), R("PAR")], [R(f"XTm_{i}") for i in xw], bias=bj, scale=gj)
                    self.dma("act", self.x1T[:, j, c0:c0 + n], xo, [R(f"XO{kb}")], [R(f"x1T_{j}_{c0}")])
                    if DEBUG:
                        self.dma("act", self.dbg_x1[:, j, c0:c0 + n], xo, [R(f"XO{kb}")], [])
                else:
                    self.dma("act", self.yT[:, j, c0 - 128:c0 - 128 + n], xo, [R(f"XO{kb}")], [])

        for j in range(NCH):
            accumulate(0, j)
        stats(0)
        for j in range(NCH):
            normalize(0, j)
            accumulate(1, j)
        stats(1)
        self.slab_issue_upto(self.sl_used + 8)
        for j in range(NCH):
            normalize(1, j)

    def swa(self):
        R = self.R
        self.carve_reset()
        KcT = self.c16(16 * 4 * 128)
        Vc = self.c16(16 * 256)
        Vtok = self.c16(10 * 256)
        KTd = self.c16(4 * NM)
        QT = self.c16(4 * NMP)
        GT = self.c16(4 * NMP)
        PT = [self.c16(512) for _ in range(8)]
        att = {"st_i": 0, "on_i": 0}
        ON = [self.c16(512) for _ in range(2)]
        D2 = [self.c32(512) for _ in range(2)]
        EG = [self.c32(512) for _ in range(2)]
        STG = [self.c32(128) for _ in range(2)]
        KcTv = KcT.rearrange("p (q k s) -> p q k s", q=16, k=4)
        Vcv = Vc.rearrange("p (q d) -> p q d", q=16)
        Vtv = Vtok.rearrange("p (t d) -> p t d", t=10)
        KTdv = KTd.rearrange("p (k n) -> p k n", k=4)
        QTv = QT.rearrange("p (c n) -> p c n", c=4)
        GTv = GT.rearrange("p (c n) -> p c n", c=4)
        C16 = self.C16
        ones16 = C16[:, C_ONE:C_ONE + 128]
        self.n_ring = 3
        ST_BANKS = [(self.ps32[:, 3 * 512:4 * 512], R("bank3")), (self.ps32[:, 4 * 512:5 * 512], R("bank4")), (self.ps16[:, :].bitcast(F32), R("bank7"))]
        self.bank_pos = 0
        XT = self.XTm
        xall = [R("XTm_0"), R("XTm_1"), R("XTm_2")]

        import os
        skip = os.environ.get("KSKIP", "")
        for kk in range(4 if "cache" not in skip else 0):
            src = self.ckT[:, kk].rearrange("q d s -> d q s")
            self.dma("pool", KcTv[0:64, :, kk, :], src, [], [R("KcT")])
            self.dma("pool", KcTv[64:128, :, kk, :], src, [], [R("KcT")])
        if "cache" not in skip:
            self.dma("pool", Vcv, self.cv.rearrange("q s d -> s q d"), [], [R("Vc")])

        if self.stop_after == "swa0":
            return
        stg_i = 0
        for si in range(4):
            slab, rsl = self.next_slab(116 + si)
            isv = si >= 2
            col = (si % 2) * 128
            tiles = range(10) if isv else (8, 9)
            for t in tiles:
                b = self.next_bank()
                out = self.ps32[:, b * 512:b * 512 + 128]
                mms = [(out, XT[:, c, t * 128:(t + 1) * 128], slab[:, c, :], c == 0, c == NCH - 1) for c in range(NCH)]
                self.mm(mms, [rsl] + xall, [R(f"bank{b}")])
                if isv:
                    self.copy("act", Vtv[:, t, col:col + 128], out, [R(f"bank{b}")], [R("Vtok")])
                if t >= 8:
                    sb = stg_i % 2
                    stg_i += 1
                    self.copy("dve", STG[sb][:, :], out, [R(f"bank{b}")], [R(f"STG{sb}")])
                    if t == 8:
                        dst = (self.vp if isv else self.kp)[:, col:col + 128]
                        self.dma("sp", dst, STG[sb][:, :], [R(f"STG{sb}")], [])
                    else:
                        dd = self.vs if isv else self.ks
                        for q in range(16 if "small" not in skip else 0):
                            self.dma("sp", dd[q, 120:128, col:col + 128], STG[sb][q * 8:(q + 1) * 8, :], [R(f"STG{sb}"), R("ks_copy"), R("vs_copy")], [])
        if self.stop_after == "swa1a":
            return
        for kvh in range(4):
            slab, rsl = self.next_slab(112 + kvh)
            for (c0, n) in [(0, 512), (512, 512), (1024, 256)]:
                o, ro = self.proj(slab, rsl, XT, xall, c0, n)
                self.copy("act", KTdv[:, kvh, c0:c0 + n], o, [ro], [R("KTd")])
        if self.stop_after == "swa1b":
            return
        st_i = 0
        on_i = 0
        for kvh in range(4):
            for cc in range(4):
                slab, rsl = self.next_slab(80 + 4 * kvh + cc)
                for (c0, n) in [(128, 512), (640, 512), (1152, 128)]:
                    o, ro = self.proj(slab, rsl, XT, xall, c0, n)
                    self.copy("act", QTv[:, cc, c0 - 128:c0 - 128 + n], o, [ro], [R("QTs")])
            for cc in range(4):
                slab, rsl = self.next_slab(96 + 4 * kvh + cc)
                for gi_, (c0, n) in enumerate([(128, 512), (640, 512), (1152, 128)]):
                    o, ro = self.proj(slab, rsl, XT, xall, c0, n)
                    eg = EG[gi_ % 2][:, :n]
                    rE = R(f"EG{gi_ % 2}")
                    self.act(eg, o, AF.Exp, [ro], [rE], scale=-1.0)
                    self.act(eg, eg, AF.Ln, [rE], [rE], bias=1.0)
                    self.act(eg, eg, AF.Exp, [rE], [rE], scale=-1.0)
                    self.tt("dve", GTv[:, cc, c0 - 128:c0 - 128 + n], o, eg, ALU.mult, [ro, rE], [R("GTs")])
            Ob = self.ps32[:, 5 * 512:6 * 512]
            Db = self.ps32[:, 6 * 512:7 * 512]
            Obv = Ob.rearrange("p (c n) -> p c n", c=4)
            Dbv = Db.rearrange("p (c n) -> p c n", c=4)
            units = [(jt, par) for jt in list(range(1, 9)) + [9] for par in range(2)]

            def S1(jt):
                sample = jt == 9
                qc = (jt - 1) * 128
                st = Item2()
                st.pts = {0: [], 1: []}
                st.ptc = {}
                kts = [jt] if sample else [jt - 1, jt]
                for kt in kts:
                    slots = []
                    mms_all = []
                    for par in range(2):
                        hp = slice(par * 64, par * 64 + 64)
                        i_ = att["st_i"]
                        att["st_i"] += 1
                        pt = PT[i_ % 8]
                        rpt = R(f"PT{i_ % 8}")
                        ST, rST = ST_BANKS[i_ % 3]
                        if sample:
                            rhs = QTv[hp, :, qc:qc + 128].rearrange("p c (q t) -> p q c t", t=8)
                        else:
                            rhs = QTv[hp, :, qc:qc + 128]
                        self.mm([(ST, KTdv[hp, kvh, kt * 128:(kt + 1) * 128], rhs, True, True)],
                                [R("KTd"), R("QTs")], [rST])
                        slots.append((pt, rpt, ST, rST, par))
                    for (pt, rpt, ST, rST, par) in slots:
                        self.act(pt[:, :], ST, AF.Exp, [rST], [rpt], bias=-SHIFT, scale=0.125)
                        if sample:
                            mk = C16[:, C_MS:C_MS + 128].rearrange("p (q t) -> p q t", t=8).unsqueeze(2).broadcast_to([128, 16, 4, 8])
                            ptv = pt.rearrange("p (q c t) -> p q c t", q=16, c=4)
                        else:
                            if kt == jt:
                                mcol = C_MC
                            else:
                                mcol = C_MP1 if jt == 1 else C_MP
                            mk = C16[:, mcol:mcol + 128].unsqueeze(1).broadcast_to([128, 4, 128])
                            ptv = pt.rearrange("p (c n) -> p c n", c=4)
                        self.tt("dve", ptv, ptv, mk, ALU.mult, [rpt, R("C16")], [rpt])
                        st.pts[par].append((pt, rpt, kt))
                if sample:
                    for par in range(2):
                        hp = slice(par * 64, par * 64 + 64)
                        i_ = att["st_i"]
                        att["st_i"] += 1
                        ptc = PT[i_ % 8]
                        rptc = R(f"PT{i_ % 8}")
                        STcb, rSTc = ST_BANKS[i_ % 3]
                        mms = [(STcb[:, q * 32:(q + 1) * 32], KcTv[hp, q, kvh, :], QTv[hp, :, qc + q * 8:qc + (q + 1) * 8], True, True) for q in range(16)]
                        self.mm(mms, [R("KcT"), R("QTs")], [rSTc])
                        self.act(ptc[:, :], STcb, AF.Exp, [rSTc], [rptc], bias=-SHIFT, scale=0.125)
                        ptcv4 = ptc.rearrange("p (q c t) -> p q c t", q=16, c=4)
                        mk = C16[:, C_MCA:C_MCA + 8].unsqueeze(1).unsqueeze(1).broadcast_to([128, 16, 4, 8])
                        self.tt("dve", ptcv4, ptcv4, mk, ALU.mult, [rptc, R("C16")], [rptc])
                        st.ptc[par] = (ptc, rptc)
                return st

            def S2(jt, st):
                sample = jt == 9
                mo = []
                md = []
                rr = [R("Vtok"), R("C16")]
                nk = len(st.pts[0])
                for i in range(nk):
                    for par in range(2):
                        hp = slice(par * 64, par * 64 + 64)
                        pt, rpt, kt = st.pts[par][i]
                        last = (i == nk - 1) and not sample
                        mo.append((Ob[hp, :], Vtv[:, kt, kvh * 64:(kvh + 1) * 64], pt[:, :], i == 0, last))
                        md.append((Db[hp, :], ones16[:, 0:64], pt[:, :], i == 0, last))
                        rr.append(rpt)
                if sample:
                    for q in range(16):
                        for par in range(2):
                            hp = slice(par * 64, par * 64 + 64)
                            ptc, rptc = st.ptc[par]
                            mo.append((Ob[hp, q * 32:(q + 1) * 32], Vcv[:, q, kvh * 64:(kvh + 1) * 64], ptc[:, q * 32:(q + 1) * 32], False, q == 15))
                            md.append((Db[hp, q * 32:(q + 1) * 32], ones16[:, 0:64], ptc[:, q * 32:(q + 1) * 32], False, q == 15))
                    rr += [st.ptc[0][1], st.ptc[1][1], R("Vc")]
                self.mm(mo, rr, [R("bank5")])
                self.mm(md, rr, [R("bank6")])

            def S3(jt):
                sample = jt == 9
                qc = (jt - 1) * 128
                ob_ = att["on_i"] % 2
                att["on_i"] += 1
                d2 = D2[ob_][:, :]
                es = self.ESINK[:, kvh * 4:(kvh + 1) * 4]
                if sample:
                    esb = es.unsqueeze(1).unsqueeze(3).broadcast_to([128, 16, 4, 8])
                    d2v = d2.rearrange("p (q c t) -> p q c t", q=16, c=4)
                    dbv = Db.rearrange("p (q c t) -> p q c t", q=16, c=4)
                else:
                    esb = es.unsqueeze(2).broadcast_to([128, 4, 128])
                    d2v = d2.rearrange("p (c n) -> p c n", c=4)
                    dbv = Dbv
                self.tt("dve", d2v, dbv, esb, ALU.add, [R("bank6"), R("ESINK")], [R(f"D2{ob_}")])
                self.act(d2, d2, AF.Ln, [R(f"D2{ob_}")], [R(f"D2{ob_}")])
                self.act(d2, d2, AF.Exp, [R(f"D2{ob_}")], [R(f"D2{ob_}")], scale=-1.0)
                self.tt("dve", ON[ob_][:, :], Ob, d2, ALU.mult, [R("bank5"), R(f"D2{ob_}")], [R(f"ON{ob_}")])
                mcol0 = 1152 if sample else jt * 128
                if sample:
                    o_out = self.OTa[:, kvh * 4:(kvh + 1) * 4, mcol0:mcol0 + 128].rearrange("p c (q t) -> p q c t", t=8)
                    o_in0 = ON[ob_].rearrange("p (q c t) -> p q c t", q=16, c=4)
                    o_in1 = GTv[:, :, qc:qc + 128].rearrange("p c (q t) -> p q c t", t=8)
                else:
                    o_out = self.OTa[:, kvh * 4:(kvh + 1) * 4, mcol0:mcol0 + 128]
                    o_in0 = ON[ob_].rearrange("p (c n) -> p c n", c=4)
                    o_in1 = GTv[:, :, qc:qc + 128]
                self.tt("pool", o_out, o_in0, o_in1, ALU.mult, [R(f"ON{ob_}"), R("GTs")], [R(f"OTa_{mcol0 // 128}")])

            tiles_ = list(range(1, 9)) + [9]
            cur = S1(tiles_[0])
            for ti_, jt in enumerate(tiles_):
                nxt_st = S1(tiles_[ti_ + 1]) if ti_ + 1 < len(tiles_) else None
                S2(jt, cur)
                S3(jt)
                cur = nxt_st


class Item2:
    pass


def build_nc(stop_after=None):
    nc = bass.Bass("TRN2", target_bir_lowering=False)
    b = Builder(nc, stop_after=stop_after)
    b.build()
    return nc


def _fm(a):
    cols = a.shape[0]
    return np.ascontiguousarray(a.reshape(cols, NCH, 128).transpose(2, 1, 0))


def _slab(w):
    return np.ascontiguousarray(w.reshape(NCH, 128, 128).transpose(1, 0, 2).reshape(128, 2048))


def _consts(half):
    c = np.zeros((128, NCONST), np.float32)
    s = np.arange(128)[:, None]
    t = np.arange(128)[None, :]
    c[:, C_ID:C_ID + 128] = (s == t)
    c[:, C_MC:C_MC + 128] = (s <= t)
    c[:, C_MP:C_MP + 128] = (s > t)
    mp1 = (s > t)
    if half == 0:
        mp1 = mp1 & (s >= 112)
    c[:, C_MP1:C_MP1 + 128] = mp1
    c[:, C_MS:C_MS + 128] = (s // 8 == t // 8) & (s % 8 <= t % 8)
    rp = np.ones(512, np.float32)
    rp[::128] = 0
    c[:, C_RP:C_RP + 512] = rp[None, :]
    rm2 = np.ones(256, np.float32)
    rm2[0] = 0
    rm2[128::8] = 0
    c[:, C_RM2:C_RM2 + 256] = rm2[None, :]
    c[:, C_SEL:C_SEL + 16] = (np.arange(128)[:, None] // 8 == np.arange(16)[None, :])
    c[:, C_MCA:C_MCA + 8] = (np.arange(128)[:, None] >= np.arange(8)[None, :] + 1)
    c[:, C_ONE:C_ONE + 128] = 1.0
    return c


def prepare_inputs(x_prompt, x_sample, state_hgrn, cache_swa_k, cache_swa_v, meta_tokens,
                   hgrn_w_in, hgrn_lb_logits, hgrn_norm_w, hgrn_w_out,
                   swa_w_in, swa_sinks, swa_w_out, ln_g, ln_b):
    f32 = np.float32
    x_prompt = np.asarray(x_prompt, f32)
    x_sample = np.asarray(x_sample, f32)
    wall = np.empty((NSLAB, 128, 2048), f32)
    hw = np.asarray(hgrn_w_in, f32)[0]
    for j in range(64):
        wall[j] = _slab(hw[:, j * 128:(j + 1) * 128])
    ho = np.asarray(hgrn_w_out, f32)[0]
    for j in range(16):
        wall[64 + j] = _slab(ho[:, j * 128:(j + 1) * 128])
    sw = np.asarray(swa_w_in, f32)[0]
    for j in range(16):
        wall[80 + j] = _slab(sw[:, j * 128:(j + 1) * 128])
        wall[96 + j] = _slab(sw[:, 2560 + j * 128:2560 + (j + 1) * 128])
    for kvh in range(4):
        wk = sw[:, 2048 + kvh * 64:2048 + (kvh + 1) * 64]
        wall[112 + kvh] = _slab(np.concatenate([wk, wk], axis=1))
    for i in range(4):
        wall[116 + i] = _slab(sw[:, 2048 + i * 128:2048 + (i + 1) * 128])
    so = np.asarray(swa_w_out, f32)[0]
    for j in range(16):
        wall[120 + j] = _slab(so[:, j * 128:(j + 1) * 128])
    pars = np.zeros((128, NPAR), f32)
    lbl = np.asarray(hgrn_lb_logits, f32)
    for i in range(3):
        pars[:, P_LBL + 16 * i:P_LBL + 16 * i + 16] = lbl[i].reshape(16, 128).T
    pars[:, P_NW:P_NW + 16] = np.asarray(hgrn_norm_w, f32)[0].T
    for l in range(2):
        pars[:, P_LNG + 16 * l:P_LNG + 16 * l + 16] = np.asarray(ln_g, f32)[l].reshape(16, 128).T
        pars[:, P_LNB + 16 * l:P_LNB + 16 * l + 16] = np.asarray(ln_b, f32)[l].reshape(16, 128).T
    sk = np.asarray(swa_sinks, f32)[0]
    pars[0:64, P_SINK:P_SINK + 16] = sk[0::2][None, :]
    pars[64:128, P_SINK:P_SINK + 16] = sk[1::2][None, :]
    meta = np.asarray(meta_tokens, f32)
    st = np.asarray(state_hgrn, f32)[0]
    ck = np.asarray(cache_swa_k, f32)[0].reshape(128, 128, 256)
    cv = np.asarray(cache_swa_v, f32)[0].reshape(128, 128, 256)
    consts = [_consts(0), _consts(1)]
    in_maps = []
    for core in range(8):
        seq, half = core // 2, core % 2
        cols = np.zeros((NX, D), f32)
        if half == 0:
            cols[NPRE + 112:NPRE + 128] = meta
            cols[NPRE + 128:] = x_prompt[seq, 0:1024]
        else:
            cols[112:128] = meta
            cols[128:] = x_prompt[seq]
        sl = slice(16 * core, 16 * core + 16)
        ckc = ck[sl]
        in_maps.append({
            "xT": _fm(cols),
            "xsT": _fm(x_sample[sl].reshape(128, D)),
            "s0": np.ascontiguousarray(st[sl]),
            "ckT": np.ascontiguousarray(ckc.reshape(16, 128, 4, 64).transpose(0, 2, 3, 1)),
            "cv": np.ascontiguousarray(cv[sl]),
            "ck": np.ascontiguousarray(ckc),
            "wall": wall,
            "consts": consts[half],
            "pars": pars,
        })
    return in_maps


_NC_CACHE = {}


def kernel(x_prompt, x_sample, state_hgrn, cache_swa_k, cache_swa_v, meta_tokens,
           hgrn_w_in, hgrn_lb_logits, hgrn_norm_w, hgrn_w_out,
           swa_w_in, swa_sinks, swa_w_out, ln_g, ln_b):
    in_maps = prepare_inputs(x_prompt, x_sample, state_hgrn, cache_swa_k, cache_swa_v, meta_tokens,
                             hgrn_w_in, hgrn_lb_logits, hgrn_norm_w, hgrn_w_out,
                             swa_w_in, swa_sinks, swa_w_out, ln_g, ln_b)
    nc = build_nc()
    res = run_bass_kernel_spmd(nc, in_maps, core_ids=list(range(8)))
    rs = res.results
    f32 = np.float32
    y_prompt = np.empty((4, 2048, D), f32)
    y_sample = np.empty((128, 8, D), f32)
    st_p = np.empty((1, 4, 16, 128, 128), f32)
    st_s = np.empty((1, 128, 16, 128, 128), f32)
    kp = np.empty((1, 4, 128, 4, 64), f32)
    vp = np.empty((1, 4, 128, 4, 64), f32)
    ks = np.empty((1, 128, 128, 4, 64), f32)
    vs = np.empty((1, 128, 128, 4, 64), f32)
    for core in range(8):
        r = rs[core]
        seq, half = core // 2, core % 2
        yT = np.asarray(r["yT"])
        ytok = yT.transpose(2, 1, 0).reshape(NMP, D)
        y_prompt[seq, half * 1024:(half + 1) * 1024] = ytok[0:1024]
        y_sample[16 * core:16 * core + 16] = ytok[1024:1152].reshape(16, 8, D)
        st_s[0, 16 * core:16 * core + 16] = np.asarray(r["ss_out"])
        ks[0, 16 * core:16 * core + 16] = np.asarray(r["ks"]).reshape(16, 128, 4, 64)
        vs[0, 16 * core:16 * core + 16] = np.asarray(r["vs"]).reshape(16, 128, 4, 64)
        if half == 1:
            st_p[0, seq] = np.asarray(r["sp_out"])
            kp[0, seq] = np.asarray(r["kp"]).reshape(128, 4, 64)
            vp[0, seq] = np.asarray(r["vp"]).reshape(128, 4, 64)
    return (y_prompt, y_sample, st_p, st_s, kp, vp, ks, vs)
```

```python
import contextlib
import numpy as np
import concourse.bass as bass
import concourse.mybir as mybir
from concourse.bass_utils import run_bass_kernel_spmd

F32 = mybir.dt.float32
BF16 = mybir.dt.bfloat16
AF = mybir.ActivationFunctionType
ALU = mybir.AluOpType

ENGS = ("pe", "act", "dve", "pool", "sp")

D = 2048
NCH = 16
NPRE = 1024
NMP = 1152
NM = 1280
NX = NPRE + NMP
ALPHA = (2.0 * 2) ** 0.25
LN_EPS = 1e-5
RMS_EPS = 1e-6
SHIFT = 30.0
NSLAB = 136
DEBUG = False

C_ID, C_MC, C_MP, C_MP1, C_MS, C_RP, C_RM2, C_SEL, C_MCA, C_ONE = 0, 128, 256, 384, 512, 640, 1152, 1408, 1424, 1432
NCONST = 1560
P_LBL, P_NW, P_LNG, P_LNB, P_SINK = 0, 48, 64, 96, 128
NPAR = 144


class Res:
    __slots__ = ("name", "writer", "readers")

    def __init__(self, name):
        self.name = name
        self.writer = None
        self.readers = {}


class Prog:
    def __init__(self, nc, n_dma_sems=(16, 8, 12)):
        self.nc = nc
        self.ops = {e: [] for e in ENGS}
        self.count = {e: 0 for e in ENGS}
        self.waited = {e: {} for e in ENGS}
        self.dma_ring = {"sp": [("dsp", i) for i in range(n_dma_sems[0])],
                         "act": [("dact", i) for i in range(n_dma_sems[1])],
                         "pool": [("dpool", i) for i in range(n_dma_sems[2])]}
        self.dma_pos = {"sp": 0, "act": 0, "pool": 0}
        self.dma_total = {}
        for q in self.dma_ring:
            for k in self.dma_ring[q]:
                self.dma_total[k] = 0

    def _collect(self, eng, reads, writes):
        need = {}

        def add(tok):
            if tok is None:
                return
            k, v = tok
            if need.get(k, 0) < v:
                need[k] = v
        for r in reads:
            add(r.writer)
        for w in writes:
            add(w.writer)
            for k, v in w.readers.items():
                add((k, v))
        out = []
        wd = self.waited[eng]
        for k, v in need.items():
            if eng == "pe" and k == "pe":
                continue
            if wd.get(k, 0) >= v:
                continue
            wd[k] = v
            out.append((k, v))
        return out

    def _mark(self, tok, reads, writes):
        k, v = tok
        for r in reads:
            if r.readers.get(k, 0) < v:
                r.readers[k] = v
        for w in writes:
            w.writer = tok
            w.readers = {}

    def op(self, eng, fn, reads=(), writes=()):
        bk = [r for r in reads if r.name.startswith("bank")]
        if bk:
            reads = [r for r in reads if not r.name.startswith("bank")]
            writes = list(writes) + [b for b in bk if b not in writes]
        waits = self._collect(eng, reads, writes)
        self.count[eng] += 1
        tok = (eng, self.count[eng])
        self._mark(tok, reads, writes)
        self.ops[eng].append((waits, fn, (eng, 1)))
        return tok

    def dma(self, q, fn, reads=(), writes=()):
        ring = self.dma_ring[q]
        key = ring[self.dma_pos[q] % len(ring)]
        self.dma_pos[q] += 1
        waits = self._collect(q, reads, writes)
        prev = self.dma_total[key]
        if prev > 0 and self.waited[q].get(key, 0) < prev:
            self.waited[q][key] = prev
            waits.append((key, prev))
        self.dma_total[key] = prev + 16
        tok = (key, prev + 16)
        self._mark(tok, reads, writes)
        self.ops[q].append((waits, fn, (key, 16)))
        return tok

    def barrier(self):
        toks = [(e, self.count[e]) for e in ("pe", "act", "dve", "pool") if self.count[e] > 0]
        toks += [(k, v) for k, v in self.dma_total.items() if v > 0]
        for e in ENGS:
            waits = []
            for k, v in toks:
                if self.waited[e].get(k, 0) >= v:
                    continue
                self.waited[e][k] = v
                waits.append((k, v))
            if waits:
                self.ops[e].append((waits, None, None))

    def emit(self):
        nc = self.nc
        keys = ["pe", "act", "dve", "pool"] + list(self.dma_total)
        with contextlib.ExitStack() as st:
            sems = {}
            for k in keys:
                nm = k if isinstance(k, str) else f"{k[0]}{k[1]}"
                sems[k] = st.enter_context(nc.semaphore("s_" + nm))
            block = st.enter_context(nc.Block())

            def run(engname):
                def body(e):
                    for waits, fn, inc in self.ops[engname]:
                        for k, v in waits:
                            e.wait_ge(sems[k], v)
                        if fn is None:
                            continue
                        ins = fn(e)
                        ins.then_inc(sems[inc[0]], inc[1])
                return body
            block.sync(run("sp"))
            block.tensor(run("pe"))
            block.scalar(run("act"))
            block.vector(run("dve"))
            block.gpsimd(run("pool"))


class Builder:
    def __init__(self, nc, stop_after=None):
        self.nc = nc
        self.P = Prog(nc)
        self.res = {}
        self.stop_after = stop_after

    def R(self, name):
        r = self.res.get(name)
        if r is None:
            r = self.res[name] = Res(name)
        return r

    def act(self, out, in_, func, reads, writes, bias=None, scale=None):
        kw = {}
        if bias is not None:
            kw["bias"] = bias
        if scale is not None:
            kw["scale"] = scale
        self.P.op("act", lambda e: e.activation(out=out, in_=in_, func=func, **kw), reads, writes)

    def tt(self, eng, out, in0, in1, op, reads, writes):
        self.P.op(eng, lambda e: e.tensor_tensor(out=out, in0=in0, in1=in1, op=op), reads, writes)

    def ts(self, eng, out, in0, s1, op0, reads, writes, s2=None, op1=None):
        if op1 is None:
            self.P.op(eng, lambda e: e.tensor_scalar(out=out, in0=in0, scalar1=s1, scalar2=None, op0=op0), reads, writes)
        else:
            self.P.op(eng, lambda e: e.tensor_scalar(out=out, in0=in0, scalar1=s1, scalar2=s2, op0=op0, op1=op1), reads, writes)

    def stt(self, out, in0, scalar, in1, op0, op1, reads, writes):
        self.P.op("dve", lambda e: e.scalar_tensor_tensor(out=out, in0=in0, scalar=scalar, in1=in1, op0=op0, op1=op1), reads, writes)

    def copy(self, eng, out, in_, reads, writes):
        if eng == "act":
            self.P.op("act", lambda e: e.activation(out=out, in_=in_, func=AF.Copy), reads, writes)
        else:
            self.P.op(eng, lambda e: e.tensor_copy(out=out, in_=in_), reads, writes)

    def memset(self, eng, ap, val, writes):
        self.P.op(eng, lambda e: e.memset(ap, val), (), writes)

    def mm(self, mms, reads, writes):
        def fn(e):
            ins = None
            for (o, l, r, s0, s1) in mms:
                ins = e.matmul(o, lhsT=l, rhs=r, start=s0, stop=s1)
            return ins
        self.P.op("pe", fn, reads, writes)

    def tr(self, trs, reads, writes):
        def fn(e):
            ins = None
            for (o, i, ident) in trs:
                ins = e.transpose(out=o, in_=i, identity=ident)
            return ins
        self.P.op("pe", fn, reads, writes)

    def dma(self, q, out, in_, reads, writes):
        self.P.dma(q, lambda e: e.dma_start(out=out, in_=in_), reads, writes)

    def slab_setup(self, schedule):
        self.sched = schedule
        self.sl_issued = 0
        self.sl_used = 0

    def slab_issue_upto(self, n):
        n = min(n, len(self.sched))
        while self.sl_issued < n:
            i = self.sl_issued
            slot = i % 8
            sid = self.sched[i]
            dst = self.RING[:, slot * 2048:(slot + 1) * 2048]
            self.dma("pool", dst, self.wall[sid], [], [self.R(f"slot{slot}")])
            self.sl_issued += 1

    def next_slab(self, sid):
        i = self.sl_used
        assert self.sched[i] == sid, (i, self.sched[i], sid)
        if getattr(self, "auto_prefetch", True):
            self.slab_issue_upto(i + 5)
        self.sl_used += 1
        slot = i % 8
        ap = self.RING[:, slot * 2048:(slot + 1) * 2048].rearrange("p (c e) -> p c e", c=NCH)
        return ap, self.R(f"slot{slot}")

    def next_bank(self):
        b = self.bank_pos % self.n_ring
        self.bank_pos += 1
        return b

    def proj(self, slab, slabres, X, xres, c0, n):
        b = self.next_bank()
        out = self.ps32[:, b * 512:b * 512 + n]
        mms = [(out, slab[:, c, :], X[:, c, c0:c0 + n], c == 0, c == NCH - 1) for c in range(NCH)]
        self.mm(mms, [slabres] + list(xres), [self.R(f"bank{b}")])
        return out, self.R(f"bank{b}")

    def build(self):
        nc = self.nc
        dt_in = lambda name, shape: nc.dram_tensor(name, shape, F32, kind="ExternalInput").ap()
        dt_out = lambda name, shape: nc.dram_tensor(name, shape, F32, kind="ExternalOutput").ap()
        self.xT = dt_in("xT", [128, NCH, NX])
        self.xsT = dt_in("xsT", [128, NCH, 128])
        self.s0 = dt_in("s0", [16, 16, 128, 128])
        self.ckT = dt_in("ckT", [16, 4, 64, 128])
        self.cv = dt_in("cv", [16, 128, 256])
        self.ck = dt_in("ck", [16, 128, 256])
        self.wall = dt_in("wall", [NSLAB, 128, 2048])
        self.consts = dt_in("consts", [128, NCONST])
        self.pars = dt_in("pars", [128, NPAR])
        self.yT = dt_out("yT", [128, NCH, NMP])
        self.sp_out = dt_out("sp_out", [16, 128, 128])
        self.ss_out = dt_out("ss_out", [16, 16, 128, 128])
        self.kp = dt_out("kp", [128, 256])
        self.vp = dt_out("vp", [128, 256])
        self.ks = dt_out("ks", [16, 128, 256])
        self.vs = dt_out("vs", [16, 128, 256])
        self.x1T = nc.dram_tensor("x1T", [128, NCH, NM], F32, kind="Internal").ap()
        if DEBUG:
            self.dbg_x1 = dt_out("dbg_x1", [128, NCH, NM])
            self.dbg_ot = dt_out("dbg_ot", [128, NCH, NM])

        with contextlib.ExitStack() as st:
            E = st.enter_context
            self.XTm_t = E(nc.sbuf_tensor("XTm", [128, NCH * NM], BF16))
            self.OTa_t = E(nc.sbuf_tensor("OTa", [128, NCH * NM], BF16))
            self.RING = E(nc.sbuf_tensor("RING", [128, 8 * 2048], BF16))
            self.C32 = E(nc.sbuf_tensor("C32", [128, NCONST], F32))
            self.C16 = E(nc.sbuf_tensor("C16", [128, NCONST], BF16))
            self.PAR = E(nc.sbuf_tensor("PAR", [128, 256], F32))
            self.SCR = E(nc.sbuf_tensor("SCR", [128, 21800], F32))
            self.ps32 = E(nc.psum_tensor("ps32", [128, 7 * 512], F32))
            self.ps16 = E(nc.psum_tensor("ps16", [128, 1024], BF16))
            self.XTm = self.XTm_t[:].rearrange("p (c n) -> p c n", c=NCH)
            self.OTa = self.OTa_t[:].rearrange("p (c n) -> p c n", c=NCH)
            self.program()
            self.P.barrier()
            self.P.emit()

    def carve_reset(self):
        self.cpos = 0

    def c32(self, n):
        a = self.SCR[:, self.cpos:self.cpos + n]
        self.cpos += n
        assert self.cpos <= 21800, self.cpos
        return a

    def c16(self, n):
        n32 = (n + 1) // 2
        a = self.SCR[:, self.cpos:self.cpos + n32].bitcast(BF16)
        self.cpos += n32
        assert self.cpos <= 21800, self.cpos
        return a

    def program(self):
        R = self.R
        sched = []
        for h in range(16):
            sched += [h, 16 + h, 32 + h, 48 + h]
        sched += [64 + j for j in range(16)] * 2
        sched += [116, 117, 118, 119, 112, 113, 114, 115]
        for kvh in range(4):
            sched += [80 + 4 * kvh + i for i in range(4)] + [96 + 4 * kvh + i for i in range(4)]
        sched += [120 + j for j in range(16)] * 2
        self.slab_setup(sched)

        self.dma("sp", self.C32[:], self.consts, [], [R("C32")])
        self.dma("sp", self.PAR[:, 0:NPAR], self.pars, [], [R("PAR")])
        self.copy("act", self.C16[:], self.C32[:], [R("C32")], [R("C16")])
        self.setup_params()
        self.hgrn()
        if self.stop_after == "hgrn":
            return
        self.P.barrier()
        self.wout_ln(0)
        if self.stop_after == "ln0":
            return
        self.P.barrier()
        self.swa()
        if self.stop_after in ("swa0", "swa1a", "swa1b", "swa"):
            return
        self.P.barrier()
        self.wout_ln(1)

    def setup_params(self):
        R = self.R
        PAR = self.PAR
        l0, l1, l2 = PAR[:, 0:16], PAR[:, 16:32], PAR[:, 32:48]
        mx = PAR[:, 144:160]
        ex = PAR[:, 160:208]
        sm = PAR[:, 208:224]
        self.LB = PAR[:, 224:240]
        self.C1 = PAR[:, 240:256]
        rP = [R("PAR")]
        self.tt("dve", mx, l0, l1, ALU.max, rP, [R("pmx")])
        self.tt("dve", mx, mx, l2, ALU.max, rP + [R("pmx")], [R("pmx")])
        for i in range(3):
            self.tt("dve", ex[:, 16 * i:16 * i + 16], PAR[:, 16 * i:16 * i + 16], mx, ALU.subtract, rP + [R("pmx")], [R(f"pex{i}")])
        self.act(ex, ex, AF.Exp, [R("pex0"), R("pex1"), R("pex2")], [R("pex")])
        self.tt("dve", sm, ex[:, 0:16], ex[:, 16:32], ALU.add, [R("pex")], [R("psm")])
        self.tt("dve", sm, sm, ex[:, 32:48], ALU.add, [R("pex"), R("psm")], [R("psm")])
        self.P.op("dve", lambda e: e.reciprocal(out=sm, in_=sm), [R("psm")], [R("psm")])
        self.tt("dve", self.LB, ex[:, 0:16], sm, ALU.mult, [R("pex"), R("psm")], [R("LB")])
        self.act(self.C1, self.LB, AF.Ln, [R("LB")], [R("C1")], bias=1.0, scale=-1.0)
        self.ESINK = PAR[:, P_SINK:P_SINK + 16]
        self.act(self.ESINK, self.ESINK, AF.Exp, rP, [R("ESINK")], bias=-SHIFT)

    def hgrn(self):
        R = self.R
        self.carve_reset()
        XTp_t = self.c16(NCH * NPRE)
        XTp = XTp_t.rearrange("p (c n) -> p c n", c=NCH)
        QT = [self.c16(512) for _ in range(2)]
        KT = [self.c16(512) for _ in range(2)]
        KH = [self.c16(512) for _ in range(2)]
        VT = [self.c16(512) for _ in range(2)]
        GATE = [self.c16(512) for _ in range(2)]
        VK = [self.c16(256) for _ in range(3)]
        ATm = [self.c16(128) for _ in range(3)]
        SQ = [self.c16(128) for _ in range(3)]
        VbAll = self.c16(2048)
        Vb = [VbAll[:, 512 * i:512 * (i + 1)] for i in range(4)]
        self.S0BF = VbAll
        self.VKS = self.c16(256)
        U, LA, Bb, L1, SG = [self.c32(512) for _ in range(5)]
        EQ = U
        DK = LA
        QT32 = [self.c32(512) for _ in range(2)]
        RS = [self.c32(128) for _ in range(3)]
        T1 = [self.c32(128) for _ in range(3)]
        S = [self.c32(128) for _ in range(2)]
        S0b = [self.c32(2048) for _ in range(2)]
        CB = [self.c32(8) for _ in range(2)]
        EBL = [self.c32(8) for _ in range(2)]
        CBs = [self.c32(16) for _ in range(2)]
        EBLs = [self.c32(16) for _ in range(2)]
        self.n_ring = 4
        self.bank_pos = 0
        self.auto_prefetch = False
        C32, C16 = self.C32, self.C16
        ident16 = C16[:, C_ID:C_ID + 128]
        ones16 = C16[:, C_ONE:C_ONE + 128]

        xgroups = [("pre", 0, 512), ("pre", 512, 512), ("main", 0, 512), ("main", 512, 512), ("main", 1024, 256)]
        self.dma("pool", XTp[:, :, 0:512], self.xT[:, :, 0:512], [], [R("XTp_0")])
        self.slab_issue_upto(4)
        self.dma("pool", XTp[:, :, 512:1024], self.xT[:, :, 512:1024], [], [R("XTp_1")])
        self.dma("pool", self.XTm[:, :, 0:512], self.xT[:, :, NPRE:NPRE + 512], [], [R("XTm_0")])
        self.dma("pool", self.XTm[:, :, 512:1024], self.xT[:, :, NPRE + 512:NPRE + 1024], [], [R("XTm_1")])
        self.dma("pool", self.XTm[:, :, 1024:1152], self.xT[:, :, NPRE + 1024:NPRE + 1152], [], [R("XTm_2")])
        self.dma("pool", self.XTm[:, :, 1152:1280], self.xsT, [], [R("XTm_2")])

        def load_s0(h):
            dst = S0b[h % 2].rearrange("p (s v) -> p s v", s=16)
            src = self.s0[:, h].rearrange("s k v -> k s v")
            self.dma("sp", dst, src, [], [R(f"S0b{h % 2}")])

        class Item:
            pass
        items = []
        for h in range(16):
            for gidx, (kind, c0, n) in enumerate(xgroups):
                it = Item()
                it.h, it.kind, it.c0, it.n, it.gidx = h, kind, c0, n, gidx
                it.main = kind == "main"
                it.m2 = it.main and c0 == 1024
                it.nt = n // 128
                it.gb = len(items) % 2
                items.append(it)
        heads = {}
        tstate = {"ti": 0}

        def setup_head(h):
            hd = Item()
            hd.sq, hd.rq = self.next_slab(h)
            hd.sf, hd.rf = self.next_slab(16 + h)
            hd.si, hd.ri = self.next_slab(32 + h)
            hd.sg, hd.rg = self.next_slab(48 + h)
            self.slab_issue_upto(4 * (h + 2))
            hd.Sh = S[h % 2]
            hd.rS = R(f"S{h % 2}")
            self.memset("dve", hd.Sh, 0.0, [hd.rS])
            hd.lbh = self.LB[:, h:h + 1]
            hd.c1h = self.C1[:, h:h + 1]
            hd.nwh = self.PAR[:, P_NW + h:P_NW + h + 1]
            heads[h] = hd

        rpar = [R("LB"), R("C1"), R("PAR")]

        def proj_chunks(it):
            hd = heads[it.h]
            if it.main:
                X, xres = self.XTm, [R(f"XTm_{it.c0 // 512}")]
            else:
                X, xres = XTp, [R(f"XTp_{it.c0 // 512}")]

            def mk(attr, rattr, slab, rsl):
                def f():
                    o, r = self.proj(slab, rsl, X, xres, it.c0, it.n)
                    setattr(it, attr, o)
                    setattr(it, rattr, r)
                return f
            ch = [mk("pf", "rpf", hd.sf, hd.rf), mk("pi", "rpi", hd.si, hd.ri)]
            if it.main:
                ch += [mk("pq", "rpq", hd.sq, hd.rq), mk("pg", "rpg", hd.sg, hd.rg)]
            return ch

        def proj_item(it):
            for f in proj_chunks(it):
                f()

        def gatingA(it):
            hd = heads[it.h]
            n, gb, m2, nt = it.n, it.gb, it.m2, it.nt
            u, la, bb, l1, dk = U[:, :n], LA[:, :n], Bb[:, :n], L1[:, :n], DK[:, :n]
            ops = []
            ops.append(lambda: self.act(u, it.pf, AF.Exp, [it.rpf], [R("U")]))
            ops.append(lambda: self.act(la, u, AF.Ln, [R("U")] + rpar, [R("LA")], bias=hd.lbh))
            ops.append(lambda: self.act(l1, u, AF.Ln, [R("U")], [R("L1")], bias=1.0))
            ops.append(lambda: self.tt("dve", la, la, l1, ALU.subtract, [R("LA"), R("L1")], [R("LA")]))
            rst = C32[:, C_RM2:C_RM2 + 256] if m2 else C32[:, C_RP:C_RP + n]
            ops.append(lambda: self.P.op("dve", lambda e: e.tensor_tensor_scan(out=bb, data0=rst, data1=la, initial=0.0, op0=ALU.mult, op1=ALU.add),
                                         [R("LA"), R("C32")], [R("B")]))
            it.nA1a = len(ops)
            ops.append(lambda: self.copy("act", VT[gb][:, :n], it.pi, [it.rpi], [R(f"VT{gb}")]))
            ntp = 1 if m2 else nt
            blast = Bb[:, 127:128 * ntp:128]
            it.nA1 = len(ops)
            ops.append(lambda: self.act(EBL[gb][:, 0:ntp], blast, AF.Exp, [R("B")], [R(f"EBL{gb}")]))
            ops.append(lambda: self.ts("dve", CB[gb][:, 0:ntp], blast, hd.c1h, ALU.add, [R("B")] + rpar, [R(f"CB{gb}")]))
            if m2:
                bl_s = Bb[:, 128 + 7:256:8]
                ops.append(lambda: self.act(EBLs[gb][:, :], bl_s, AF.Exp, [R("B")], [R(f"EBLs{gb}")]))
                ops.append(lambda: self.ts("dve", CBs[gb][:, :], bl_s, hd.c1h, ALU.add, [R("B")] + rpar, [R(f"CBs{gb}")]))
            ops.append(lambda: self.tt("dve", l1, bb, l1, ALU.add, [R("B"), R("L1")], [R("L1")]))
            dkv = DK[:, 0:128 * ntp].rearrange("p (t n) -> p t n", n=128)
            zv = L1[:, 0:128 * ntp].rearrange("p (t n) -> p t n", n=128)
            cbv = CB[gb][:, 0:ntp].unsqueeze(2).broadcast_to([128, ntp, 128])
            ops.append(lambda: self.tt("dve", dkv, cbv, zv, ALU.subtract, [R(f"CB{gb}"), R("L1")], [R("LA")]))
            if m2:
                dkv2 = DK[:, 128:256].rearrange("p (s n) -> p s n", n=8)
                zv2 = L1[:, 128:256].rearrange("p (s n) -> p s n", n=8)
                cbv2 = CBs[gb][:, :].unsqueeze(2).broadcast_to([128, 16, 8])
                ops.append(lambda: self.tt("dve", dkv2, cbv2, zv2, ALU.subtract, [R(f"CBs{gb}"), R("L1")], [R("LA")]))
            ops.append(lambda: self.act(KH[gb][:, :n], dk, AF.Exp, [R("LA")], [R(f"KH{gb}")]))
            return ops

        def gatingB(it):
            if not it.main:
                return []
            hd = heads[it.h]
            n, gb = it.n, it.gb
            bb, l1, eq, sgt = Bb[:, :n], L1[:, :n], EQ[:, :n], SG[:, :n]
            ops = []
            ops.append(lambda: self.act(eq, bb, AF.Exp, [R("B")], [R("U")]))
            ops.append(lambda: self.act(KT[gb][:, :n], l1, AF.Exp, [R("L1")] + rpar, [R(f"KT{gb}")], bias=hd.c1h, scale=-1.0))
            ops.append(lambda: self.tt("dve", QT32[gb][:, :n], it.pq, eq, ALU.mult, [it.rpq, R("U")], [R(f"QT32{gb}")]))
            ops.append(lambda: self.tt("dve", QT[gb][:, :n], it.pq, eq, ALU.mult, [it.rpq, R("U")], [R(f"QT{gb}")]))
            ops.append(lambda: self.act(sgt, it.pg, AF.Exp, [it.rpg], [R("SG")], scale=-1.0))
            ops.append(lambda: self.act(sgt, sgt, AF.Ln, [R("SG")], [R("SG")], bias=1.0))
            ops.append(lambda: self.act(sgt, sgt, AF.Exp, [R("SG")], [R("SG")], scale=-1.0))
            ops.append(lambda: self.stt(GATE[gb][:, :n], it.pg, hd.nwh, sgt, ALU.mult, ALU.mult, [it.rpg, R("SG"), R("PAR")], [R(f"GATE{gb}")]))
            return ops

        def mk_tile(it, t):
            td = Item()
            td.it, td.t = it, t
            td.tb = tstate["ti"] % 3
            tstate["ti"] += 1
            td.stage = 0
            return td

        def tile_views(td):
            tb = td.tb
            sm0 = (4 + tb) * 512
            v = Item()
            v.AT = self.ps32[:, sm0:sm0 + 128]
            v.Op = self.ps32[:, sm0 + 128:sm0 + 256]
            v.SSp = self.ps32[:, sm0 + 256:sm0 + 384]
            v.SPp = self.ps32[:, sm0 + 384:sm0 + 512]
            v.psb = self.ps16[:, tb * 256:(tb + 1) * 256]
            v.bk = R(f"bank{4 + tb}")
            v.cs = slice(td.t * 128, (td.t + 1) * 128)
            v.sample = td.it.m2 and td.t == 1
            return v

        def stageA(td):
            it, tb = td.it, td.tb
            gb = it.gb
            v = tile_views(td)
            self.tr([(v.psb[:, 0:128], VT[gb][:, v.cs], ident16), (v.psb[:, 128:256], KH[gb][:, v.cs], ident16)],
                    [R(f"VT{gb}"), R(f"KH{gb}"), R("C16")], [R("bank7")])
            self.copy("act", VK[tb][:, :], v.psb, [R("bank7")], [R(f"VK{tb}")])
            if it.main:
                self.mm([(v.AT, KT[gb][:, v.cs], QT[gb][:, v.cs], True, True)], [R(f"KT{gb}"), R(f"QT{gb}")], [v.bk])
                mask = C32[:, C_MS:C_MS + 128] if v.sample else C32[:, C_MC:C_MC + 128]
                self.tt("dve", ATm[tb][:, :], v.AT, mask, ALU.mult, [v.bk, R("C32")], [R(f"ATm{tb}")])

        def stageB(td):
            it, tb = td.it, td.tb
            hd = heads[it.h]
            h, gb = it.h, it.gb
            Sh, rS = hd.Sh, hd.rS
            v = tile_views(td)
            if it.main:
                if not v.sample:
                    self.mm([(v.Op, VK[tb][:, 0:128], ATm[tb][:, :], True, False),
                             (v.Op, Sh, QT32[gb][:, v.cs], False, True)],
                            [R(f"VK{tb}"), R(f"ATm{tb}"), rS, R(f"QT32{gb}")], [v.bk])
                else:
                    s0v = self.S0BF.rearrange("p (s v) -> p s v", s=16)
                    mms = [(v.Op, VK[tb][:, 0:128], ATm[tb][:, :], True, False)]
                    for sq_ in range(16):
                        mms.append((v.Op[:, sq_ * 8:(sq_ + 1) * 8], s0v[:, sq_, :], QT[gb][:, 128 + sq_ * 8:128 + (sq_ + 1) * 8], False, sq_ == 15))
                    self.mm(mms, [R(f"VK{tb}"), R(f"ATm{tb}"), R("Vb0"), R("Vb1"), R("Vb2"), R("Vb3"), R(f"QT{gb}")], [v.bk])
            if not v.sample:
                self.mm([(v.SPp, VK[tb][:, 128:256], VK[tb][:, 0:128], True, True)], [R(f"VK{tb}")], [v.bk])
                self.stt(Sh, Sh, EBL[gb][:, td.t:td.t + 1], v.SPp, ALU.mult, ALU.add, [rS, R(f"EBL{gb}"), v.bk], [rS])
                if it.m2:
                    self.dma("sp", self.sp_out[h], Sh, [rS], [])
            else:
                self.copy("dve", self.VKS[:, :], VK[tb][:, :], [R(f"VK{tb}")], [R("VKS")])
            if it.main:
                self.act(SQ[tb][:, :], v.Op, AF.Square, [v.bk], [R(f"SQ{tb}")])

        def stageC(td):
            it, tb = td.it, td.tb
            if not it.main:
                return
            gb = it.gb
            v = tile_views(td)
            mcol = it.c0 + td.t * 128
            self.mm([(v.SSp, ones16, SQ[tb][:, :], True, True)], [R(f"SQ{tb}"), R("C16")], [v.bk])
            self.act(RS[tb][:, :], v.SSp, AF.Ln, [v.bk], [R(f"RS{tb}")], bias=RMS_EPS, scale=1.0 / 128.0)
            self.act(RS[tb][:, :], RS[tb][:, :], AF.Exp, [R(f"RS{tb}")], [R(f"RS{tb}")], scale=-0.5)
            self.tt("dve", T1[tb][:, :], v.Op, RS[tb][:, :], ALU.mult, [v.bk, R(f"RS{tb}")], [R(f"T1{tb}")])
            self.tt("pool", self.OTa[:, it.h, mcol:mcol + 128], T1[tb][:, :], GATE[gb][:, v.cs], ALU.mult,
                    [R(f"T1{tb}"), R(f"GATE{gb}")], [R(f"OTa_{mcol // 128}")])

        tqueue = []

        def step(td_new):
            if td_new is not None:
                stageA(td_new)
            for td in reversed(tqueue):
                if td.stage == 1:
                    stageB(td)
                    td.stage = 2
                elif td.stage == 2:
                    stageC(td)
                    td.stage = 3
            tqueue[:] = [td for td in tqueue if td.stage < 3]
            if td_new is not None:
                td_new.stage = 1
                tqueue.append(td_new)

        def sample_state(it):
            h, gb = it.h, it.gb
            s0v = S0b[h % 2].rearrange("p (s v) -> p s v", s=16)
            VKs = self.VKS
            holder = {}

            def vb_op(q4):
                def f():
                    vbv = Vb[q4].rearrange("p (s v) -> p s v", s=4)
                    vin = VKs[:, 0:128].unsqueeze(1).broadcast_to([128, 4, 128])
                    sel = C16[:, C_SEL + 4 * q4:C_SEL + 4 * q4 + 4].unsqueeze(2).broadcast_to([128, 4, 128])
                    self.tt("dve", vbv, vin, sel, ALU.mult, [R("VKS"), R("C16")], [R(f"Vb{q4}")])
                return f

            def mm_op(q4):
                def f():
                    b = self.next_bank()
                    holder[q4] = b
                    sps = self.ps32[:, b * 512:(b + 1) * 512]
                    self.mm([(sps, VKs[:, 128:256], Vb[q4], True, True)], [R("VKS"), R(f"Vb{q4}")], [R(f"bank{b}")])
                return f

            def wb_op(q4):
                def f():
                    b = holder[q4]
                    sps = self.ps32[:, b * 512:(b + 1) * 512]
                    for s_ in range(4):
                        sq_ = q4 * 4 + s_
                        self.stt(s0v[:, sq_, :], s0v[:, sq_, :], EBLs[gb][:, sq_:sq_ + 1], sps[:, s_ * 128:(s_ + 1) * 128],
                                 ALU.mult, ALU.add, [R(f"S0b{h % 2}"), R(f"EBLs{gb}"), R(f"bank{b}")], [R(f"S0b{h % 2}")])
                return f
            dst = self.ss_out[:, h].rearrange("s k v -> k s v")
            out_op = lambda: self.dma("sp", dst, s0v, [R(f"S0b{h % 2}")], [])
            chunks = [[vb_op(0), vb_op(1), vb_op(2), vb_op(3)],
                      [mm_op(0), mm_op(1)],
                      [wb_op(0), wb_op(1), mm_op(2), mm_op(3)],
                      [wb_op(2), wb_op(3), out_op]]
            return chunks

        def run(ops):
            for o in ops:
                o()

        load_s0(0)
        setup_head(0)
        proj_item(items[0])
        run(gatingA(items[0]))
        run(gatingB(items[0]))
        deferred = []
        for i, it in enumerate(items):
            nxt = items[i + 1] if i + 1 < len(items) else None
            P, A1, A2, B = [], [], [], []
            if nxt is not None:
                if nxt.gidx == 0:
                    setup_head(nxt.h)
                P = proj_chunks(nxt)
            avail = []
            if it.m2:
                self.copy("dve", self.S0BF[:, :], S0b[it.h % 2][:, :], [R(f"S0b{it.h % 2}")], [R("Vb0"), R("Vb1"), R("Vb2"), R("Vb3")])
            nsteps = it.nt
            T = max(nsteps, len(P))
            for t in range(T):
                if t < len(P):
                    P[t]()
                if t == 0 and nxt is not None:
                    ga = gatingA(nxt)
                    run(ga[:nxt.nA1a])
                if t == 1 and nxt is not None:
                    run(ga[nxt.nA1a:nxt.nA1])
                    A2 = ga[nxt.nA1:]
                if t == 3 and nxt is not None and nxt.main:
                    B = gatingB(nxt)
                if t < nsteps:
                    step(mk_tile(it, t))
                if t == 0:
                    avail = [(lambda ch: (lambda: run(ch)))(ch) for ch in deferred] + avail
                    deferred = []
                if t == 1:
                    run(A2)
                if t == 3:
                    avail = avail + B
                remaining = max(T - t - 1, 0)
                k = -(-len(avail) // (remaining + 1)) if avail else 0
                run(avail[:k])
                avail = avail[k:]
            run(avail)
            if it.gidx == 0 and it.h + 1 < 16:
                load_s0(it.h + 1)
            if it.m2:
                deferred = sample_state(it)
        step(None)
        step(None)
        for ch in deferred:
            run(ch)
        self.slab_issue_upto(self.sl_used + 8)
        self.auto_prefetch = True
        if DEBUG:
            self.dbg_dump_bf16(self.dbg_ot, self.OTa, [R(f"OTa_{i}") for i in range(10)])

    def dbg_dump_bf16(self, dst, src3, reads):
        self.P.barrier()
        tmp = self.SCR[:, 21800 - 1280:21800]
        for c in range(NCH):
            self.copy("dve", tmp, src3[:, c, :], reads, [self.R("dbgtmp")])
            self.dma("sp", dst[:, c, :], tmp, [self.R("dbgtmp")], [self.R("dbgout")])
        self.P.barrier()

    def wout_ln(self, layer):
        R = self.R
        self.carve_reset()
        ZW = 640
        Z_t = self.c32(NCH * ZW)
        Z = Z_t.rearrange("p (c n) -> p c n", c=NCH)
        ACC = [self.c32(ZW), self.c32(ZW)]
        ACC2 = [self.c32(ZW), self.c32(ZW)]
        MEANb = self.c32(ZW)
        MEAN = [MEANb, MEANb]
        RSTD = [self.c32(ZW), self.c32(ZW)]
        NMR = [self.c32(ZW), self.c32(ZW)]
        M2 = self.c32(512)
        SQT = [self.c32(512) for _ in range(2)]
        XR = [self.c32(512) for _ in range(4)]
        XO = [self.c32(512) for _ in range(4)]
        self.n_ring = 5
        ones32 = self.C32[:, C_ONE:C_ONE + 128]
        if layer == 0:
            passes = [[(0, 512), (512, 128)], [(640, 512), (1152, 128)]]
            sbase = 64
        else:
            passes = [[(128, 512), (640, 128)], [(768, 384), (1152, 128)]]
            sbase = 120
        cnt = {"k": 0, "ko": 0}
        if layer == 0:
            self.dma("sp", self.ks[:, 0:120, :], self.ck[:, 8:128, :], [], [R("ks_copy")])
            self.dma("sp", self.vs[:, 0:120, :], self.cv[:, 8:128, :], [], [R("vs_copy")])

        steps = [(pi, j, c0, n) for pi in range(2) for j in range(NCH) for (c0, n) in passes[pi]]
        xrs = {"issued": 0}

        def issue_xr(upto):
            upto = min(upto, len(steps))
            while xrs["issued"] < upto:
                i = xrs["issued"]
                pi_, j_, c0, n = steps[i]
                xb = i % 4
                xr = XR[xb][:, :n]
                if layer == 0:
                    src = self.xsT[:, j_, :] if c0 >= NMP else self.xT[:, j_, NPRE + c0:NPRE + c0 + n]
                else:
                    src = self.x1T[:, j_, c0:c0 + n]
                if layer == 1:
                    l0_pieces = [(0, 512), (512, 128), (640, 512), (1152, 128)]
                    xdeps = [R(f"x1T_{j_}_{p0_}") for (p0_, pn_) in l0_pieces if p0_ < c0 + n and c0 < p0_ + pn_]
                else:
                    xdeps = []
                self.dma("sp", xr, src, xdeps, [R(f"XR{xb}")])
                xrs["issued"] += 1

        def accumulate(pi, j):
            cgs = passes[pi]
            p0 = cgs[0][0]
            slab, rsl = self.next_slab(sbase + j)
            for (c0, n) in cgs:
                kb = cnt["k"] % 2
                xb = cnt["k"] % 4
                assert steps[cnt["k"]] == (pi, j, c0, n)
                issue_xr(cnt["k"] + 3)
                cnt["k"] += 1
                xr = XR[xb][:, :n]
                y, ry = self.proj(slab, rsl, self.OTa, [R(f"OTa_{i}") for i in range(c0 // 128, (c0 + n) // 128)], c0, n)
                sl = slice(c0 - p0, c0 - p0 + n)
                zs = Z[:, j, sl]
                self.stt(zs, xr, ALPHA, y, ALU.mult, ALU.add, [R(f"XR{xb}"), ry], [R(f"Z{j}_{sl.start}")])
                sqt = SQT[kb][:, :n]
                self.act(sqt, zs, AF.Square, [R(f"Z{j}_{sl.start}")], [R(f"SQT{kb}")])
                a1 = ACC[pi][:, sl]
                a2 = ACC2[pi][:, sl]
                if j == 0:
                    self.copy("dve", a1, zs, [R(f"Z{j}_{sl.start}")], [R(f"ACC{pi}")])
                    self.copy("dve", a2, sqt, [R(f"SQT{kb}")], [R(f"ACC2{pi}")])
                else:
                    self.tt("dve", a1, a1, zs, ALU.add, [R(f"Z{j}_{sl.start}"), R(f"ACC{pi}")], [R(f"ACC{pi}")])
                    self.tt("dve", a2, a2, sqt, ALU.add, [R(f"SQT{kb}"), R(f"ACC2{pi}")], [R(f"ACC2{pi}")])

        def stats(pi):
            cgs = passes[pi]
            p0 = cgs[0][0]
            chains = []
            for (c0, n) in cgs:
                sl = slice(c0 - p0, c0 - p0 + n)
                tot = self.ps32[:, 5 * 512:5 * 512 + n]
                tot2 = self.ps32[:, 6 * 512:6 * 512 + n]
                k = sl.start
                rM, rR, rN = R(f"MEAN_{k}"), R(f"RSTD{pi}_{k}"), R(f"NMR{pi}_{k}")

                def mk(sl=sl, n=n, tot=tot, tot2=tot2, rM=rM, rR=rR, rN=rN):
                    return [
                        lambda: self.mm([(tot, ones32, ACC[pi][:, sl], True, True)], [R(f"ACC{pi}"), R("C32")], [R("bank5")]),
                        lambda: self.mm([(tot2, ones32, ACC2[pi][:, sl], True, True)], [R(f"ACC2{pi}"), R("C32")], [R("bank6")]),
                        lambda: self.act(MEAN[pi][:, sl], tot, AF.Copy, [R("bank5")], [rM], scale=1.0 / D),
                        lambda: self.tt("dve", M2[:, :n], MEAN[pi][:, sl], MEAN[pi][:, sl], ALU.mult, [rM], [R("M2")]),
                        lambda: self.stt(RSTD[pi][:, sl], tot2, 1.0 / D, M2[:, :n], ALU.mult, ALU.subtract, [R("bank6"), R("M2")], [rR]),
                        lambda: self.act(RSTD[pi][:, sl], RSTD[pi][:, sl], AF.Ln, [rR], [rR], bias=LN_EPS),
                        lambda: self.act(RSTD[pi][:, sl], RSTD[pi][:, sl], AF.Exp, [rR], [rR], scale=-0.5),
                        lambda: self.stt(NMR[pi][:, sl], MEAN[pi][:, sl], -1.0, RSTD[pi][:, sl], ALU.mult, ALU.mult, [rM, rR], [rN]),
                    ]
                chains.append(mk())
            SK = 3
            order = []
            n0 = len(chains[0])
            for i in range(n0 + SK):
                if i < n0:
                    order.append(chains[0][i])
                if len(chains) > 1 and 0 <= i - SK < n0:
                    order.append(chains[1][i - SK])
            for f in order:
                f()

        def normalize(pi, j):
            cgs = passes[pi]
            p0 = cgs[0][0]
            gj = self.PAR[:, P_LNG + layer * 16 + j:P_LNG + layer * 16 + j + 1]
            bj = self.PAR[:, P_LNB + layer * 16 + j:P_LNB + layer * 16 + j + 1]
            for (c0, n) in cgs:
                sl = slice(c0 - p0, c0 - p0 + n)
                kb = cnt["ko"] % 4
                cnt["ko"] += 1
                zs = Z[:, j, sl]
                self.tt("dve", zs, zs, RSTD[pi][:, sl], ALU.mult, [R(f"Z{j}_{sl.start}"), R(f"RSTD{pi}_{sl.start}")], [R(f"Z{j}_{sl.start}")])
                self.tt("dve", zs, zs, NMR[pi][:, sl], ALU.add, [R(f"Z{j}_{sl.start}"), R(f"NMR{pi}_{sl.start}")], [R(f"Z{j}_{sl.start}")])
                xo = XO[kb][:, :n]
                self.act(xo, zs, AF.Identity, [R(f"Z{j}_{sl.start}"), R("PAR")], [R(f"XO{kb}")], bias=bj, scale=gj)
                if layer == 0:
                    xw = sorted({min(c0 // 512, 2), min((c0 + n - 1) // 512, 2)})
                    self.copy("act", self.XTm[:, j, c0:c0 + n], xo, [R(f"XO{kb}")], [R(f"XTm_{i}") for i in xw])
                    self.dma("act", self.x1T[:, j, c0:c0 + n], xo, [R(f"XO{kb}")], [R(f"x1T_{j}_{c0}")])
                    if DEBUG:
                        self.dma("act", self.dbg_x1[:, j, c0:c0 + n], xo, [R(f"XO{kb}")], [])
                else:
                    self.dma("act", self.yT[:, j, c0 - 128:c0 - 128 + n], xo, [R(f"XO{kb}")], [])

        for j in range(NCH):
            accumulate(0, j)
        stats(0)
        for j in range(NCH):
            normalize(0, j)
            accumulate(1, j)
        stats(1)
        self.slab_issue_upto(self.sl_used + 8)
        for j in range(NCH):
            normalize(1, j)

    def swa(self):
        R = self.R
        self.carve_reset()
        KcT = self.c16(16 * 4 * 128)
        Vc = self.c16(16 * 256)
        Vtok = self.c16(10 * 256)
        KTd = self.c16(4 * NM)
        QT = self.c16(4 * NMP)
        GT = self.c16(4 * NMP)
        PT = [self.c16(512) for _ in range(8)]
        att = {"st_i": 0, "on_i": 0}
        ON = [self.c16(512) for _ in range(2)]
        D2 = [self.c32(512) for _ in range(2)]
        EG = [self.c32(512) for _ in range(2)]
        STG = [self.c32(128) for _ in range(2)]
        KcTv = KcT.rearrange("p (q k s) -> p q k s", q=16, k=4)
        Vcv = Vc.rearrange("p (q d) -> p q d", q=16)
        Vtv = Vtok.rearrange("p (t d) -> p t d", t=10)
        KTdv = KTd.rearrange("p (k n) -> p k n", k=4)
        QTv = QT.rearrange("p (c n) -> p c n", c=4)
        GTv = GT.rearrange("p (c n) -> p c n", c=4)
        C16 = self.C16
        ones16 = C16[:, C_ONE:C_ONE + 128]
        self.n_ring = 3
        ST_BANKS = [(self.ps32[:, 3 * 512:4 * 512], R("bank3")), (self.ps32[:, 4 * 512:5 * 512], R("bank4")), (self.ps16[:, :].bitcast(F32), R("bank7"))]
        self.bank_pos = 0
        XT = self.XTm
        xall = [R("XTm_0"), R("XTm_1"), R("XTm_2")]

        import os
        skip = os.environ.get("KSKIP", "")
        for kk in range(4 if "cache" not in skip else 0):
            src = self.ckT[:, kk].rearrange("q d s -> d q s")
            self.dma("pool", KcTv[0:64, :, kk, :], src, [], [R("KcT")])
            self.dma("pool", KcTv[64:128, :, kk, :], src, [], [R("KcT")])
        if "cache" not in skip:
            self.dma("pool", Vcv, self.cv.rearrange("q s d -> s q d"), [], [R("Vc")])

        if self.stop_after == "swa0":
            return
        stg_i = 0
        for si in range(4):
            slab, rsl = self.next_slab(116 + si)
            isv = si >= 2
            col = (si % 2) * 128
            tiles = range(10) if isv else (8, 9)
            for t in tiles:
                b = self.next_bank()
                out = self.ps32[:, b * 512:b * 512 + 128]
                mms = [(out, XT[:, c, t * 128:(t + 1) * 128], slab[:, c, :], c == 0, c == NCH - 1) for c in range(NCH)]
                self.mm(mms, [rsl] + xall, [R(f"bank{b}")])
                if isv:
                    self.copy("act", Vtv[:, t, col:col + 128], out, [R(f"bank{b}")], [R("Vtok")])
                if t >= 8:
                    sb = stg_i % 2
                    stg_i += 1
                    self.copy("dve", STG[sb][:, :], out, [R(f"bank{b}")], [R(f"STG{sb}")])
                    if t == 8:
                        dst = (self.vp if isv else self.kp)[:, col:col + 128]
                        self.dma("sp", dst, STG[sb][:, :], [R(f"STG{sb}")], [])
                    else:
                        dd = self.vs if isv else self.ks
                        for q in range(16 if "small" not in skip else 0):
                            self.dma("sp", dd[q, 120:128, col:col + 128], STG[sb][q * 8:(q + 1) * 8, :], [R(f"STG{sb}"), R("ks_copy"), R("vs_copy")], [])
        if self.stop_after == "swa1a":
            return
        for kvh in range(4):
            slab, rsl = self.next_slab(112 + kvh)
            for (c0, n) in [(0, 512), (512, 512), (1024, 256)]:
                o, ro = self.proj(slab, rsl, XT, xall, c0, n)
                self.copy("act", KTdv[:, kvh, c0:c0 + n], o, [ro], [R("KTd")])
        if self.stop_after == "swa1b":
            return
        st_i = 0
        on_i = 0
        for kvh in range(4):
            for cc in range(4):
                slab, rsl = self.next_slab(80 + 4 * kvh + cc)
                for (c0, n) in [(128, 512), (640, 512), (1152, 128)]:
                    o, ro = self.proj(slab, rsl, XT, xall, c0, n)
                    self.copy("act", QTv[:, cc, c0 - 128:c0 - 128 + n], o, [ro], [R("QTs")])
            for cc in range(4):
                slab, rsl = self.next_slab(96 + 4 * kvh + cc)
                for gi_, (c0, n) in enumerate([(128, 512), (640, 512), (1152, 128)]):
                    o, ro = self.proj(slab, rsl, XT, xall, c0, n)
                    eg = EG[gi_ % 2][:, :n]
                    rE = R(f"EG{gi_ % 2}")
                    self.act(eg, o, AF.Exp, [ro], [rE], scale=-1.0)
                    self.act(eg, eg, AF.Ln, [rE], [rE], bias=1.0)
                    self.act(eg, eg, AF.Exp, [rE], [rE], scale=-1.0)
                    self.tt("dve", GTv[:, cc, c0 - 128:c0 - 128 + n], o, eg, ALU.mult, [ro, rE], [R("GTs")])
            Ob = self.ps32[:, 5 * 512:6 * 512]
            Db = self.ps32[:, 6 * 512:7 * 512]
            Obv = Ob.rearrange("p (c n) -> p c n", c=4)
            Dbv = Db.rearrange("p (c n) -> p c n", c=4)
            units = [(jt, par) for jt in list(range(1, 9)) + [9] for par in range(2)]

            def S1(jt):
                sample = jt == 9
                qc = (jt - 1) * 128
                st = Item2()
                st.pts = {0: [], 1: []}
                st.ptc = {}
                kts = [jt] if sample else [jt - 1, jt]
                for kt in kts:
                    slots = []
                    mms_all = []
                    for par in range(2):
                        hp = slice(par * 64, par * 64 + 64)
                        i_ = att["st_i"]
                        att["st_i"] += 1
                        pt = PT[i_ % 8]
                        rpt = R(f"PT{i_ % 8}")
                        ST, rST = ST_BANKS[i_ % 3]
                        if sample:
                            rhs = QTv[hp, :, qc:qc + 128].rearrange("p c (q t) -> p q c t", t=8)
                        else:
                            rhs = QTv[hp, :, qc:qc + 128]
                        self.mm([(ST, KTdv[hp, kvh, kt * 128:(kt + 1) * 128], rhs, True, True)],
                                [R("KTd"), R("QTs")], [rST])
                        slots.append((pt, rpt, ST, rST, par))
                    for (pt, rpt, ST, rST, par) in slots:
                        self.act(pt[:, :], ST, AF.Exp, [rST], [rpt], bias=-SHIFT, scale=0.125)
                        if sample:
                            mk = C16[:, C_MS:C_MS + 128].rearrange("p (q t) -> p q t", t=8).unsqueeze(2).broadcast_to([128, 16, 4, 8])
                            ptv = pt.rearrange("p (q c t) -> p q c t", q=16, c=4)
                        else:
                            if kt == jt:
                                mcol = C_MC
                            else:
                                mcol = C_MP1 if jt == 1 else C_MP
                            mk = C16[:, mcol:mcol + 128].unsqueeze(1).broadcast_to([128, 4, 128])
                            ptv = pt.rearrange("p (c n) -> p c n", c=4)
                        self.tt("dve", ptv, ptv, mk, ALU.mult, [rpt, R("C16")], [rpt])
                        st.pts[par].append((pt, rpt, kt))
                if sample:
                    for par in range(2):
                        hp = slice(par * 64, par * 64 + 64)
                        i_ = att["st_i"]
                        att["st_i"] += 1
                        ptc = PT[i_ % 8]
                        rptc = R(f"PT{i_ % 8}")
                        STcb, rSTc = ST_BANKS[i_ % 3]
                        mms = [(STcb[:, q * 32:(q + 1) * 32], KcTv[hp, q, kvh, :], QTv[hp, :, qc + q * 8:qc + (q + 1) * 8], True, True) for q in range(16)]
                        self.mm(mms, [R("KcT"), R("QTs")], [rSTc])
                        self.act(ptc[:, :], STcb, AF.Exp, [rSTc], [rptc], bias=-SHIFT, scale=0.125)
                        ptcv4 = ptc.rearrange("p (q c t) -> p q c t", q=16, c=4)
                        mk = C16[:, C_MCA:C_MCA + 8].unsqueeze(1).unsqueeze(1).broadcast_to([128, 16, 4, 8])
                        self.tt("dve", ptcv4, ptcv4, mk, ALU.mult, [rptc, R("C16")], [rptc])
                        st.ptc[par] = (ptc, rptc)
                return st

            def S2(jt, st):
                sample = jt == 9
                mo = []
                md = []
                rr = [R("Vtok"), R("C16")]
                nk = len(st.pts[0])
                for i in range(nk):
                    for par in range(2):
                        hp = slice(par * 64, par * 64 + 64)
                        pt, rpt, kt = st.pts[par][i]
                        last = (i == nk - 1) and not sample
                        mo.append((Ob[hp, :], Vtv[:, kt, kvh * 64:(kvh + 1) * 64], pt[:, :], i == 0, last))
                        md.append((Db[hp, :], ones16[:, 0:64], pt[:, :], i == 0, last))
                        rr.append(rpt)
                if sample:
                    for q in range(16):
                        for par in range(2):
                            hp = slice(par * 64, par * 64 + 64)
                            ptc, rptc = st.ptc[par]
                            mo.append((Ob[hp, q * 32:(q + 1) * 32], Vcv[:, q, kvh * 64:(kvh + 1) * 64], ptc[:, q * 32:(q + 1) * 32], False, q == 15))
                            md.append((Db[hp, q * 32:(q + 1) * 32], ones16[:, 0:64], ptc[:, q * 32:(q + 1) * 32], False, q == 15))
                    rr += [st.ptc[0][1], st.ptc[1][1], R("Vc")]
                self.mm(mo, rr, [R("bank5")])
                self.mm(md, rr, [R("bank6")])

            def S3(jt):
                sample = jt == 9
                qc = (jt - 1) * 128
                ob_ = att["on_i"] % 2
                att["on_i"] += 1
                d2 = D2[ob_][:, :]
                es = self.ESINK[:, kvh * 4:(kvh + 1) * 4]
                if sample:
                    esb = es.unsqueeze(1).unsqueeze(3).broadcast_to([128, 16, 4, 8])
                    d2v = d2.rearrange("p (q c t) -> p q c t", q=16, c=4)
                    dbv = Db.rearrange("p (q c t) -> p q c t", q=16, c=4)
                else:
                    esb = es.unsqueeze(2).broadcast_to([128, 4, 128])
                    d2v = d2.rearrange("p (c n) -> p c n", c=4)
                    dbv = Dbv
                self.tt("dve", d2v, dbv, esb, ALU.add, [R("bank6"), R("ESINK")], [R(f"D2{ob_}")])
                self.act(d2, d2, AF.Ln, [R(f"D2{ob_}")], [R(f"D2{ob_}")])
                self.act(d2, d2, AF.Exp, [R(f"D2{ob_}")], [R(f"D2{ob_}")], scale=-1.0)
                self.tt("dve", ON[ob_][:, :], Ob, d2, ALU.mult, [R("bank5"), R(f"D2{ob_}")], [R(f"ON{ob_}")])
                mcol0 = 1152 if sample else jt * 128
                if sample:
                    o_out = self.OTa[:, kvh * 4:(kvh + 1) * 4, mcol0:mcol0 + 128].rearrange("p c (q t) -> p q c t", t=8)
                    o_in0 = ON[ob_].rearrange("p (q c t) -> p q c t", q=16, c=4)
                    o_in1 = GTv[:, :, qc:qc + 128].rearrange("p c (q t) -> p q c t", t=8)
                else:
                    o_out = self.OTa[:, kvh * 4:(kvh + 1) * 4, mcol0:mcol0 + 128]
                    o_in0 = ON[ob_].rearrange("p (c n) -> p c n", c=4)
                    o_in1 = GTv[:, :, qc:qc + 128]
                self.tt("pool", o_out, o_in0, o_in1, ALU.mult, [R(f"ON{ob_}"), R("GTs")], [R(f"OTa_{mcol0 // 128}")])

            tiles_ = list(range(1, 9)) + [9]
            cur = S1(tiles_[0])
            for ti_, jt in enumerate(tiles_):
                nxt_st = S1(tiles_[ti_ + 1]) if ti_ + 1 < len(tiles_) else None
                S2(jt, cur)
                S3(jt)
                cur = nxt_st


class Item2:
    pass


def build_nc(stop_after=None):
    nc = bass.Bass("TRN2", target_bir_lowering=False)
    b = Builder(nc, stop_after=stop_after)
    b.build()
    return nc


def _fm(a):
    cols = a.shape[0]
    return np.ascontiguousarray(a.reshape(cols, NCH, 128).transpose(2, 1, 0))


def _slab(w):
    return np.ascontiguousarray(w.reshape(NCH, 128, 128).transpose(1, 0, 2).reshape(128, 2048))


def _consts(half):
    c = np.zeros((128, NCONST), np.float32)
    s = np.arange(128)[:, None]
    t = np.arange(128)[None, :]
    c[:, C_ID:C_ID + 128] = (s == t)
    c[:, C_MC:C_MC + 128] = (s <= t)
    c[:, C_MP:C_MP + 128] = (s > t)
    mp1 = (s > t)
    if half == 0:
        mp1 = mp1 & (s >= 112)
    c[:, C_MP1:C_MP1 + 128] = mp1
    c[:, C_MS:C_MS + 128] = (s // 8 == t // 8) & (s % 8 <= t % 8)
    rp = np.ones(512, np.float32)
    rp[::128] = 0
    c[:, C_RP:C_RP + 512] = rp[None, :]
    rm2 = np.ones(256, np.float32)
    rm2[0] = 0
    rm2[128::8] = 0
    c[:, C_RM2:C_RM2 + 256] = rm2[None, :]
    c[:, C_SEL:C_SEL + 16] = (np.arange(128)[:, None] // 8 == np.arange(16)[None, :])
    c[:, C_MCA:C_MCA + 8] = (np.arange(128)[:, None] >= np.arange(8)[None, :] + 1)
    c[:, C_ONE:C_ONE + 128] = 1.0
    return c


def prepare_inputs(x_prompt, x_sample, state_hgrn, cache_swa_k, cache_swa_v, meta_tokens,
                   hgrn_w_in, hgrn_lb_logits, hgrn_norm_w, hgrn_w_out,
                   swa_w_in, swa_sinks, swa_w_out, ln_g, ln_b):
    f32 = np.float32
    x_prompt = np.asarray(x_prompt, f32)
    x_sample = np.asarray(x_sample, f32)
    wall = np.empty((NSLAB, 128, 2048), f32)
    hw = np.asarray(hgrn_w_in, f32)[0]
    for j in range(64):
        wall[j] = _slab(hw[:, j * 128:(j + 1) * 128])
    ho = np.asarray(hgrn_w_out, f32)[0]
    for j in range(16):
        wall[64 + j] = _slab(ho[:, j * 128:(j + 1) * 128])
    sw = np.asarray(swa_w_in, f32)[0]
    for j in range(16):
        wall[80 + j] = _slab(sw[:, j * 128:(j + 1) * 128])
        wall[96 + j] = _slab(sw[:, 2560 + j * 128:2560 + (j + 1) * 128])
    for kvh in range(4):
        wk = sw[:, 2048 + kvh * 64:2048 + (kvh + 1) * 64]
        wall[112 + kvh] = _slab(np.concatenate([wk, wk], axis=1))
    for i in range(4):
        wall[116 + i] = _slab(sw[:, 2048 + i * 128:2048 + (i + 1) * 128])
    so = np.asarray(swa_w_out, f32)[0]
    for j in range(16):
        wall[120 + j] = _slab(so[:, j * 128:(j + 1) * 128])
    pars = np.zeros((128, NPAR), f32)
    lbl = np.asarray(hgrn_lb_logits, f32)
    for i in range(3):
        pars[:, P_LBL + 16 * i:P_LBL + 16 * i + 16] = lbl[i].reshape(16, 128).T
    pars[:, P_NW:P_NW + 16] = np.asarray(hgrn_norm_w, f32)[0].T
    for l in range(2):
        pars[:, P_LNG + 16 * l:P_LNG + 16 * l + 16] = np.asarray(ln_g, f32)[l].reshape(16, 128).T
        pars[:, P_LNB + 16 * l:P_LNB + 16 * l + 16] = np.asarray(ln_b, f32)[l].reshape(16, 128).T
    sk = np.asarray(swa_sinks, f32)[0]
    pars[0:64, P_SINK:P_SINK + 16] = sk[0::2][None, :]
    pars[64:128, P_SINK:P_SINK + 16] = sk[1::2][None, :]
    meta = np.asarray(meta_tokens, f32)
    st = np.asarray(state_hgrn, f32)[0]
    ck = np.asarray(cache_swa_k, f32)[0].reshape(128, 128, 256)
    cv = np.asarray(cache_swa_v, f32)[0].reshape(128, 128, 256)
    consts = [_consts(0), _consts(1)]
    in_maps = []
    for core in range(8):
        seq, half = core // 2, core % 2
        cols = np.zeros((NX, D), f32)
        if half == 0:
            cols[NPRE + 112:NPRE + 128] = meta
            cols[NPRE + 128:] = x_prompt[seq, 0:1024]
        else:
            cols[112:128] = meta
            cols[128:] = x_prompt[seq]
        sl = slice(16 * core, 16 * core + 16)
        ckc = ck[sl]
        in_maps.append({
            "xT": _fm(cols),
            "xsT": _fm(x_sample[sl].reshape(128, D)),
            "s0": np.ascontiguousarray(st[sl]),
            "ckT": np.ascontiguousarray(ckc.reshape(16, 128, 4, 64).transpose(0, 2, 3, 1)),
            "cv": np.ascontiguousarray(cv[sl]),
            "ck": np.ascontiguousarray(ckc),
            "wall": wall,
            "consts": consts[half],
            "pars": pars,
        })
    return in_maps


_NC_CACHE = {}


def kernel(x_prompt, x_sample, state_hgrn, cache_swa_k, cache_swa_v, meta_tokens,
           hgrn_w_in, hgrn_lb_logits, hgrn_norm_w, hgrn_w_out,
           swa_w_in, swa_sinks, swa_w_out, ln_g, ln_b):
    in_maps = prepare_inputs(x_prompt, x_sample, state_hgrn, cache_swa_k, cache_swa_v, meta_tokens,
                             hgrn_w_in, hgrn_lb_logits, hgrn_norm_w, hgrn_w_out,
                             swa_w_in, swa_sinks, swa_w_out, ln_g, ln_b)
    nc = build_nc()
    res = run_bass_kernel_spmd(nc, in_maps, core_ids=list(range(8)))
    rs = res.results
    f32 = np.float32
    y_prompt = np.empty((4, 2048, D), f32)
    y_sample = np.empty((128, 8, D), f32)
    st_p = np.empty((1, 4, 16, 128, 128), f32)
    st_s = np.empty((1, 128, 16, 128, 128), f32)
    kp = np.empty((1, 4, 128, 4, 64), f32)
    vp = np.empty((1, 4, 128, 4, 64), f32)
    ks = np.empty((1, 128, 128, 4, 64), f32)
    vs = np.empty((1, 128, 128, 4, 64), f32)
    for core in range(8):
        r = rs[core]
        seq, half = core // 2, core % 2
        yT = np.asarray(r["yT"])
        ytok = yT.transpose(2, 1, 0).reshape(NMP, D)
        y_prompt[seq, half * 1024:(half + 1) * 1024] = ytok[0:1024]
        y_sample[16 * core:16 * core + 16] = ytok[1024:1152].reshape(16, 8, D)
        st_s[0, 16 * core:16 * core + 16] = np.asarray(r["ss_out"])
        ks[0, 16 * core:16 * core + 16] = np.asarray(r["ks"]).reshape(16, 128, 4, 64)
        vs[0, 16 * core:16 * core + 16] = np.asarray(r["vs"]).reshape(16, 128, 4, 64)
        if half == 1:
            st_p[0, seq] = np.asarray(r["sp_out"])
            kp[0, seq] = np.asarray(r["kp"]).reshape(128, 4, 64)
            vp[0, seq] = np.asarray(r["vp"]).reshape(128, 4, 64)
    return (y_prompt, y_sample, st_p, st_s, kp, vp, ks, vs)
```

```python
import contextlib
import numpy as np
import concourse.bass as bass
import concourse.mybir as mybir
from concourse.bass_utils import run_bass_kernel_spmd

F32 = mybir.dt.float32
BF16 = mybir.dt.bfloat16
AF = mybir.ActivationFunctionType
ALU = mybir.AluOpType

ENGS = ("pe", "act", "dve", "pool", "sp")

D = 2048
NCH = 16
NPRE = 1024
NMP = 1152
NM = 1280
NX = NPRE + NMP
ALPHA = (2.0 * 2) ** 0.25
LN_EPS = 1e-5
RMS_EPS = 1e-6
SHIFT = 30.0
NSLAB = 136
DEBUG = False

C_ID, C_MC, C_MP, C_MP1, C_MS, C_RP, C_RM2, C_SEL, C_MCA, C_ONE = 0, 128, 256, 384, 512, 640, 1152, 1408, 1424, 1432
NCONST = 1560
P_LBL, P_NW, P_LNG, P_LNB, P_SINK = 0, 48, 64, 96, 128
NPAR = 144


class Res:
    __slots__ = ("name", "writer", "readers")

    def __init__(self, name):
        self.name = name
        self.writer = None
        self.readers = {}


class Prog:
    def __init__(self, nc, n_dma_sems=(16, 8, 12)):
        self.nc = nc
        self.ops = {e: [] for e in ENGS}
        self.count = {e: 0 for e in ENGS}
        self.waited = {e: {} for e in ENGS}
        self.dma_ring = {"sp": [("dsp", i) for i in range(n_dma_sems[0])],
                         "act": [("dact", i) for i in range(n_dma_sems[1])],
                         "pool": [("dpool", i) for i in range(n_dma_sems[2])]}
        self.dma_pos = {"sp": 0, "act": 0, "pool": 0}
        self.dma_total = {}
        for q in self.dma_ring:
            for k in self.dma_ring[q]:
                self.dma_total[k] = 0

    def _collect(self, eng, reads, writes):
        need = {}

        def add(tok):
            if tok is None:
                return
            k, v = tok
            if need.get(k, 0) < v:
                need[k] = v
        for r in reads:
            add(r.writer)
        for w in writes:
            add(w.writer)
            for k, v in w.readers.items():
                add((k, v))
        out = []
        wd = self.waited[eng]
        for k, v in need.items():
            if eng == "pe" and k == "pe":
                continue
            if wd.get(k, 0) >= v:
                continue
            wd[k] = v
            out.append((k, v))
        return out

    def _mark(self, tok, reads, writes):
        k, v = tok
        for r in reads:
            if r.readers.get(k, 0) < v:
                r.readers[k] = v
        for w in writes:
            w.writer = tok
            w.readers = {}

    def op(self, eng, fn, reads=(), writes=()):
        bk = [r for r in reads if r.name.startswith("bank")]
        if bk:
            reads = [r for r in reads if not r.name.startswith("bank")]
            writes = list(writes) + [b for b in bk if b not in writes]
        waits = self._collect(eng, reads, writes)
        self.count[eng] += 1
        tok = (eng, self.count[eng])
        self._mark(tok, reads, writes)
        self.ops[eng].append((waits, fn, (eng, 1)))
        return tok

    def dma(self, q, fn, reads=(), writes=()):
        ring = self.dma_ring[q]
        key = ring[self.dma_pos[q] % len(ring)]
        self.dma_pos[q] += 1
        waits = self._collect(q, reads, writes)
        prev = self.dma_total[key]
        if prev > 0 and self.waited[q].get(key, 0) < prev:
            self.waited[q][key] = prev
            waits.append((key, prev))
        self.dma_total[key] = prev + 16
        tok = (key, prev + 16)
        self._mark(tok, reads, writes)
        self.ops[q].append((waits, fn, (key, 16)))
        return tok

    def barrier(self):
        toks = [(e, self.count[e]) for e in ("pe", "act", "dve", "pool") if self.count[e] > 0]
        toks += [(k, v) for k, v in self.dma_total.items() if v > 0]
        for e in ENGS:
            waits = []
            for k, v in toks:
                if self.waited[e].get(k, 0) >= v:
                    continue
                self.waited[e][k] = v
                waits.append((k, v))
            if waits:
                self.ops[e].append((waits, None, None))

    def emit(self):
        nc = self.nc
        keys = ["pe", "act", "dve", "pool"] + list(self.dma_total)
        with contextlib.ExitStack() as st:
            sems = {}
            for k in keys:
                nm = k if isinstance(k, str) else f"{k[0]}{k[1]}"
                sems[k] = st.enter_context(nc.semaphore("s_" + nm))
            block = st.enter_context(nc.Block())

            def run(engname):
                def body(e):
                    for waits, fn, inc in self.ops[engname]:
                        for k, v in waits:
                            e.wait_ge(sems[k], v)
                        if fn is None:
                            continue
                        ins = fn(e)
                        ins.then_inc(sems[inc[0]], inc[1])
                return body
            block.sync(run("sp"))
            block.tensor(run("pe"))
            block.scalar(run("act"))
            block.vector(run("dve"))
            block.gpsimd(run("pool"))


class Builder:
    def __init__(self, nc, stop_after=None):
        self.nc = nc
        self.P = Prog(nc)
        self.res = {}
        self.stop_after = stop_after

    def R(self, name):
        r = self.res.get(name)
        if r is None:
            r = self.res[name] = Res(name)
        return r

    def act(self, out, in_, func, reads, writes, bias=None, scale=None):
        kw = {}
        if bias is not None:
            kw["bias"] = bias
        if scale is not None:
            kw["scale"] = scale
        self.P.op("act", lambda e: e.activation(out=out, in_=in_, func=func, **kw), reads, writes)

    def tt(self, eng, out, in0, in1, op, reads, writes):
        self.P.op(eng, lambda e: e.tensor_tensor(out=out, in0=in0, in1=in1, op=op), reads, writes)

    def ts(self, eng, out, in0, s1, op0, reads, writes, s2=None, op1=None):
        if op1 is None:
            self.P.op(eng, lambda e: e.tensor_scalar(out=out, in0=in0, scalar1=s1, scalar2=None, op0=op0), reads, writes)
        else:
            self.P.op(eng, lambda e: e.tensor_scalar(out=out, in0=in0, scalar1=s1, scalar2=s2, op0=op0, op1=op1), reads, writes)

    def stt(self, out, in0, scalar, in1, op0, op1, reads, writes):
        self.P.op("dve", lambda e: e.scalar_tensor_tensor(out=out, in0=in0, scalar=scalar, in1=in1, op0=op0, op1=op1), reads, writes)

    def copy(self, eng, out, in_, reads, writes):
        if eng == "act":
            self.P.op("act", lambda e: e.activation(out=out, in_=in_, func=AF.Copy), reads, writes)
        else:
            self.P.op(eng, lambda e: e.tensor_copy(out=out, in_=in_), reads, writes)

    def memset(self, eng, ap, val, writes):
        self.P.op(eng, lambda e: e.memset(ap, val), (), writes)

    def mm(self, mms, reads, writes):
        def fn(e):
            ins = None
            for (o, l, r, s0, s1) in mms:
                ins = e.matmul(o, lhsT=l, rhs=r, start=s0, stop=s1)
            return ins
        self.P.op("pe", fn, reads, writes)

    def tr(self, trs, reads, writes):
        def fn(e):
            ins = None
            for (o, i, ident) in trs:
                ins = e.transpose(out=o, in_=i, identity=ident)
            return ins
        self.P.op("pe", fn, reads, writes)

    def dma(self, q, out, in_, reads, writes):
        self.P.dma(q, lambda e: e.dma_start(out=out, in_=in_), reads, writes)

    def slab_setup(self, schedule):
        self.sched = schedule
        self.sl_issued = 0
        self.sl_used = 0

    def slab_issue_upto(self, n):
        n = min(n, len(self.sched))
        while self.sl_issued < n:
            i = self.sl_issued
            slot = i % 8
            sid = self.sched[i]
            dst = self.RING[:, slot * 2048:(slot + 1) * 2048]
            self.dma("pool", dst, self.wall[sid], [], [self.R(f"slot{slot}")])
            self.sl_issued += 1

    def next_slab(self, sid):
        i = self.sl_used
        assert self.sched[i] == sid, (i, self.sched[i], sid)
        if getattr(self, "auto_prefetch", True):
            self.slab_issue_upto(i + 7)
        self.sl_used += 1
        slot = i % 8
        ap = self.RING[:, slot * 2048:(slot + 1) * 2048].rearrange("p (c e) -> p c e", c=NCH)
        return ap, self.R(f"slot{slot}")

    def next_bank(self):
        b = self.bank_pos % self.n_ring
        self.bank_pos += 1
        return b

    def proj(self, slab, slabres, X, xres, c0, n):
        b = self.next_bank()
        out = self.ps32[:, b * 512:b * 512 + n]
        mms = [(out, slab[:, c, :], X[:, c, c0:c0 + n], c == 0, c == NCH - 1) for c in range(NCH)]
        self.mm(mms, [slabres] + list(xres), [self.R(f"bank{b}")])
        return out, self.R(f"bank{b}")

    def build(self):
        nc = self.nc
        dt_in = lambda name, shape: nc.dram_tensor(name, shape, F32, kind="ExternalInput").ap()
        dt_out = lambda name, shape: nc.dram_tensor(name, shape, F32, kind="ExternalOutput").ap()
        self.xT = dt_in("xT", [128, NCH, NX])
        self.xsT = dt_in("xsT", [128, NCH, 128])
        self.s0 = dt_in("s0", [16, 16, 128, 128])
        self.ckT = dt_in("ckT", [16, 4, 64, 128])
        self.cv = dt_in("cv", [16, 128, 256])
        self.ck = dt_in("ck", [16, 128, 256])
        self.wall = dt_in("wall", [NSLAB, 128, 2048])
        self.consts = dt_in("consts", [128, NCONST])
        self.pars = dt_in("pars", [128, NPAR])
        self.yT = dt_out("yT", [128, NCH, NMP])
        self.sp_out = dt_out("sp_out", [16, 128, 128])
        self.ss_out = dt_out("ss_out", [16, 16, 128, 128])
        self.kp = dt_out("kp", [128, 256])
        self.vp = dt_out("vp", [128, 256])
        self.ks = dt_out("ks", [16, 128, 256])
        self.vs = dt_out("vs", [16, 128, 256])
        self.x1T = nc.dram_tensor("x1T", [128, NCH, NM], F32, kind="Internal").ap()
        if DEBUG:
            self.dbg_x1 = dt_out("dbg_x1", [128, NCH, NM])
            self.dbg_ot = dt_out("dbg_ot", [128, NCH, NM])

        with contextlib.ExitStack() as st:
            E = st.enter_context
            self.XTm_t = E(nc.sbuf_tensor("XTm", [128, NCH * NM], BF16))
            self.OTa_t = E(nc.sbuf_tensor("OTa", [128, NCH * NM], BF16))
            self.RING = E(nc.sbuf_tensor("RING", [128, 8 * 2048], BF16))
            self.C32 = E(nc.sbuf_tensor("C32", [128, NCONST], F32))
            self.C16 = E(nc.sbuf_tensor("C16", [128, NCONST], BF16))
            self.PAR = E(nc.sbuf_tensor("PAR", [128, 256], F32))
            self.SCR = E(nc.sbuf_tensor("SCR", [128, 21800], F32))
            self.ps32 = E(nc.psum_tensor("ps32", [128, 7 * 512], F32))
            self.ps16 = E(nc.psum_tensor("ps16", [128, 1024], BF16))
            self.XTm = self.XTm_t[:].rearrange("p (c n) -> p c n", c=NCH)
            self.OTa = self.OTa_t[:].rearrange("p (c n) -> p c n", c=NCH)
            self.program()
            self.P.barrier()
            self.P.emit()

    def carve_reset(self):
        self.cpos = 0

    def c32(self, n):
        a = self.SCR[:, self.cpos:self.cpos + n]
        self.cpos += n
        assert self.cpos <= 21800, self.cpos
        return a

    def c16(self, n):
        n32 = (n + 1) // 2
        a = self.SCR[:, self.cpos:self.cpos + n32].bitcast(BF16)
        self.cpos += n32
        assert self.cpos <= 21800, self.cpos
        return a

    def program(self):
        R = self.R
        sched = []
        for h in range(16):
            sched += [h, 16 + h, 32 + h, 48 + h]
        sched += [64 + j for j in range(16)] * 2
        sched += [116, 117, 118, 119, 112, 113, 114, 115]
        for kvh in range(4):
            sched += [80 + 4 * kvh + i for i in range(4)] + [96 + 4 * kvh + i for i in range(4)]
        sched += [120 + j for j in range(16)] * 2
        self.slab_setup(sched)

        self.dma("sp", self.C32[:], self.consts, [], [R("C32")])
        self.dma("sp", self.PAR[:, 0:NPAR], self.pars, [], [R("PAR")])
        self.copy("act", self.C16[:], self.C32[:], [R("C32")], [R("C16")])
        self.setup_params()
        self.hgrn()
        if self.stop_after == "hgrn":
            return
        self.P.barrier()
        self.wout_ln(0)
        if self.stop_after == "ln0":
            return
        self.P.barrier()
        self.swa()
        if self.stop_after in ("swa0", "swa1a", "swa1b", "swa"):
            return
        self.P.barrier()
        self.wout_ln(1)

    def setup_params(self):
        R = self.R
        PAR = self.PAR
        l0, l1, l2 = PAR[:, 0:16], PAR[:, 16:32], PAR[:, 32:48]
        mx = PAR[:, 144:160]
        ex = PAR[:, 160:208]
        sm = PAR[:, 208:224]
        self.LB = PAR[:, 224:240]
        self.C1 = PAR[:, 240:256]
        rP = [R("PAR")]
        self.tt("dve", mx, l0, l1, ALU.max, rP, [R("pmx")])
        self.tt("dve", mx, mx, l2, ALU.max, rP + [R("pmx")], [R("pmx")])
        for i in range(3):
            self.tt("dve", ex[:, 16 * i:16 * i + 16], PAR[:, 16 * i:16 * i + 16], mx, ALU.subtract, rP + [R("pmx")], [R(f"pex{i}")])
        self.act(ex, ex, AF.Exp, [R("pex0"), R("pex1"), R("pex2")], [R("pex")])
        self.tt("dve", sm, ex[:, 0:16], ex[:, 16:32], ALU.add, [R("pex")], [R("psm")])
        self.tt("dve", sm, sm, ex[:, 32:48], ALU.add, [R("pex"), R("psm")], [R("psm")])
        self.P.op("dve", lambda e: e.reciprocal(out=sm, in_=sm), [R("psm")], [R("psm")])
        self.tt("dve", self.LB, ex[:, 0:16], sm, ALU.mult, [R("pex"), R("psm")], [R("LB")])
        self.act(self.C1, self.LB, AF.Ln, [R("LB")], [R("C1")], bias=1.0, scale=-1.0)
        self.ESINK = PAR[:, P_SINK:P_SINK + 16]
        self.act(self.ESINK, self.ESINK, AF.Exp, rP, [R("ESINK")], bias=-SHIFT)

    def hgrn(self):
        R = self.R
        self.carve_reset()
        XTp_t = self.c16(NCH * NPRE)
        XTp = XTp_t.rearrange("p (c n) -> p c n", c=NCH)
        QT = [self.c16(512) for _ in range(2)]
        KT = [self.c16(512) for _ in range(2)]
        KH = [self.c16(512) for _ in range(2)]
        VT = [self.c16(512) for _ in range(2)]
        GATE = [self.c16(512) for _ in range(2)]
        VK = [self.c16(256) for _ in range(3)]
        ATm = [self.c16(128) for _ in range(3)]
        SQ = [self.c16(128) for _ in range(3)]
        VbAll = self.c16(2048)
        Vb = [VbAll[:, 512 * i:512 * (i + 1)] for i in range(4)]
        self.S0BF = VbAll
        self.VKS = self.c16(256)
        U, LA, Bb, L1, SG = [self.c32(512) for _ in range(5)]
        EQ = U
        DK = LA
        QT32 = [self.c32(512) for _ in range(2)]
        RS = [self.c32(128) for _ in range(3)]
        T1 = [self.c32(128) for _ in range(3)]
        S = [self.c32(128) for _ in range(2)]
        S0b = [self.c32(2048) for _ in range(2)]
        CB = [self.c32(8) for _ in range(2)]
        EBL = [self.c32(8) for _ in range(2)]
        CBs = [self.c32(16) for _ in range(2)]
        EBLs = [self.c32(16) for _ in range(2)]
        self.n_ring = 4
        self.bank_pos = 0
        self.auto_prefetch = False
        C32, C16 = self.C32, self.C16
        ident16 = C16[:, C_ID:C_ID + 128]
        ones16 = C16[:, C_ONE:C_ONE + 128]

        xgroups = [("pre", 0, 512), ("pre", 512, 512), ("main", 0, 512), ("main", 512, 512), ("main", 1024, 256)]
        self.dma("pool", XTp[:, :, 0:512], self.xT[:, :, 0:512], [], [R("XTp_0")])
        self.slab_issue_upto(4)
        self.dma("pool", XTp[:, :, 512:1024], self.xT[:, :, 512:1024], [], [R("XTp_1")])
        self.dma("pool", self.XTm[:, :, 0:512], self.xT[:, :, NPRE:NPRE + 512], [], [R("XTm_0")])
        self.dma("pool", self.XTm[:, :, 512:1024], self.xT[:, :, NPRE + 512:NPRE + 1024], [], [R("XTm_1")])
        self.dma("pool", self.XTm[:, :, 1024:1152], self.xT[:, :, NPRE + 1024:NPRE + 1152], [], [R("XTm_2")])
        self.dma("pool", self.XTm[:, :, 1152:1280], self.xsT, [], [R("XTm_2")])

        def load_s0(h):
            dst = S0b[h % 2].rearrange("p (s v) -> p s v", s=16)
            src = self.s0[:, h].rearrange("s k v -> k s v")
            self.dma("sp", dst, src, [], [R(f"S0b{h % 2}")])

        class Item:
            pass
        items = []
        for h in range(16):
            for gidx, (kind, c0, n) in enumerate(xgroups):
                it = Item()
                it.h, it.kind, it.c0, it.n, it.gidx = h, kind, c0, n, gidx
                it.main = kind == "main"
                it.m2 = it.main and c0 == 1024
                it.nt = n // 128
                it.gb = len(items) % 2
                items.append(it)
        heads = {}
        tstate = {"ti": 0}

        def setup_head(h):
            hd = Item()
            hd.sq, hd.rq = self.next_slab(h)
            hd.sf, hd.rf = self.next_slab(16 + h)
            hd.si, hd.ri = self.next_slab(32 + h)
            hd.sg, hd.rg = self.next_slab(48 + h)
            self.slab_issue_upto(4 * (h + 2))
            hd.Sh = S[h % 2]
            hd.rS = R(f"S{h % 2}")
            self.memset("dve", hd.Sh, 0.0, [hd.rS])
            hd.lbh = self.LB[:, h:h + 1]
            hd.c1h = self.C1[:, h:h + 1]
            hd.nwh = self.PAR[:, P_NW + h:P_NW + h + 1]
            heads[h] = hd

        rpar = [R("LB"), R("C1"), R("PAR")]

        def proj_chunks(it):
            hd = heads[it.h]
            if it.main:
                X, xres = self.XTm, [R(f"XTm_{it.c0 // 512}")]
            else:
                X, xres = XTp, [R(f"XTp_{it.c0 // 512}")]

            def mk(attr, rattr, slab, rsl):
                def f():
                    o, r = self.proj(slab, rsl, X, xres, it.c0, it.n)
                    setattr(it, attr, o)
                    setattr(it, rattr, r)
                return f
            ch = [mk("pf", "rpf", hd.sf, hd.rf), mk("pi", "rpi", hd.si, hd.ri)]
            if it.main:
                ch += [mk("pq", "rpq", hd.sq, hd.rq), mk("pg", "rpg", hd.sg, hd.rg)]
            return ch

        def proj_item(it):
            for f in proj_chunks(it):
                f()

        def gatingA(it):
            hd = heads[it.h]
            n, gb, m2, nt = it.n, it.gb, it.m2, it.nt
            u, la, bb, l1, dk = U[:, :n], LA[:, :n], Bb[:, :n], L1[:, :n], DK[:, :n]
            ops = []
            ops.append(lambda: self.act(u, it.pf, AF.Exp, [it.rpf], [R("U")]))
            ops.append(lambda: self.act(la, u, AF.Ln, [R("U")] + rpar, [R("LA")], bias=hd.lbh))
            ops.append(lambda: self.act(l1, u, AF.Ln, [R("U")], [R("L1")], bias=1.0))
            ops.append(lambda: self.tt("dve", la, la, l1, ALU.subtract, [R("LA"), R("L1")], [R("LA")]))
            rst = C32[:, C_RM2:C_RM2 + 256] if m2 else C32[:, C_RP:C_RP + n]
            ops.append(lambda: self.P.op("dve", lambda e: e.tensor_tensor_scan(out=bb, data0=rst, data1=la, initial=0.0, op0=ALU.mult, op1=ALU.add),
                                         [R("LA"), R("C32")], [R("B")]))
            it.nA1a = len(ops)
            ops.append(lambda: self.copy("act", VT[gb][:, :n], it.pi, [it.rpi], [R(f"VT{gb}")]))
            ntp = 1 if m2 else nt
            blast = Bb[:, 127:128 * ntp:128]
            it.nA1 = len(ops)
            ops.append(lambda: self.act(EBL[gb][:, 0:ntp], blast, AF.Exp, [R("B")], [R(f"EBL{gb}")]))
            ops.append(lambda: self.ts("dve", CB[gb][:, 0:ntp], blast, hd.c1h, ALU.add, [R("B")] + rpar, [R(f"CB{gb}")]))
            if m2:
                bl_s = Bb[:, 128 + 7:256:8]
                ops.append(lambda: self.act(EBLs[gb][:, :], bl_s, AF.Exp, [R("B")], [R(f"EBLs{gb}")]))
                ops.append(lambda: self.ts("dve", CBs[gb][:, :], bl_s, hd.c1h, ALU.add, [R("B")] + rpar, [R(f"CBs{gb}")]))
            ops.append(lambda: self.tt("dve", l1, bb, l1, ALU.add, [R("B"), R("L1")], [R("L1")]))
            dkv = DK[:, 0:128 * ntp].rearrange("p (t n) -> p t n", n=128)
            zv = L1[:, 0:128 * ntp].rearrange("p (t n) -> p t n", n=128)
            cbv = CB[gb][:, 0:ntp].unsqueeze(2).broadcast_to([128, ntp, 128])
            ops.append(lambda: self.tt("dve", dkv, cbv, zv, ALU.subtract, [R(f"CB{gb}"), R("L1")], [R("LA")]))
            if m2:
                dkv2 = DK[:, 128:256].rearrange("p (s n) -> p s n", n=8)
                zv2 = L1[:, 128:256].rearrange("p (s n) -> p s n", n=8)
                cbv2 = CBs[gb][:, :].unsqueeze(2).broadcast_to([128, 16, 8])
                ops.append(lambda: self.tt("dve", dkv2, cbv2, zv2, ALU.subtract, [R(f"CBs{gb}"), R("L1")], [R("LA")]))
            ops.append(lambda: self.act(KH[gb][:, :n], dk, AF.Exp, [R("LA")], [R(f"KH{gb}")]))
            return ops

        def gatingB(it):
            if not it.main:
                return []
            hd = heads[it.h]
            n, gb = it.n, it.gb
            bb, l1, eq, sgt = Bb[:, :n], L1[:, :n], EQ[:, :n], SG[:, :n]
            ops = []
            ops.append(lambda: self.act(eq, bb, AF.Exp, [R("B")], [R("U")]))
            ops.append(lambda: self.act(KT[gb][:, :n], l1, AF.Exp, [R("L1")] + rpar, [R(f"KT{gb}")], bias=hd.c1h, scale=-1.0))
            ops.append(lambda: self.tt("dve", QT32[gb][:, :n], it.pq, eq, ALU.mult, [it.rpq, R("U")], [R(f"QT32{gb}")]))
            ops.append(lambda: self.tt("dve", QT[gb][:, :n], it.pq, eq, ALU.mult, [it.rpq, R("U")], [R(f"QT{gb}")]))
            ops.append(lambda: self.act(sgt, it.pg, AF.Exp, [it.rpg], [R("SG")], scale=-1.0))
            ops.append(lambda: self.act(sgt, sgt, AF.Ln, [R("SG")], [R("SG")], bias=1.0))
            ops.append(lambda: self.act(sgt, sgt, AF.Exp, [R("SG")], [R("SG")], scale=-1.0))
            ops.append(lambda: self.stt(GATE[gb][:, :n], it.pg, hd.nwh, sgt, ALU.mult, ALU.mult, [it.rpg, R("SG"), R("PAR")], [R(f"GATE{gb}")]))
            return ops

        def mk_tile(it, t):
            td = Item()
            td.it, td.t = it, t
            td.tb = tstate["ti"] % 3
            tstate["ti"] += 1
            td.stage = 0
            return td

        def tile_views(td):
            tb = td.tb
            sm0 = (4 + tb) * 512
            v = Item()
            v.AT = self.ps32[:, sm0:sm0 + 128]
            v.Op = self.ps32[:, sm0 + 128:sm0 + 256]
            v.SSp = self.ps32[:, sm0 + 256:sm0 + 384]
            v.SPp = self.ps32[:, sm0 + 384:sm0 + 512]
            v.psb = self.ps16[:, tb * 256:(tb + 1) * 256]
            v.bk = R(f"bank{4 + tb}")
            v.cs = slice(td.t * 128, (td.t + 1) * 128)
            v.sample = td.it.m2 and td.t == 1
            return v

        def stageA(td):
            it, tb = td.it, td.tb
            gb = it.gb
            v = tile_views(td)
            self.tr([(v.psb[:, 0:128], VT[gb][:, v.cs], ident16), (v.psb[:, 128:256], KH[gb][:, v.cs], ident16)],
                    [R(f"VT{gb}"), R(f"KH{gb}"), R("C16")], [R("bank7")])
            self.copy("act", VK[tb][:, :], v.psb, [R("bank7")], [R(f"VK{tb}")])
            if it.main:
                self.mm([(v.AT, KT[gb][:, v.cs], QT[gb][:, v.cs], True, True)], [R(f"KT{gb}"), R(f"QT{gb}")], [v.bk])
                mask = C32[:, C_MS:C_MS + 128] if v.sample else C32[:, C_MC:C_MC + 128]
                self.tt("dve", ATm[tb][:, :], v.AT, mask, ALU.mult, [v.bk, R("C32")], [R(f"ATm{tb}")])

        def stageB(td):
            it, tb = td.it, td.tb
            hd = heads[it.h]
            h, gb = it.h, it.gb
            Sh, rS = hd.Sh, hd.rS
            v = tile_views(td)
            if it.main:
                if not v.sample:
                    self.mm([(v.Op, VK[tb][:, 0:128], ATm[tb][:, :], True, False),
                             (v.Op, Sh, QT32[gb][:, v.cs], False, True)],
                            [R(f"VK{tb}"), R(f"ATm{tb}"), rS, R(f"QT32{gb}")], [v.bk])
                else:
                    s0v = self.S0BF.rearrange("p (s v) -> p s v", s=16)
                    mms = [(v.Op, VK[tb][:, 0:128], ATm[tb][:, :], True, False)]
                    for sq_ in range(16):
                        mms.append((v.Op[:, sq_ * 8:(sq_ + 1) * 8], s0v[:, sq_, :], QT[gb][:, 128 + sq_ * 8:128 + (sq_ + 1) * 8], False, sq_ == 15))
                    self.mm(mms, [R(f"VK{tb}"), R(f"ATm{tb}"), R("Vb0"), R("Vb1"), R("Vb2"), R("Vb3"), R(f"QT{gb}")], [v.bk])
            if not v.sample:
                self.mm([(v.SPp, VK[tb][:, 128:256], VK[tb][:, 0:128], True, True)], [R(f"VK{tb}")], [v.bk])
                self.stt(Sh, Sh, EBL[gb][:, td.t:td.t + 1], v.SPp, ALU.mult, ALU.add, [rS, R(f"EBL{gb}"), v.bk], [rS])
                if it.m2:
                    self.dma("sp", self.sp_out[h], Sh, [rS], [])
            else:
                self.copy("dve", self.VKS[:, :], VK[tb][:, :], [R(f"VK{tb}")], [R("VKS")])
            if it.main:
                self.act(SQ[tb][:, :], v.Op, AF.Square, [v.bk], [R(f"SQ{tb}")])

        def stageC(td):
            it, tb = td.it, td.tb
            if not it.main:
                return
            gb = it.gb
            v = tile_views(td)
            mcol = it.c0 + td.t * 128
            self.mm([(v.SSp, ones16, SQ[tb][:, :], True, True)], [R(f"SQ{tb}"), R("C16")], [v.bk])
            self.act(RS[tb][:, :], v.SSp, AF.Ln, [v.bk], [R(f"RS{tb}")], bias=RMS_EPS, scale=1.0 / 128.0)
            self.act(RS[tb][:, :], RS[tb][:, :], AF.Exp, [R(f"RS{tb}")], [R(f"RS{tb}")], scale=-0.5)
            self.tt("dve", T1[tb][:, :], v.Op, RS[tb][:, :], ALU.mult, [v.bk, R(f"RS{tb}")], [R(f"T1{tb}")])
            self.tt("pool", self.OTa[:, it.h, mcol:mcol + 128], T1[tb][:, :], GATE[gb][:, v.cs], ALU.mult,
                    [R(f"T1{tb}"), R(f"GATE{gb}")], [R(f"OTa_{mcol // 128}")])

        tqueue = []

        def step(td_new):
            if td_new is not None:
                stageA(td_new)
            for td in reversed(tqueue):
                if td.stage == 1:
                    stageB(td)
                    td.stage = 2
                elif td.stage == 2:
                    stageC(td)
                    td.stage = 3
            tqueue[:] = [td for td in tqueue if td.stage < 3]
            if td_new is not None:
                td_new.stage = 1
                tqueue.append(td_new)

        def sample_state(it):
            h, gb = it.h, it.gb
            s0v = S0b[h % 2].rearrange("p (s v) -> p s v", s=16)
            VKs = self.VKS
            holder = {}

            def vb_op(q4):
                def f():
                    vbv = Vb[q4].rearrange("p (s v) -> p s v", s=4)
                    vin = VKs[:, 0:128].unsqueeze(1).broadcast_to([128, 4, 128])
                    sel = C16[:, C_SEL + 4 * q4:C_SEL + 4 * q4 + 4].unsqueeze(2).broadcast_to([128, 4, 128])
                    self.tt("dve", vbv, vin, sel, ALU.mult, [R("VKS"), R("C16")], [R(f"Vb{q4}")])
                return f

            def mm_op(q4):
                def f():
                    b = self.next_bank()
                    holder[q4] = b
                    sps = self.ps32[:, b * 512:(b + 1) * 512]
                    self.mm([(sps, VKs[:, 128:256], Vb[q4], True, True)], [R("VKS"), R(f"Vb{q4}")], [R(f"bank{b}")])
                return f

            def wb_op(q4):
                def f():
                    b = holder[q4]
                    sps = self.ps32[:, b * 512:(b + 1) * 512]
                    for s_ in range(4):
                        sq_ = q4 * 4 + s_
                        self.stt(s0v[:, sq_, :], s0v[:, sq_, :], EBLs[gb][:, sq_:sq_ + 1], sps[:, s_ * 128:(s_ + 1) * 128],
                                 ALU.mult, ALU.add, [R(f"S0b{h % 2}"), R(f"EBLs{gb}"), R(f"bank{b}")], [R(f"S0b{h % 2}")])
                return f
            dst = self.ss_out[:, h].rearrange("s k v -> k s v")
            out_op = lambda: self.dma("sp", dst, s0v, [R(f"S0b{h % 2}")], [])
            chunks = [[vb_op(0), vb_op(1), vb_op(2), vb_op(3)],
                      [mm_op(0), mm_op(1)],
                      [wb_op(0), wb_op(1), mm_op(2), mm_op(3)],
                      [wb_op(2), wb_op(3), out_op]]
            return chunks

        def run(ops):
            for o in ops:
                o()

        load_s0(0)
        setup_head(0)
        proj_item(items[0])
        run(gatingA(items[0]))
        run(gatingB(items[0]))
        deferred = []
        for i, it in enumerate(items):
            nxt = items[i + 1] if i + 1 < len(items) else None
            P, A1, A2, B = [], [], [], []
            if nxt is not None:
                if nxt.gidx == 0:
                    setup_head(nxt.h)
                P = proj_chunks(nxt)
            avail = []
            if it.m2:
                self.copy("dve", self.S0BF[:, :], S0b[it.h % 2][:, :], [R(f"S0b{it.h % 2}")], [R("Vb0"), R("Vb1"), R("Vb2"), R("Vb3")])
            nsteps = it.nt
            T = max(nsteps, len(P))
            for t in range(T):
                if t < len(P):
                    P[t]()
                if t == 0 and nxt is not None:
                    ga = gatingA(nxt)
                    run(ga[:nxt.nA1a])
                if t == 1 and nxt is not None:
                    run(ga[nxt.nA1a:nxt.nA1])
                    A2 = ga[nxt.nA1:]
                if t == 3 and nxt is not None and nxt.main:
                    B = gatingB(nxt)
                if t < nsteps:
                    step(mk_tile(it, t))
                if t == 0:
                    avail = [(lambda ch: (lambda: run(ch)))(ch) for ch in deferred] + avail
                    deferred = []
                if t == 1:
                    run(A2)
                if t == 3:
                    avail = avail + B
                remaining = max(T - t - 1, 0)
                k = -(-len(avail) // (remaining + 1)) if avail else 0
                run(avail[:k])
                avail = avail[k:]
            run(avail)
            if it.gidx == 0 and it.h + 1 < 16:
                load_s0(it.h + 1)
            if it.m2:
                deferred = sample_state(it)
        step(None)
        step(None)
        for ch in deferred:
            run(ch)
        self.slab_issue_upto(self.sl_used + 8)
        self.auto_prefetch = True
        if DEBUG:
            self.dbg_dump_bf16(self.dbg_ot, self.OTa, [R(f"OTa_{i}") for i in range(10)])

    def dbg_dump_bf16(self, dst, src3, reads):
        self.P.barrier()
        tmp = self.SCR[:, 21800 - 1280:21800]
        for c in range(NCH):
            self.copy("dve", tmp, src3[:, c, :], reads, [self.R("dbgtmp")])
            self.dma("sp", dst[:, c, :], tmp, [self.R("dbgtmp")], [self.R("dbgout")])
        self.P.barrier()

    def wout_ln(self, layer):
        R = self.R
        self.carve_reset()
        ZW = 640
        Z_t = self.c32(NCH * ZW)
        Z = Z_t.rearrange("p (c n) -> p c n", c=NCH)
        ACC = [self.c32(ZW), self.c32(ZW)]
        ACC2 = [self.c32(ZW), self.c32(ZW)]
        MEANb = self.c32(ZW)
        MEAN = [MEANb, MEANb]
        RSTD = [self.c32(ZW), self.c32(ZW)]
        NMR = [self.c32(ZW), self.c32(ZW)]
        M2 = self.c32(512)
        SQT = [self.c32(512) for _ in range(2)]
        XR = [self.c32(512) for _ in range(4)]
        XO = [self.c32(512) for _ in range(4)]
        self.n_ring = 5
        ones32 = self.C32[:, C_ONE:C_ONE + 128]
        if layer == 0:
            passes = [[(0, 512), (512, 128)], [(640, 512), (1152, 128)]]
            sbase = 64
        else:
            passes = [[(128, 512), (640, 128)], [(768, 384), (1152, 128)]]
            sbase = 120
        cnt = {"k": 0, "ko": 0}
        if layer == 0:
            self.dma("sp", self.ks[:, 0:120, :], self.ck[:, 8:128, :], [], [R("ks_copy")])
            self.dma("sp", self.vs[:, 0:120, :], self.cv[:, 8:128, :], [], [R("vs_copy")])

        steps = [(pi, j, c0, n) for pi in range(2) for j in range(NCH) for (c0, n) in passes[pi]]
        xrs = {"issued": 0}

        def issue_xr(upto):
            upto = min(upto, len(steps))
            while xrs["issued"] < upto:
                i = xrs["issued"]
                pi_, j_, c0, n = steps[i]
                xb = i % 4
                xr = XR[xb][:, :n]
                if layer == 0:
                    src = self.xsT[:, j_, :] if c0 >= NMP else self.xT[:, j_, NPRE + c0:NPRE + c0 + n]
                else:
                    src = self.x1T[:, j_, c0:c0 + n]
                if layer == 1:
                    l0_pieces = [(0, 512), (512, 128), (640, 512), (1152, 128)]
                    xdeps = [R(f"x1T_{j_}_{p0_}") for (p0_, pn_) in l0_pieces if p0_ < c0 + n and c0 < p0_ + pn_]
                else:
                    xdeps = []
                self.dma("sp", xr, src, xdeps, [R(f"XR{xb}")])
                xrs["issued"] += 1

        def accumulate(pi, j):
            cgs = passes[pi]
            p0 = cgs[0][0]
            slab, rsl = self.next_slab(sbase + j)
            for (c0, n) in cgs:
                kb = cnt["k"] % 2
                xb = cnt["k"] % 4
                assert steps[cnt["k"]] == (pi, j, c0, n)
                issue_xr(cnt["k"] + 3)
                cnt["k"] += 1
                xr = XR[xb][:, :n]
                y, ry = self.proj(slab, rsl, self.OTa, [R(f"OTa_{i}") for i in range(c0 // 128, (c0 + n) // 128)], c0, n)
                sl = slice(c0 - p0, c0 - p0 + n)
                zs = Z[:, j, sl]
                self.stt(zs, xr, ALPHA, y, ALU.mult, ALU.add, [R(f"XR{xb}"), ry], [R(f"Z{j}_{sl.start}")])
                sqt = SQT[kb][:, :n]
                self.act(sqt, zs, AF.Square, [R(f"Z{j}_{sl.start}")], [R(f"SQT{kb}")])
                a1 = ACC[pi][:, sl]
                a2 = ACC2[pi][:, sl]
                if j == 0:
                    self.copy("dve", a1, zs, [R(f"Z{j}_{sl.start}")], [R(f"ACC{pi}")])
                    self.copy("dve", a2, sqt, [R(f"SQT{kb}")], [R(f"ACC2{pi}")])
                else:
                    self.tt("dve", a1, a1, zs, ALU.add, [R(f"Z{j}_{sl.start}"), R(f"ACC{pi}")], [R(f"ACC{pi}")])
                    self.tt("dve", a2, a2, sqt, ALU.add, [R(f"SQT{kb}"), R(f"ACC2{pi}")], [R(f"ACC2{pi}")])

        def stats(pi):
            cgs = passes[pi]
            p0 = cgs[0][0]
            chains = []
            for (c0, n) in cgs:
                sl = slice(c0 - p0, c0 - p0 + n)
                tot = self.ps32[:, 5 * 512:5 * 512 + n]
                tot2 = self.ps32[:, 6 * 512:6 * 512 + n]
                k = sl.start
                rM, rR, rN = R(f"MEAN_{k}"), R(f"RSTD{pi}_{k}"), R(f"NMR{pi}_{k}")

                def mk(sl=sl, n=n, tot=tot, tot2=tot2, rM=rM, rR=rR, rN=rN):
                    return [
                        lambda: self.mm([(tot, ones32, ACC[pi][:, sl], True, True)], [R(f"ACC{pi}"), R("C32")], [R("bank5")]),
                        lambda: self.mm([(tot2, ones32, ACC2[pi][:, sl], True, True)], [R(f"ACC2{pi}"), R("C32")], [R("bank6")]),
                        lambda: self.act(MEAN[pi][:, sl], tot, AF.Copy, [R("bank5")], [rM], scale=1.0 / D),
                        lambda: self.tt("dve", M2[:, :n], MEAN[pi][:, sl], MEAN[pi][:, sl], ALU.mult, [rM], [R("M2")]),
                        lambda: self.stt(RSTD[pi][:, sl], tot2, 1.0 / D, M2[:, :n], ALU.mult, ALU.subtract, [R("bank6"), R("M2")], [rR]),
                        lambda: self.act(RSTD[pi][:, sl], RSTD[pi][:, sl], AF.Ln, [rR], [rR], bias=LN_EPS),
                        lambda: self.act(RSTD[pi][:, sl], RSTD[pi][:, sl], AF.Exp, [rR], [rR], scale=-0.5),
                        lambda: self.stt(NMR[pi][:, sl], MEAN[pi][:, sl], -1.0, RSTD[pi][:, sl], ALU.mult, ALU.mult, [rM, rR], [rN]),
                    ]
                chains.append(mk())
            SK = 3
            order = []
            n0 = len(chains[0])
            for i in range(n0 + SK):
                if i < n0:
                    order.append(chains[0][i])
                if len(chains) > 1 and 0 <= i - SK < n0:
                    order.append(chains[1][i - SK])
            for f in order:
                f()

        def normalize(pi, j):
            cgs = passes[pi]
            p0 = cgs[0][0]
            gj = self.PAR[:, P_LNG + layer * 16 + j:P_LNG + layer * 16 + j + 1]
            bj = self.PAR[:, P_LNB + layer * 16 + j:P_LNB + layer * 16 + j + 1]
            for (c0, n) in cgs:
                sl = slice(c0 - p0, c0 - p0 + n)
                kb = cnt["ko"] % 4
                cnt["ko"] += 1
                zs = Z[:, j, sl]
                self.tt("dve", zs, zs, RSTD[pi][:, sl], ALU.mult, [R(f"Z{j}_{sl.start}"), R(f"RSTD{pi}_{sl.start}")], [R(f"Z{j}_{sl.start}")])
                self.tt("dve", zs, zs, NMR[pi][:, sl], ALU.add, [R(f"Z{j}_{sl.start}"), R(f"NMR{pi}_{sl.start}")], [R(f"Z{j}_{sl.start}")])
                xo = XO[kb][:, :n]
                self.act(xo, zs, AF.Identity, [R(f"Z{j}_{sl.start}"), R("PAR")], [R(f"XO{kb}")], bias=bj, scale=gj)
                if layer == 0:
                    xw = sorted({min(c0 // 512, 2), min((c0 + n - 1) // 512, 2)})
                    self.act(self.XTm[:, j, c0:c0 + n], zs, AF.Identity, [R(f"Z{j}_{sl.start}"), R("PAR")], [R(f"XTm_{i}") for i in xw], bias=bj, scale=gj)
                    self.dma("act", self.x1T[:, j, c0:c0 + n], xo, [R(f"XO{kb}")], [R(f"x1T_{j}_{c0}")])
                    if DEBUG:
                        self.dma("act", self.dbg_x1[:, j, c0:c0 + n], xo, [R(f"XO{kb}")], [])
                else:
                    self.dma("act", self.yT[:, j, c0 - 128:c0 - 128 + n], xo, [R(f"XO{kb}")], [])

        for j in range(NCH):
            accumulate(0, j)
        stats(0)
        for j in range(NCH):
            normalize(0, j)
            accumulate(1, j)
        stats(1)
        self.slab_issue_upto(self.sl_used + 8)
        for j in range(NCH):
            normalize(1, j)

    def swa(self):
        R = self.R
        self.carve_reset()
        KcT = self.c16(16 * 4 * 128)
        Vc = self.c16(16 * 256)
        Vtok = self.c16(10 * 256)
        KTd = self.c16(4 * NM)
        QT = self.c16(4 * NMP)
        GT = self.c16(4 * NMP)
        PT = [self.c16(512) for _ in range(8)]
        att = {"st_i": 0, "on_i": 0}
        ON = [self.c16(512) for _ in range(2)]
        D2 = [self.c32(512) for _ in range(2)]
        EG = [self.c32(512) for _ in range(2)]
        STG = [self.c32(128) for _ in range(2)]
        KcTv = KcT.rearrange("p (q k s) -> p q k s", q=16, k=4)
        Vcv = Vc.rearrange("p (q d) -> p q d", q=16)
        Vtv = Vtok.rearrange("p (t d) -> p t d", t=10)
        KTdv = KTd.rearrange("p (k n) -> p k n", k=4)
        QTv = QT.rearrange("p (c n) -> p c n", c=4)
        GTv = GT.rearrange("p (c n) -> p c n", c=4)
        C16 = self.C16
        ones16 = C16[:, C_ONE:C_ONE + 128]
        self.n_ring = 3
        ST_BANKS = [(self.ps32[:, 3 * 512:4 * 512], R("bank3")), (self.ps32[:, 4 * 512:5 * 512], R("bank4")), (self.ps16[:, :].bitcast(F32), R("bank7"))]
        self.bank_pos = 0
        XT = self.XTm
        xall = [R("XTm_0"), R("XTm_1"), R("XTm_2")]

        import os
        skip = os.environ.get("KSKIP", "")
        for kk in range(4 if "cache" not in skip else 0):
            src = self.ckT[:, kk].rearrange("q d s -> d q s")
            self.dma("pool", KcTv[0:64, :, kk, :], src, [], [R("KcT")])
            self.dma("pool", KcTv[64:128, :, kk, :], src, [], [R("KcT")])
        if "cache" not in skip:
            self.dma("pool", Vcv, self.cv.rearrange("q s d -> s q d"), [], [R("Vc")])

        if self.stop_after == "swa0":
            return
        stg_i = 0
        for si in range(4):
            slab, rsl = self.next_slab(116 + si)
            isv = si >= 2
            col = (si % 2) * 128
            tiles = range(10) if isv else (8, 9)
            for t in tiles:
                b = self.next_bank()
                out = self.ps32[:, b * 512:b * 512 + 128]
                mms = [(out, XT[:, c, t * 128:(t + 1) * 128], slab[:, c, :], c == 0, c == NCH - 1) for c in range(NCH)]
                self.mm(mms, [rsl] + xall, [R(f"bank{b}")])
                if isv:
                    self.copy("act", Vtv[:, t, col:col + 128], out, [R(f"bank{b}")], [R("Vtok")])
                if t >= 8:
                    sb = stg_i % 2
                    stg_i += 1
                    self.copy("dve", STG[sb][:, :], out, [R(f"bank{b}")], [R(f"STG{sb}")])
                    if t == 8:
                        dst = (self.vp if isv else self.kp)[:, col:col + 128]
                        self.dma("sp", dst, STG[sb][:, :], [R(f"STG{sb}")], [])
                    else:
                        dd = self.vs if isv else self.ks
                        for q in range(16 if "small" not in skip else 0):
                            self.dma("sp", dd[q, 120:128, col:col + 128], STG[sb][q * 8:(q + 1) * 8, :], [R(f"STG{sb}"), R("ks_copy"), R("vs_copy")], [])
        if self.stop_after == "swa1a":
            return
        for kvh in range(4):
            slab, rsl = self.next_slab(112 + kvh)
            for (c0, n) in [(0, 512), (512, 512), (1024, 256)]:
                o, ro = self.proj(slab, rsl, XT, xall, c0, n)
                self.copy("act", KTdv[:, kvh, c0:c0 + n], o, [ro], [R("KTd")])
        if self.stop_after == "swa1b":
            return
        st_i = 0
        on_i = 0
        for kvh in range(4):
            for cc in range(4):
                slab, rsl = self.next_slab(80 + 4 * kvh + cc)
                for (c0, n) in [(128, 512), (640, 512), (1152, 128)]:
                    o, ro = self.proj(slab, rsl, XT, xall, c0, n)
                    self.copy("act", QTv[:, cc, c0 - 128:c0 - 128 + n], o, [ro], [R("QTs")])
            for cc in range(4):
                slab, rsl = self.next_slab(96 + 4 * kvh + cc)
                for gi_, (c0, n) in enumerate([(128, 512), (640, 512), (1152, 128)]):
                    o, ro = self.proj(slab, rsl, XT, xall, c0, n)
                    eg = EG[gi_ % 2][:, :n]
                    rE = R(f"EG{gi_ % 2}")
                    self.act(eg, o, AF.Exp, [ro], [rE], scale=-1.0)
                    self.act(eg, eg, AF.Ln, [rE], [rE], bias=1.0)
                    self.act(eg, eg, AF.Exp, [rE], [rE], scale=-1.0)
                    self.tt("dve", GTv[:, cc, c0 - 128:c0 - 128 + n], o, eg, ALU.mult, [ro, rE], [R("GTs")])
            Ob = self.ps32[:, 5 * 512:6 * 512]
            Db = self.ps32[:, 6 * 512:7 * 512]
            Obv = Ob.rearrange("p (c n) -> p c n", c=4)
            Dbv = Db.rearrange("p (c n) -> p c n", c=4)
            units = [(jt, par) for jt in list(range(1, 9)) + [9] for par in range(2)]

            def S1(jt):
                sample = jt == 9
                qc = (jt - 1) * 128
                st = Item2()
                st.pts = {0: [], 1: []}
                st.ptc = {}
                kts = [jt] if sample else [jt - 1, jt]
                for kt in kts:
                    slots = []
                    mms_all = []
                    for par in range(2):
                        hp = slice(par * 64, par * 64 + 64)
                        i_ = att["st_i"]
                        att["st_i"] += 1
                        pt = PT[i_ % 8]
                        rpt = R(f"PT{i_ % 8}")
                        ST, rST = ST_BANKS[i_ % 3]
                        if sample:
                            rhs = QTv[hp, :, qc:qc + 128].rearrange("p c (q t) -> p q c t", t=8)
                        else:
                            rhs = QTv[hp, :, qc:qc + 128]
                        self.mm([(ST, KTdv[hp, kvh, kt * 128:(kt + 1) * 128], rhs, True, True)],
                                [R("KTd"), R("QTs")], [rST])
                        slots.append((pt, rpt, ST, rST, par))
                    for (pt, rpt, ST, rST, par) in slots:
                        self.act(pt[:, :], ST, AF.Exp, [rST], [rpt], bias=-SHIFT, scale=0.125)
                        if sample:
                            mk = C16[:, C_MS:C_MS + 128].rearrange("p (q t) -> p q t", t=8).unsqueeze(2).broadcast_to([128, 16, 4, 8])
                            ptv = pt.rearrange("p (q c t) -> p q c t", q=16, c=4)
                        else:
                            if kt == jt:
                                mcol = C_MC
                            else:
                                mcol = C_MP1 if jt == 1 else C_MP
                            mk = C16[:, mcol:mcol + 128].unsqueeze(1).broadcast_to([128, 4, 128])
                            ptv = pt.rearrange("p (c n) -> p c n", c=4)
                        self.tt("dve", ptv, ptv, mk, ALU.mult, [rpt, R("C16")], [rpt])
                        st.pts[par].append((pt, rpt, kt))
                if sample:
                    for par in range(2):
                        hp = slice(par * 64, par * 64 + 64)
                        i_ = att["st_i"]
                        att["st_i"] += 1
                        ptc = PT[i_ % 8]
                        rptc = R(f"PT{i_ % 8}")
                        STcb, rSTc = ST_BANKS[i_ % 3]
                        mms = [(STcb[:, q * 32:(q + 1) * 32], KcTv[hp, q, kvh, :], QTv[hp, :, qc + q * 8:qc + (q + 1) * 8], True, True) for q in range(16)]
                        self.mm(mms, [R("KcT"), R("QTs")], [rSTc])
                        self.act(ptc[:, :], STcb, AF.Exp, [rSTc], [rptc], bias=-SHIFT, scale=0.125)
                        ptcv4 = ptc.rearrange("p (q c t) -> p q c t", q=16, c=4)
                        mk = C16[:, C_MCA:C_MCA + 8].unsqueeze(1).unsqueeze(1).broadcast_to([128, 16, 4, 8])
                        self.tt("dve", ptcv4, ptcv4, mk, ALU.mult, [rptc, R("C16")], [rptc])
                        st.ptc[par] = (ptc, rptc)
                return st

            def S2(jt, st):
                sample = jt == 9
                mo = []
                md = []
                rr = [R("Vtok"), R("C16")]
                nk = len(st.pts[0])
                for i in range(nk):
                    for par in range(2):
                        hp = slice(par * 64, par * 64 + 64)
                        pt, rpt, kt = st.pts[par][i]
                        last = (i == nk - 1) and not sample
                        mo.append((Ob[hp, :], Vtv[:, kt, kvh * 64:(kvh + 1) * 64], pt[:, :], i == 0, last))
                        md.append((Db[hp, :], ones16[:, 0:64], pt[:, :], i == 0, last))
                        rr.append(rpt)
                if sample:
                    for q in range(16):
                        for par in range(2):
                            hp = slice(par * 64, par * 64 + 64)
                            ptc, rptc = st.ptc[par]
                            mo.append((Ob[hp, q * 32:(q + 1) * 32], Vcv[:, q, kvh * 64:(kvh + 1) * 64], ptc[:, q * 32:(q + 1) * 32], False, q == 15))
                            md.append((Db[hp, q * 32:(q + 1) * 32], ones16[:, 0:64], ptc[:, q * 32:(q + 1) * 32], False, q == 15))
                    rr += [st.ptc[0][1], st.ptc[1][1], R("Vc")]
                self.mm(mo, rr, [R("bank5")])
                self.mm(md, rr, [R("bank6")])

            def S3(jt):
                sample = jt == 9
                qc = (jt - 1) * 128
                ob_ = att["on_i"] % 2
                att["on_i"] += 1
                d2 = D2[ob_][:, :]
                es = self.ESINK[:, kvh * 4:(kvh + 1) * 4]
                if sample:
                    esb = es.unsqueeze(1).unsqueeze(3).broadcast_to([128, 16, 4, 8])
                    d2v = d2.rearrange("p (q c t) -> p q c t", q=16, c=4)
                    dbv = Db.rearrange("p (q c t) -> p q c t", q=16, c=4)
                else:
                    esb = es.unsqueeze(2).broadcast_to([128, 4, 128])
                    d2v = d2.rearrange("p (c n) -> p c n", c=4)
                    dbv = Dbv
                self.tt("dve", d2v, dbv, esb, ALU.add, [R("bank6"), R("ESINK")], [R(f"D2{ob_}")])
                self.act(d2, d2, AF.Ln, [R(f"D2{ob_}")], [R(f"D2{ob_}")])
                self.act(d2, d2, AF.Exp, [R(f"D2{ob_}")], [R(f"D2{ob_}")], scale=-1.0)
                self.tt("dve", ON[ob_][:, :], Ob, d2, ALU.mult, [R("bank5"), R(f"D2{ob_}")], [R(f"ON{ob_}")])
                mcol0 = 1152 if sample else jt * 128
                if sample:
                    o_out = self.OTa[:, kvh * 4:(kvh + 1) * 4, mcol0:mcol0 + 128].rearrange("p c (q t) -> p q c t", t=8)
                    o_in0 = ON[ob_].rearrange("p (q c t) -> p q c t", q=16, c=4)
                    o_in1 = GTv[:, :, qc:qc + 128].rearrange("p c (q t) -> p q c t", t=8)
                else:
                    o_out = self.OTa[:, kvh * 4:(kvh + 1) * 4, mcol0:mcol0 + 128]
                    o_in0 = ON[ob_].rearrange("p (c n) -> p c n", c=4)
                    o_in1 = GTv[:, :, qc:qc + 128]
                self.tt("pool", o_out, o_in0, o_in1, ALU.mult, [R(f"ON{ob_}"), R("GTs")], [R(f"OTa_{mcol0 // 128}")])

            tiles_ = list(range(1, 9)) + [9]
            cur = S1(tiles_[0])
            for ti_, jt in enumerate(tiles_):
                nxt_st = S1(tiles_[ti_ + 1]) if ti_ + 1 < len(tiles_) else None
                S2(jt, cur)
                S3(jt)
                cur = nxt_st


class Item2:
    pass


def build_nc(stop_after=None):
    nc = bass.Bass("TRN2", target_bir_lowering=False)
    b = Builder(nc, stop_after=stop_after)
    b.build()
    return nc


def _fm(a):
    cols = a.shape[0]
    return np.ascontiguousarray(a.reshape(cols, NCH, 128).transpose(2, 1, 0))


def _slab(w):
    return np.ascontiguousarray(w.reshape(NCH, 128, 128).transpose(1, 0, 2).reshape(128, 2048))


def _consts(half):
    c = np.zeros((128, NCONST), np.float32)
    s = np.arange(128)[:, None]
    t = np.arange(128)[None, :]
    c[:, C_ID:C_ID + 128] = (s == t)
    c[:, C_MC:C_MC + 128] = (s <= t)
    c[:, C_MP:C_MP + 128] = (s > t)
    mp1 = (s > t)
    if half == 0:
        mp1 = mp1 & (s >= 112)
    c[:, C_MP1:C_MP1 + 128] = mp1
    c[:, C_MS:C_MS + 128] = (s // 8 == t // 8) & (s % 8 <= t % 8)
    rp = np.ones(512, np.float32)
    rp[::128] = 0
    c[:, C_RP:C_RP + 512] = rp[None, :]
    rm2 = np.ones(256, np.float32)
    rm2[0] = 0
    rm2[128::8] = 0
    c[:, C_RM2:C_RM2 + 256] = rm2[None, :]
    c[:, C_SEL:C_SEL + 16] = (np.arange(128)[:, None] // 8 == np.arange(16)[None, :])
    c[:, C_MCA:C_MCA + 8] = (np.arange(128)[:, None] >= np.arange(8)[None, :] + 1)
    c[:, C_ONE:C_ONE + 128] = 1.0
    return c


def prepare_inputs(x_prompt, x_sample, state_hgrn, cache_swa_k, cache_swa_v, meta_tokens,
                   hgrn_w_in, hgrn_lb_logits, hgrn_norm_w, hgrn_w_out,
                   swa_w_in, swa_sinks, swa_w_out, ln_g, ln_b):
    f32 = np.float32
    x_prompt = np.asarray(x_prompt, f32)
    x_sample = np.asarray(x_sample, f32)
    wall = np.empty((NSLAB, 128, 2048), f32)
    hw = np.asarray(hgrn_w_in, f32)[0]
    for j in range(64):
        wall[j] = _slab(hw[:, j * 128:(j + 1) * 128])
    ho = np.asarray(hgrn_w_out, f32)[0]
    for j in range(16):
        wall[64 + j] = _slab(ho[:, j * 128:(j + 1) * 128])
    sw = np.asarray(swa_w_in, f32)[0]
    for j in range(16):
        wall[80 + j] = _slab(sw[:, j * 128:(j + 1) * 128])
        wall[96 + j] = _slab(sw[:, 2560 + j * 128:2560 + (j + 1) * 128])
    for kvh in range(4):
        wk = sw[:, 2048 + kvh * 64:2048 + (kvh + 1) * 64]
        wall[112 + kvh] = _slab(np.concatenate([wk, wk], axis=1))
    for i in range(4):
        wall[116 + i] = _slab(sw[:, 2048 + i * 128:2048 + (i + 1) * 128])
    so = np.asarray(swa_w_out, f32)[0]
    for j in range(16):
        wall[120 + j] = _slab(so[:, j * 128:(j + 1) * 128])
    pars = np.zeros((128, NPAR), f32)
    lbl = np.asarray(hgrn_lb_logits, f32)
    for i in range(3):
        pars[:, P_LBL + 16 * i:P_LBL + 16 * i + 16] = lbl[i].reshape(16, 128).T
    pars[:, P_NW:P_NW + 16] = np.asarray(hgrn_norm_w, f32)[0].T
    for l in range(2):
        pars[:, P_LNG + 16 * l:P_LNG + 16 * l + 16] = np.asarray(ln_g, f32)[l].reshape(16, 128).T
        pars[:, P_LNB + 16 * l:P_LNB + 16 * l + 16] = np.asarray(ln_b, f32)[l].reshape(16, 128).T
    sk = np.asarray(swa_sinks, f32)[0]
    pars[0:64, P_SINK:P_SINK + 16] = sk[0::2][None, :]
    pars[64:128, P_SINK:P_SINK + 16] = sk[1::2][None, :]
    meta = np.asarray(meta_tokens, f32)
    st = np.asarray(state_hgrn, f32)[0]
    ck = np.asarray(cache_swa_k, f32)[0].reshape(128, 128, 256)
    cv = np.asarray(cache_swa_v, f32)[0].reshape(128, 128, 256)
    consts = [_consts(0), _consts(1)]
    in_maps = []
    for core in range(8):
        seq, half = core // 2, core % 2
        cols = np.zeros((NX, D), f32)
        if half == 0:
            cols[NPRE + 112:NPRE + 128] = meta
            cols[NPRE + 128:] = x_prompt[seq, 0:1024]
        else:
            cols[112:128] = meta
            cols[128:] = x_prompt[seq]
        sl = slice(16 * core, 16 * core + 16)
        ckc = ck[sl]
        in_maps.append({
            "xT": _fm(cols),
            "xsT": _fm(x_sample[sl].reshape(128, D)),
            "s0": np.ascontiguousarray(st[sl]),
            "ckT": np.ascontiguousarray(ckc.reshape(16, 128, 4, 64).transpose(0, 2, 3, 1)),
            "cv": np.ascontiguousarray(cv[sl]),
            "ck": np.ascontiguousarray(ckc),
            "wall": wall,
            "consts": consts[half],
            "pars": pars,
        })
    return in_maps


_NC_CACHE = {}


def kernel(x_prompt, x_sample, state_hgrn, cache_swa_k, cache_swa_v, meta_tokens,
           hgrn_w_in, hgrn_lb_logits, hgrn_norm_w, hgrn_w_out,
           swa_w_in, swa_sinks, swa_w_out, ln_g, ln_b):
    in_maps = prepare_inputs(x_prompt, x_sample, state_hgrn, cache_swa_k, cache_swa_v, meta_tokens,
                             hgrn_w_in, hgrn_lb_logits, hgrn_norm_w, hgrn_w_out,
                             swa_w_in, swa_sinks, swa_w_out, ln_g, ln_b)
    nc = build_nc()
    res = run_bass_kernel_spmd(nc, in_maps, core_ids=list(range(8)))
    rs = res.results
    f32 = np.float32
    y_prompt = np.empty((4, 2048, D), f32)
    y_sample = np.empty((128, 8, D), f32)
    st_p = np.empty((1, 4, 16, 128, 128), f32)
    st_s = np.empty((1, 128, 16, 128, 128), f32)
    kp = np.empty((1, 4, 128, 4, 64), f32)
    vp = np.empty((1, 4, 128, 4, 64), f32)
    ks = np.empty((1, 128, 128, 4, 64), f32)
    vs = np.empty((1, 128, 128, 4, 64), f32)
    for core in range(8):
        r = rs[core]
        seq, half = core // 2, core % 2
        yT = np.asarray(r["yT"])
        ytok = yT.transpose(2, 1, 0).reshape(NMP, D)
        y_prompt[seq, half * 1024:(half + 1) * 1024] = ytok[0:1024]
        y_sample[16 * core:16 * core + 16] = ytok[1024:1152].reshape(16, 8, D)
        st_s[0, 16 * core:16 * core + 16] = np.asarray(r["ss_out"])
        ks[0, 16 * core:16 * core + 16] = np.asarray(r["ks"]).reshape(16, 128, 4, 64)
        vs[0, 16 * core:16 * core + 16] = np.asarray(r["vs"]).reshape(16, 128, 4, 64)
        if half == 1:
            st_p[0, seq] = np.asarray(r["sp_out"])
            kp[0, seq] = np.asarray(r["kp"]).reshape(128, 4, 64)
            vp[0, seq] = np.asarray(r["vp"]).reshape(128, 4, 64)
    return (y_prompt, y_sample, st_p, st_s, kp, vp, ks, vs)
```
